# Optimizing a Trainium2 kernel written in Bass

```python
import math
import jax, jax.numpy as jnp
from jax import lax
import numpy as np

D_MODEL = 1024
BATCH = 4
SEQ = 8192
DEPTH = 1
DEC_BATCH = 32
DEC_SEQ = 4
PAST_LEN = 16384
PAGE_SIZE = 128

D_CONV = 512
CONV_WIDTH = 31
HEAD_DIM = 64
HEADS_PER_GROUP = 4
GROUPS = ((128, 1), (512, 4), (2048, 16))
N_ATT_HEADS = HEADS_PER_GROUP * len(GROUPS)
D_ATT = N_ATT_HEADS * HEAD_DIM
D_ATT_OUT = HEADS_PER_GROUP * HEAD_DIM
D_FF = ((8 * D_MODEL + 3 * 256 - 1) // (3 * 256)) * 256
D_PLE = 256
BLK = 128
N_IN = 2 * D_CONV + 3 * D_ATT + 2 * D_MODEL
EPS = 1e-6
NEG_INF = -1e30

kernel_name = "hybrid_conformer_dilated_attn_decode_step"


def _rms_norm(x, g):
    xf = x.astype(jnp.float32)
    y = xf * lax.rsqrt(jnp.mean(xf * xf, axis=-1, keepdims=True) + EPS)
    return (y * g.astype(jnp.float32)).astype(x.dtype)


def _layer_norm(x, g, b):
    xf = x.astype(jnp.float32)
    mu = jnp.mean(xf, axis=-1, keepdims=True)
    xc = xf - mu
    y = xc * lax.rsqrt(jnp.mean(xc * xc, axis=-1, keepdims=True) + EPS)
    return (y * g.astype(jnp.float32) + b.astype(jnp.float32)).astype(x.dtype)


def _alibi_slopes():
    return 2.0 ** (-8.0 * (jnp.arange(N_ATT_HEADS, dtype=jnp.float32) + 1.0) / N_ATT_HEADS)


def _dilated_prompt(q, k, v, slopes, window, dil):
    B, S, H, Dh = q.shape
    span = dil * BLK
    S_pad = -(-S // span) * span
    nb = S_pad // span
    pad = ((0, 0), (0, S_pad - S), (0, 0), (0, 0))

    def split(a):
        return jnp.pad(a, pad).reshape(B, nb, BLK, dil, H, Dh)

    def with_prev(a):
        prev = jnp.pad(a, ((0, 0), (1, 0), (0, 0), (0, 0), (0, 0), (0, 0)))[:, :-1]
        return jnp.concatenate([prev, a], axis=2)

    qb = split(q)
    kk = with_prev(split(k))
    vv = with_prev(split(v))
    s = jnp.einsum('bnqrhd,bnkrhd->bnrhqk', qb, kk,
                   preferred_element_type=jnp.float32) * (1.0 / math.sqrt(Dh))
    qi = jnp.arange(BLK)[:, None]
    ki = jnp.arange(2 * BLK)[None, :]
    dist = qi + BLK - ki
    band = (dist >= 0) & (dist <= window // dil)
    first = (jnp.arange(nb)[:, None, None] > 0) | (ki >= BLK)
    mask = band[None] & first
    bias = -slopes[:, None, None] * (dist * dil).astype(jnp.float32)[None]
    s = jnp.where(mask[None, :, None, None], s + bias[None, None, None], NEG_INF)
    m = jnp.max(s, axis=-1, keepdims=True)
    p = jnp.exp(s - m)
    l = jnp.sum(p, axis=-1, keepdims=True)
    p = p / l
    o = jnp.einsum('bnrhqk,bnkrhd->bnqrhd', p, vv.astype(jnp.float32))
    o = o.reshape(B, S_pad, H, Dh)[:, :S]
    lse = (m + jnp.log(l))[..., 0]
    lse = jnp.transpose(lse, (0, 1, 4, 2, 3)).reshape(B, S_pad, H)[:, :S]
    return o, lse


def _dilated_sample(q, k_all, v_all, slopes, window, dil, L):
    T = q.shape[1]
    Dh = q.shape[-1]
    j = jnp.arange(window // dil + 1)
    idx = L + jnp.arange(T)[:, None] - j[None, :] * dil
    valid = idx >= 0
    idx_c = jnp.maximum(idx, 0)
    kg = k_all[:, idx_c]
    vg = v_all[:, idx_c]
    s = jnp.einsum('bthd,btjhd->bhtj', q, kg,
                   preferred_element_type=jnp.float32) * (1.0 / math.sqrt(Dh))
    bias = -slopes[:, None, None] * (j * dil).astype(jnp.float32)[None, None, :]
    s = jnp.where(valid[None, None], s + bias[None], NEG_INF)
    m = jnp.max(s, axis=-1, keepdims=True)
    p = jnp.exp(s - m)
    l = jnp.sum(p, axis=-1, keepdims=True)
    p = p / l
    o = jnp.einsum('bhtj,btjhd->bthd', p, vg.astype(jnp.float32))
    lse = jnp.transpose((m + jnp.log(l))[..., 0], (0, 2, 1))
    return o, lse


def _mixer(h, conv_buf, kv_bufs, w_in, w_dw, b_dw, ln_g, ln_b, w_conv_out, w_att_out, w_out):
    B, S, _ = h.shape
    z = h @ w_in
    o_q = 2 * D_CONV
    o_k = o_q + D_ATT
    o_v = o_k + D_ATT
    o_g = o_v + D_ATT
    u = z[..., :D_CONV] * jax.nn.sigmoid(z[..., D_CONV:o_q])
    q = z[..., o_q:o_k].reshape(B, S, N_ATT_HEADS, HEAD_DIM)
    k = z[..., o_k:o_v].reshape(B, S, N_ATT_HEADS, HEAD_DIM)
    v = z[..., o_v:o_g].reshape(B, S, N_ATT_HEADS, HEAD_DIM)
    gate_c = jax.nn.sigmoid(z[..., o_g:o_g + D_MODEL])
    gate_a = jax.nn.sigmoid(z[..., o_g + D_MODEL:])

    if conv_buf is None:
        conv_buf = jnp.zeros((B, CONV_WIDTH - 1, D_CONV), u.dtype)
    xin = jnp.concatenate([conv_buf.astype(u.dtype), u], axis=1)
    new_conv = xin[:, -(CONV_WIDTH - 1):]
    c = lax.conv_general_dilated(xin, w_dw[:, None, :], (1,), 'VALID',
                                 dimension_numbers=('NWC', 'WIO', 'NWC'),
                                 feature_group_count=D_CONV) + b_dw
    c = jax.nn.silu(_layer_norm(c, ln_g, ln_b))
    y_conv = c @ w_conv_out

    slopes = _alibi_slopes()
    outs, lses, new_kv = [], [], []
    for g, (window, dil) in enumerate(GROUPS):
        hs = slice(g * HEADS_PER_GROUP, (g + 1) * HEADS_PER_GROUP)
        qg, kg, vg = q[:, :, hs], k[:, :, hs], v[:, :, hs]
        kv_new = jnp.stack([kg, vg], axis=2)
        if kv_bufs is None:
            o, lse = _dilated_prompt(qg, kg, vg, slopes[hs], window, dil)
            new_kv.append(kv_new[:, -min(window, S):])
        else:
            buf = kv_bufs[g].astype(kv_new.dtype)
            L = buf.shape[1]
            kv_all = jnp.concatenate([buf, kv_new], axis=1)
            o, lse = _dilated_sample(qg, kv_all[:, :, 0], kv_all[:, :, 1], slopes[hs], window, dil, L)
            new_kv.append(kv_all[:, -L:])
        outs.append(o)
        lses.append(lse)
    wts = jax.nn.softmax(jnp.stack(lses, axis=0), axis=0)
    o = jnp.sum(wts[..., None] * jnp.stack(outs, axis=0), axis=0)
    y_att = o.reshape(B, S, D_ATT_OUT).astype(h.dtype) @ w_att_out

    mix = (gate_c * y_conv + gate_a * y_att) @ w_out
    return mix, new_conv, new_kv


def _layer(x, p, conv_buf, kv_bufs, g_mix, w_in, w_dw, b_dw, ln_g, ln_b, w_conv_out, w_att_out,
           w_out, g_ffn, w_ffn_gate, w_ffn_up, w_ffn_down, g_ple, w_ple_gate, w_ple):
    mix, new_conv, new_kv = _mixer(_rms_norm(x, g_mix), conv_buf, kv_bufs, w_in, w_dw, b_dw,
                                   ln_g, ln_b, w_conv_out, w_att_out, w_out)
    x = x + mix
    hf = _rms_norm(x, g_ffn)
    x = x + (jax.nn.silu(hf @ w_ffn_gate) * (hf @ w_ffn_up)) @ w_ffn_down
    x = x + jax.nn.sigmoid(_rms_norm(x, g_ple) @ w_ple_gate) * (p.astype(x.dtype) @ w_ple)
    return x, new_conv, new_kv


def setup_inputs(seed: int = 0) -> dict:
    key = jax.random.key(seed)
    ks = iter(jax.random.split(key, 40))
    f32 = jnp.float32

    def nrm(shape, scale):
        return jax.random.normal(next(ks), shape, f32) * scale

    def gain(shape):
        return 1.0 + 0.01 * jax.random.normal(next(ks), shape, f32)

    d = {}
    d["x_prompt"] = nrm((BATCH, SEQ, D_MODEL), 1.0)
    d["x_sample"] = nrm((DEC_BATCH, DEC_SEQ, D_MODEL), 1.0)
    d["state_conv"] = nrm((DEPTH, DEC_BATCH, CONV_WIDTH - 1, D_CONV), 0.5)
    for g, (window, _) in enumerate(GROUPS):
        d["cache_kv_g%d" % g] = nrm((DEPTH, DEC_BATCH, min(window, PAST_LEN), 2, HEADS_PER_GROUP, HEAD_DIM), 1.0)
    d["p_prompt"] = nrm((DEPTH, BATCH, SEQ, D_PLE), 1.0)
    d["p_sample"] = nrm((DEPTH, DEC_BATCH, DEC_SEQ, D_PLE), 1.0)
    d["g_mix"] = gain((DEPTH, D_MODEL))
    d["w_in"] = nrm((DEPTH, D_MODEL, N_IN), D_MODEL ** -0.5)
    d["w_dw"] = nrm((DEPTH, CONV_WIDTH, D_CONV), CONV_WIDTH ** -0.5)
    d["b_dw"] = nrm((DEPTH, D_CONV), 0.01)
    d["ln_g"] = gain((DEPTH, D_CONV))
    d["ln_b"] = nrm((DEPTH, D_CONV), 0.01)
    d["w_conv_out"] = nrm((DEPTH, D_CONV, D_MODEL), D_CONV ** -0.5)
    d["w_att_out"] = nrm((DEPTH, D_ATT_OUT, D_MODEL), D_ATT_OUT ** -0.5)
    d["w_out"] = nrm((DEPTH, D_MODEL, D_MODEL), D_MODEL ** -0.5)
    d["g_ffn"] = gain((DEPTH, D_MODEL))
    d["w_ffn_gate"] = nrm((DEPTH, D_MODEL, D_FF), D_MODEL ** -0.5)
    d["w_ffn_up"] = nrm((DEPTH, D_MODEL, D_FF), D_MODEL ** -0.5)
    d["w_ffn_down"] = nrm((DEPTH, D_FF, D_MODEL), D_FF ** -0.5)
    d["g_ple"] = gain((DEPTH, D_MODEL))
    d["w_ple_gate"] = nrm((DEPTH, D_MODEL, D_MODEL), D_MODEL ** -0.5)
    d["w_ple"] = nrm((DEPTH, D_PLE, D_MODEL), D_PLE ** -0.5)
    d["g_final"] = gain((D_MODEL,))
    return d


def reference(x_prompt, x_sample, state_conv, cache_kv_g0, cache_kv_g1, cache_kv_g2, p_prompt, p_sample,
              g_mix, w_in, w_dw, b_dw, ln_g, ln_b, w_conv_out, w_att_out, w_out, g_ffn, w_ffn_gate,
              w_ffn_up, w_ffn_down, g_ple, w_ple_gate, w_ple, g_final):
    xp, xs = x_prompt, x_sample
    conv_p, conv_s = [], []
    kv_p = [[], [], []]
    kv_s = [[], [], []]
    for i in range(DEPTH):
        lw = (g_mix[i], w_in[i], w_dw[i], b_dw[i], ln_g[i], ln_b[i], w_conv_out[i], w_att_out[i],
              w_out[i], g_ffn[i], w_ffn_gate[i], w_ffn_up[i], w_ffn_down[i], g_ple[i], w_ple_gate[i], w_ple[i])
        xp, ncp, nkvp = _layer(xp, p_prompt[i], None, None, *lw)
        xs, ncs, nkvs = _layer(xs, p_sample[i], state_conv[i],
                               (cache_kv_g0[i], cache_kv_g1[i], cache_kv_g2[i]), *lw)
        conv_p.append(ncp)
        conv_s.append(ncs)
        for g in range(len(GROUPS)):
            kv_p[g].append(nkvp[g])
            kv_s[g].append(nkvs[g])
    y_prompt = _rms_norm(xp, g_final)
    y_sample = _rms_norm(xs, g_final)
    return (y_prompt, y_sample, jnp.stack(conv_p), jnp.stack(conv_s),
            jnp.stack(kv_p[0]), jnp.stack(kv_s[0]), jnp.stack(kv_p[1]), jnp.stack(kv_s[1]),
            jnp.stack(kv_p[2]), jnp.stack(kv_s[2]))
```

```python
import numpy as np
from contextlib import ExitStack
import concourse.bass as bass
import concourse.mybir as mybir
from concourse.bass_utils import run_bass_kernel_spmd

F32 = mybir.dt.float32
BF16 = mybir.dt.bfloat16
ALU = mybir.AluOpType
AF = mybir.ActivationFunctionType

D = 1024
DC = 512
NIN = 5376
DFF = 2816
DPLE = 256
EPS = 1e-6
GROUPS = ((128, 1), (512, 4), (2048, 16))
NCH = (2, 5, 17)
RING = (5, 8, 20)
NCORES = 8
NSEQ_S = 4
TS = 16
HALO = 2048
MAIN = 4096
SLOPES = [2.0 ** (-8.0 * (h + 1) / 12.0) for h in range(12)]


class Sched:
    def __init__(self, nc, stack):
        self.nc = nc
        self.stack = stack
        self.ops = []
        self.eng_obj = {"pe": nc.tensor, "act": nc.scalar, "dve": nc.vector,
                        "pool": nc.gpsimd, "sp": nc.sync}
        self.sems = {}
        self.nsem = 0

    def sem(self, stream):
        if stream not in self.sems:
            self.nsem += 1
            self.sems[stream] = self.stack.enter_context(self.nc.semaphore("sm%d" % self.nsem))
        return self.sems[stream]

    def op(self, eng, fn, reads=(), writes=(), dma_key=None):
        self.ops.append(dict(eng=eng, fn=fn, reads=tuple(reads), writes=tuple(writes),
                             dma_key=dma_key, signal=False))

    def dma(self, eng, fn, reads=(), writes=(), key=None):
        self.op(eng, fn, reads, writes, dma_key=key)

    def finalize(self, final_eng="sp"):
        ops = self.ops
        last_dma = {}
        for i, o in enumerate(ops):
            if o["dma_key"] is not None:
                last_dma[o["dma_key"]] = i
        ops.append(dict(eng=final_eng, fn=None, reads=(), writes=(), dma_key=None,
                        signal=False, extra_deps=set(last_dma.values())))
        seqctr, last_write, readers, last_on_key = {}, {}, {}, {}
        known = {e: {} for e in self.eng_obj}
        for i, o in enumerate(ops):
            E = o["eng"]
            stream = E if o["dma_key"] is None else ("dma", o["dma_key"])
            o["stream"] = stream
            deps = set(o.get("extra_deps", ()))
            for k in o["reads"]:
                if k in last_write:
                    deps.add(last_write[k])
                if isinstance(k, tuple) and k[0] == "ps":
                    for rs, ri in readers.get(k, {}).items():
                        if rs != stream:
                            deps.add(ri)
            for k in o["writes"]:
                if k in last_write:
                    deps.add(last_write[k])
                deps.update(readers.get(k, {}).values())
            if o["dma_key"] is not None and o["dma_key"] in last_on_key:
                deps.add(last_on_key[o["dma_key"]])
            best = {}
            for d in deps:
                Dd = ops[d]
                s, q = Dd["stream"], Dd["seq"]
                if s not in best or best[s][0] < q:
                    best[s] = (q, d)
            waits = []
            for s, (q, d) in best.items():
                if E == "pe" and s == "pe":
                    continue
                if known[E].get(s, 0) >= q:
                    continue
                waits.append(d)
                ops[d]["signal"] = True
                for ks, kq in ops[d]["clock"].items():
                    if known[E].get(ks, 0) < kq:
                        known[E][ks] = kq
            o["waits"] = waits
            seqctr[stream] = seqctr.get(stream, 0) + 1
            o["seq"] = seqctr[stream]
            clk = dict(known[E])
            clk[stream] = o["seq"]
            o["clock"] = clk
            if E == "pe" and o["dma_key"] is None:
                known["pe"]["pe"] = o["seq"]
            for k in o["reads"]:
                readers.setdefault(k, {})[stream] = i
            for k in o["writes"]:
                last_write[k] = i
                readers[k] = {}
            if o["dma_key"] is not None:
                last_on_key[o["dma_key"]] = i
        sigctr = {}
        for o in ops:
            e = self.eng_obj[o["eng"]]
            for d in o["waits"]:
                Dd = ops[d]
                e.wait_ge(self.sem(Dd["stream"]), Dd["sigval"])
            if o["fn"] is None:
                continue
            r = o["fn"](e)
            stt = o["stream"]
            if o["dma_key"] is not None:
                insts = r if isinstance(r, (list, tuple)) else [r]
                s = self.sem(stt)
                for ins in insts:
                    ins.then_inc(s, 16)
                sigctr[stt] = sigctr.get(stt, 0) + 16 * len(insts)
                o["sigval"] = sigctr[stt]
            elif o["signal"]:
                ins = r[-1] if isinstance(r, (list, tuple)) else r
                ins.then_inc(self.sem(stt), 1)
                sigctr[stt] = sigctr.get(stt, 0) + 1
                o["sigval"] = sigctr[stt]
        return len(ops)


def g2_tables():
    out = np.full((128, 5, 32), 1e6, np.float32)
    u = np.arange(32)[:, None]
    q = np.arange(32)[None, :]
    for rot in range(4):
        for sl in range(4):
            a = ((rot - sl - 1) % 4) + 1
            du = 32 * a + q - u
            out[32 * sl:32 * sl + 32, rot, :] = np.where(du <= 128, 16.0 * du, 1e6)
    du = q - u
    out[0:32, 4, :] = np.where(du >= 0, 16.0 * du, 1e6)
    return out


def dist_tables():
    out = np.empty((128, 7, 128), np.float32)
    ki = np.arange(128)[:, None]
    qi = np.arange(128)[None, :]
    t = 0
    for g, (w, d) in enumerate(GROUPS[:2]):
        for m in range(NCH[g]):
            dist = 128 * m + qi - ki
            ok = (dist >= 0) & (dist <= w) & (dist % d == 0)
            out[:, t, :] = np.where(ok, dist, 1e6)
            t += 1
    return out


def sample_dist_tables():
    out = np.full((128, 12, 16), 1e6, np.float32)
    rho = np.arange(128)
    for s in range(NSEQ_S):
        for i in range(4):
            col = 4 * s + i
            dist = 128 + i - rho
            out[:, 0, col] = np.where(rho >= i, dist, 1e6)
            out[:, 1 + i, col] = 4 * (128 - rho)
            out[:, 5 + i, col] = 16 * (128 - rho)
    for g, (w, d) in enumerate(GROUPS):
        for kk in range(16):
            for qq in range(16):
                if kk // 4 == qq // 4:
                    dd = (qq % 4) - (kk % 4)
                    if dd >= 0 and dd % d == 0:
                        out[kk, 9 + g, qq] = dd
    return out


PLAN = dict(halo=range(4), main=range(8), sample=True)


def build_program():
    nc = bass.Bass("TRN2", target_bir_lowering=False)
    dt_in = lambda n, sh: nc.dram_tensor(n, sh, F32, kind="ExternalInput").ap()
    dt_out = lambda n, sh: nc.dram_tensor(n, sh, F32, kind="ExternalOutput").ap()
    xh = dt_in("xh", [HALO + MAIN, D])
    pp = dt_in("pp", [MAIN, DPLE])
    flag = dt_in("flag", [128, 1])
    xs = dt_in("xs", [TS, D])
    psm = dt_in("psm", [TS, DPLE])
    stc = dt_in("stc", [NSEQ_S, 30, DC])
    cg = [dt_in("cg%d" % g, [NSEQ_S, GROUPS[g][0], 512]) for g in range(3)]
    dtab = dt_in("dtab", [128, 7, 128])
    d2tab = dt_in("d2tab", [128, 5, 32])
    sdtab = dt_in("sdtab", [128, 12, 16])
    g3 = dt_in("g3", [3, D])
    gfin = dt_in("gfin", [1, D])
    cvp = dt_in("cvp", [34, DC])
    w_in = dt_in("w_in", [D, NIN])
    w_co = dt_in("w_co", [DC, D])
    w_ao = dt_in("w_ao", [256, D])
    w_o = dt_in("w_o", [D, D])
    w_fg = dt_in("w_fg", [D, DFF])
    w_fu = dt_in("w_fu", [D, DFF])
    w_fd = dt_in("w_fd", [DFF, D])
    w_pg = dt_in("w_pg", [D, D])
    w_pl = dt_in("w_pl", [DPLE, D])

    y = dt_out("y", [MAIN, D])
    ys = dt_out("ys", [TS, D])
    convp = dt_out("convp", [30, DC])
    convs = dt_out("convs", [NSEQ_S, 30, DC])
    kvp = [dt_out("kvp%d" % g, [GROUPS[g][0], 512]) for g in range(3)]
    kvs = [dt_out("kvs%d" % g, [NSEQ_S, GROUPS[g][0], 512]) for g in range(3)]

    WSPEC = {
        "in": (w_in, 8, 128, NIN), "co": (w_co, 4, 128, D), "ao": (w_ao, 4, 64, D),
        "o": (w_o, 8, 128, D), "fg": (w_fg, 8, 128, DFF), "fu": (w_fu, 8, 128, DFF),
        "fd0": (w_fd, 11, 128, D), "fd1": (w_fd, 11, 128, D),
        "pg": (w_pg, 8, 128, D), "pl": (w_pl, 2, 128, D),
    }
    scratch = {}
    for nm, (ap, nk, pd, ncol) in WSPEC.items():
        scratch[nm] = nc.dram_tensor("wsc_" + nm, [ncol // 128, 128, nk * 128], BF16, kind="Internal").ap()

    with ExitStack() as st:
        S = Sched(nc, st)
        sb = lambda n, sh, dt: st.enter_context(nc.sbuf_tensor(n, sh, dt))
        ident = sb("ident", [128, 128], F32)
        identb = sb("identb", [128, 128], BF16)
        gcols = sb("gcols", [128, 8, 3], F32)
        cvcols = sb("cvcols", [128, 4, 34], F32)
        gfb = sb("gfb", [128, D], F32)
        flag_sb = sb("flag_sb", [128, 1], F32)
        mhalf = sb("mhalf", [128, 8], F32)
        onesb = sb("onesb", [128, 128], BF16)
        dist = sb("dist", [128, 7, 128], F32)
        d2 = sb("d2", [128, 5, 32], F32)
        sdist = sb("sdist", [128, 12, 16], F32)
        x_tm = sb("x_tm", [128, 4, D], F32)
        h_tm = sb("h_tm", [128, 2, D], F32)
        stat = sb("stat", [128, 16], F32)
        hT = sb("hT", [128, 8, 512], BF16)
        qT = sb("qT", [128, 6, 512], BF16)
        kT = [sb("kT%d" % g, [128, 2, RING[g] * 128], BF16) for g in range(2)]
        Vr = [sb("Vr%d" % g, [128, RING[g] * 256], BF16) for g in range(2)]
        kTr2 = sb("kTr2", [128, 2, 16, 128], BF16)
        kTc2 = sb("kTc2", [128, 2, 512], BF16)
        Vr2r = sb("Vr2r", [128, 16, 320], BF16)
        Vc2 = sb("Vc2", [128, 16, 320], BF16)
        gates = sb("gates", [128, 16, 512], BF16)
        u_ext = sb("u_ext", [128, 4, 30 + 512], F32)
        c_sb = sb("c_sb", [128, 2, 512], F32)
        chat = sb("chat", [128, 4, 512], BF16)
        sT = sb("sT", [128, 4, 512], BF16)
        P_sb = sb("P_sb", [128, 4, 512], BF16)
        Oacc = sb("Oacc", [128, 4, 512], F32)
        rl = sb("rl", [128, 512], F32)
        OnT = sb("OnT", [64, 4, 512], BF16)
        tmpA = sb("tmpA", [128, 512], F32)
        tmpB = sb("tmpB", [128, 512], F32)
        actT = sb("actT", [128, 22, 512], BF16)
        p_tm = sb("p_tm", [128, 4, DPLE], F32)
        pT = sb("pT", [128, 2, 512], BF16)
        kv_tm = sb("kv_tm", [128, 4, 512], F32)
        NWB = 7
        WLA = 5
        assert NWB >= WLA + 2
        wbuf = [sb("wbuf%d" % i, [128, 11 * 128], BF16) for i in range(NWB)]
        ps = st.enter_context(nc.psum_tensor("ps", [128, 8, 512], F32))
        prm_rows = tmpA
        g3_rows = h_tm[:, 0, :]

        def psb(bank):
            return ps[:, bank, :].bitcast(BF16)

        wstate = dict(n=0, nstg=0, seen=set(), cur={})

        def wload(nm, ci):
            ap, nk, pd, ncol = WSPEC[nm]
            slot = wstate["n"] % NWB
            wstate["n"] += 1
            key = "wbuf%d" % slot
            wb = wbuf[slot]
            dstv = wb[0:pd, 0:nk * 128]
            sc = scratch[nm][ci]
            if (nm, ci) in wstate["seen"]:
                S.dma("sp", lambda e, dstv=dstv, sc=sc, pd=pd, nk=nk: e.dma_start(out=dstv, in_=sc[0:pd, 0:nk * 128]),
                      reads=[("wsc", nm, ci)], writes=[key], key=key)
            else:
                wstate["seen"].add((nm, ci))
                if nm == "ao":
                    src = ap.rearrange("(s d) n -> d s n", d=64)[:, :, ci * 128:(ci + 1) * 128]
                elif nm == "fd1":
                    src = ap[11 * 128:22 * 128, :].rearrange("(k p) n -> p k n", p=128)[:, :, ci * 128:(ci + 1) * 128]
                elif nm == "fd0":
                    src = ap[0:11 * 128, :].rearrange("(k p) n -> p k n", p=128)[:, :, ci * 128:(ci + 1) * 128]
                else:
                    src = ap.rearrange("(k p) n -> p k n", p=128)[:, :, ci * 128:(ci + 1) * 128]
                dst3 = dstv.rearrange("p (k n) -> p k n", n=128)
                S.dma("pool", lambda e, dst3=dst3, src=src: e.dma_start(out=dst3, in_=src), writes=[key], key=("wbufc", slot))
                S.dma("sp", lambda e, dstv=dstv, sc=sc, pd=pd, nk=nk: e.dma_start(out=sc[0:pd, 0:nk * 128], in_=dstv),
                      reads=[key], writes=[("wsc", nm, ci)], key=("wst", slot))
            wstate["cur"][(nm, ci)] = (slot, key)

        def wget(nm, ci):
            if (nm, ci) not in wstate["cur"]:
                wload(nm, ci)
            slot, key = wstate["cur"].pop((nm, ci))
            return wbuf[slot], key

        def wstream(seq):
            seq = list(seq)
            for i in range(min(WLA, len(seq))):
                if seq[i] not in wstate["cur"]:
                    wload(*seq[i])
            for i, (nm, ci) in enumerate(seq):
                if i + WLA < len(seq) and seq[i + WLA] not in wstate["cur"]:
                    wload(*seq[i + WLA])
                wb, key = wget(nm, ci)
                yield nm, ci, wb, key

        S.dma("sp", lambda e: e.dma_start(out=dist[:], in_=dtab[:]), writes=["dist"], key="c_dist")
        S.dma("sp", lambda e: e.dma_start(out=d2[:], in_=d2tab[:]), writes=["d2"], key="c_d2")
        S.dma("sp", lambda e: e.dma_start(out=sdist[:], in_=sdtab[:]), writes=["sdist"], key="c_sdist")
        S.dma("sp", lambda e: e.dma_start(out=prm_rows[0:34, :], in_=cvp[:]), writes=["tmpA"], key="c_prm")
        S.dma("sp", lambda e: e.dma_start(out=g3_rows[0:3, :], in_=g3[:]), writes=[("h_tm", 0)], key="c_g3")
        S.dma("sp", lambda e: e.dma_start(out=gfb[:], in_=gfin.to_broadcast([128, D])), writes=["gfb"], key="c_gf")
        S.dma("sp", lambda e: e.dma_start(out=flag_sb[:], in_=flag[:]), writes=["flag"], key="c_flag")
        S.op("pool", lambda e: e.memset(ident[:], 1.0), writes=["ident"])
        S.op("pool", lambda e: e.affine_select(ident[:], ident[:], [[1, 128]], ALU.is_equal, 0.0, base=0,
                                               channel_multiplier=-1), reads=["ident"], writes=["ident"])
        S.op("pool", lambda e: e.memset(mhalf[:], -0.5), writes=["mhalf"])
        S.op("dve", lambda e: e.tensor_copy(identb[:], ident[:]), reads=["ident"], writes=["identb"])
        S.op("pool", lambda e: e.memset(onesb[:, :], 1.0), writes=["onesb"])
        S.op("dve", lambda e: e.tensor_copy(onesb[:, 64:128], flag_sb[:, 0:1].to_broadcast([128, 64])),
             reads=["flag", "onesb"], writes=["onesb"])
        for c in range(4):
            S.op("pe", lambda e, c=c: e.transpose(ps[:, 0, c * 34:c * 34 + 34], prm_rows[0:34, c * 128:(c + 1) * 128], ident[0:34, 0:34]),
                 reads=["tmpA", "ident"], writes=[("ps", 0)])
        S.op("dve", lambda e: e.tensor_copy(cvcols[:].rearrange("p c j -> p (c j)"), ps[:, 0, 0:136]), reads=[("ps", 0)], writes=["cvcols"])
        for c in range(8):
            S.op("pe", lambda e, c=c: e.transpose(ps[:, 1, c * 3:c * 3 + 3], g3_rows[0:3, c * 128:(c + 1) * 128], ident[0:3, 0:3]),
                 reads=[("h_tm", 0), "ident"], writes=[("ps", 1)])
        S.op("dve", lambda e: e.tensor_copy(gcols[:].rearrange("p c j -> p (c j)"), ps[:, 1, 0:24]), reads=[("ps", 1)], writes=["gcols"])
        S.op("dve", lambda e: e.memset(u_ext[:, :, 0:30], 0.0), writes=["u_ext_h"])

        def norm_to_hT(nblk, bt, gi, dst, dstkey):
            TT = nblk * bt
            for b in range(nblk):
                hb = b % 2
                S.op("act", lambda e, b=b, hb=hb: e.activation(h_tm[0:bt, hb, :], x_tm[0:bt, b, :], AF.Square,
                                                               accum_out=stat[0:bt, b:b + 1]),
                     reads=[("x", b)], writes=[("h_tm", hb), ("stat", b)])
            S.op("dve", lambda e: e.tensor_scalar(stat[0:bt, 4:4 + nblk], stat[0:bt, 0:nblk], 1.0 / D, EPS, ALU.mult, ALU.add),
                 reads=[("stat", b) for b in range(nblk)], writes=["stat_b"])
            S.op("pool", lambda e: e.tensor_tensor(stat[0:bt, 8:8 + nblk], stat[0:bt, 4:4 + nblk], mhalf[0:bt, 0:nblk], ALU.pow),
                 reads=["stat_b", "mhalf"], writes=["rstd"])
            for b in range(nblk):
                hb = b % 2
                S.op("act", lambda e, b=b, hb=hb: e.activation(h_tm[0:bt, hb, :], x_tm[0:bt, b, :], AF.Copy,
                                                               scale=stat[0:bt, 8 + b:9 + b]),
                     reads=[("x", b), "rstd"], writes=[("h_tm", hb)])
                for half in range(2):
                    bank = 2 + half
                    for k4 in range(4):
                        kc = half * 4 + k4
                        S.op("pe", lambda e, hb=hb, kc=kc, k4=k4, bank=bank: e.transpose(
                            ps[:, bank, k4 * 128:k4 * 128 + bt], h_tm[0:bt, hb, kc * 128:(kc + 1) * 128], ident[0:bt, 0:bt]),
                            reads=[("h_tm", hb), "ident"], writes=[("ps", bank)])
                    S.op("dve", lambda e, b=b, half=half, bank=bank: e.tensor_tensor(
                        dst[:, half * 4:half * 4 + 4, b * 128:b * 128 + bt],
                        ps[:, bank, 0:512].rearrange("p (k n) -> p k n", n=128)[:, :, 0:bt],
                        gcols[:, half * 4:half * 4 + 4, gi:gi + 1].to_broadcast([128, 4, bt]), ALU.mult),
                        reads=[("ps", bank), "gcols"], writes=[dstkey])

        fm_bank = [0]

        def fm_matmul(wb, wkey, nk, pd, rhs_fn, rhs_keys, TT, bank=None):
            if bank is None:
                bank = fm_bank[0] % 2
                fm_bank[0] += 1
            key = ("ps", bank)

            def fn(e, bank=bank, wb=wb):
                r = None
                for k in range(nk):
                    r = e.matmul(ps[:, bank, 0:TT], wb[0:pd, k * 128:(k + 1) * 128], rhs_fn(k),
                                 start=(k == 0), stop=(k == nk - 1))
                return r
            S.op("pe", fn, reads=[wkey] + list(rhs_keys), writes=[key])
            return bank, key

        def tm_matmul(bank, pieces, lhs_fn, lhs_keys, nblk, bt, ncol=128, col0=0):
            key = ("ps", bank)
            tot = sum(p[2] for p in pieces)

            def fn(e):
                r = None
                for b in range(nblk):
                    i = 0
                    for (wb, wkey, nk, kofs, pd) in pieces:
                        for k in range(nk):
                            r = e.matmul(ps[0:bt, bank, b * 128 + col0:b * 128 + col0 + ncol], lhs_fn(kofs + k, b),
                                         wb[0:pd, k * 128:k * 128 + ncol], start=(i == 0), stop=(i == tot - 1))
                            i += 1
                return r
            S.op("pe", fn, reads=[p[1] for p in pieces] + list(lhs_keys), writes=[key])
            return key

        def resid_add(bank, oc, nblk, bt):
            S.op("dve", lambda e: e.tensor_tensor(
                x_tm[0:bt, 0:nblk, oc * 128:(oc + 1) * 128],
                ps[0:bt, bank, 0:nblk * 128].rearrange("p (b n) -> p b n", n=128),
                x_tm[0:bt, 0:nblk, oc * 128:(oc + 1) * 128], ALU.add),
                reads=[("ps", bank)] + [("x", b) for b in range(nblk)], writes=[("x", b) for b in range(nblk)])

        def attn_block(h, qcols, chunks, pso_cols):
            n = len(chunks)
            for b0 in range(0, n, 4):
                attn_block.pending.append(dict(
                    h=h, qcols=qcols, grp=chunks[b0:b0 + 4], pso=pso_cols, obank=attn_block.obank,
                    first=(b0 == 0 and attn_block.first), last=(b0 + 4 >= n and attn_block.last),
                    pre=attn_block.pre, post=[]))
                attn_block.pre = []
        attn_block.pending = []
        attn_block.pre = []
        attn_block.first = True
        attn_block.last = True
        attn_block.obank = 6
        SBANKS = (4, 5, 0)

        def attn_flush(filler=None, L=2, filler2=None):
            bl = attn_block.pending
            attn_block.pending = []
            n = len(bl)

            def stage_S(t):
                B = bl[t]
                for f in B["pre"]:
                    f()
                if "custom" in B:
                    B["custom"]["SE"](SBANKS[t % 3], t % 4)
                    return
                h = B["h"]
                po, qc = 64 * (h % 2), h // 2
                qcols = B["qcols"]
                nq = qcols.stop - qcols.start
                sbank = SBANKS[t % 3]
                pb = t % 4
                grp = B["grp"]
                skey = ("ps", sbank)

                def fn_s(e, grp=grp, sbank=sbank, po=po, qc=qc, qcols=qcols, nq=nq):
                    r = None
                    for j, c in enumerate(grp):
                        r = e.matmul(ps[0:c[4], sbank, j * 128:j * 128 + nq], c[0], qT[po:po + 64, qc, qcols],
                                     start=True, stop=True)
                    return r
                rk = set()
                for c in grp:
                    rk.update(c[1])
                B["rk"] = rk
                S.op("pe", fn_s, reads=["qT"] + list(rk), writes=[skey])
                same = all(c[4] == 128 for c in grp) and nq == 128 and grp[0][5] is not None
                sl = -SLOPES[h]
                if same:
                    W = len(grp) * 128
                    dall = grp[0][5][:, 0:W]
                    S.op("dve", lambda e, sbank=sbank, dall=dall, W=W, sl=sl: e.scalar_tensor_tensor(
                        ps[:, sbank, 0:W], dall, sl, ps[:, sbank, 0:W], ALU.mult, ALU.add),
                        reads=[skey, "dist"], writes=[skey])
                    S.op("act", lambda e, sbank=sbank, pb=pb, W=W: e.activation(P_sb[:, pb, 0:W], ps[:, sbank, 0:W], AF.Exp),
                         reads=[skey], writes=[("P_sb", pb)])
                else:
                    for j, c in enumerate(grp):
                        nk = c[4]
                        S.op("dve", lambda e, sbank=sbank, dap=c[3], nk=nk, j=j, sl=sl, nq=nq: e.scalar_tensor_tensor(
                            ps[0:nk, sbank, j * 128:j * 128 + nq], dap, sl, ps[0:nk, sbank, j * 128:j * 128 + nq],
                            ALU.mult, ALU.add), reads=[skey, "sdist"], writes=[skey])
                        S.op("act", lambda e, sbank=sbank, pb=pb, nk=nk, j=j, nq=nq: e.activation(
                            P_sb[0:nk, pb, j * 128:j * 128 + nq], ps[0:nk, sbank, j * 128:j * 128 + nq], AF.Exp),
                            reads=[skey], writes=[("P_sb", pb)])

            def stage_O(t):
                B = bl[t]
                if "custom" in B:
                    B["custom"]["O"](t % 4)
                    for f in B["post"]:
                        f()
                    return
                qcols = B["qcols"]
                nq = qcols.stop - qcols.start
                pb = t % 4
                grp, ob, pso = B["grp"], B["obank"], B["pso"]
                ng = len(grp)

                def fn_o(e, grp=grp, pb=pb, ob=ob, pso=pso, nq=nq, first=B["first"], last=B["last"], ng=ng):
                    r = None
                    for j, c in enumerate(grp):
                        st_ = (j == 0 and first)
                        sp_ = (j == ng - 1 and last)
                        e.matmul(ps[0:64, ob, pso], c[2], P_sb[0:c[4], pb, j * 128:j * 128 + nq], start=st_, stop=sp_)
                        r = e.matmul(ps[64:128, ob, pso], c[6], P_sb[0:c[4], pb, j * 128:j * 128 + nq], start=st_, stop=sp_)
                    return r
                S.op("pe", fn_o, reads=[("P_sb", pb), "onesb"] + list(B["rk"]), writes=[("ps", ob)])
                for f in B["post"]:
                    f()

            fdone = [False]
            for t in range(n + L):
                if t < n:
                    stage_S(t)
                for _r in range(2):
                    if filler is not None and not fdone[0]:
                        v = next(filler, None)
                        if v != "tap":
                            fdone[0] = True
                if filler2 is not None and t % 3 == 1:
                    next(filler2, None)
                if t - L >= 0:
                    stage_O(t - L)

        def g2_insert(tg):
            sl = tg % 4
            for hc in range(2):
                S.op("act", lambda e, hc=hc, sl=sl: e.activation(
                    kTr2[:, hc, :, sl * 32:(sl + 1) * 32], kTc2[:, hc, :].rearrange("p (r u) -> p r u", u=32), AF.Copy),
                    reads=["kTc2"], writes=["kTr2"])
            S.dma("sp", lambda e, sl=sl: e.dma_start(out=Vr2r[32 * sl:32 * sl + 32, :, :], in_=Vc2[0:32, :, :]),
                  reads=["Vc2"], writes=["Vr2r"], key="ins_v2")

        def ring_slot(g, gb):
            return gb % RING[g]

        def vap_of(g, slot, hs, halo):
            base = slot * 256 + hs * 64
            return Vr[g][:, base:base + 64]

        def run_tile(kind, idx):
            sample = kind == "sample"
            halo = kind == "halo"
            nblk, bt = (1, TS) if sample else (4, 128)
            TT = nblk * bt
            gb0 = (4 * idx) if halo else (16 + 4 * idx)
            last = (kind == "main" and idx == 7)
            for b in range(nblk):
                if sample:
                    src = xs[:, :]
                elif halo:
                    src = xh[(idx * 4 + b) * 128:(idx * 4 + b + 1) * 128, :]
                else:
                    src = xh[HALO + (idx * 4 + b) * 128:HALO + (idx * 4 + b + 1) * 128, :]
                S.dma("sp", lambda e, b=b, src=src: e.dma_start(out=x_tm[0:bt, b, :], in_=src),
                      writes=[("x", b)], key=("ldx", b))
            if PLAN.get("stage", 99) < 1:
                return
            norm_to_hT(nblk, bt, 0, hT, "hT")
            if PLAN.get("stage", 99) < 2:
                return
            if halo:
                grps = [2] + ([1] if idx == 3 else []) + ([0] if idx == 3 else [])
                need_ab = idx == 3
            else:
                grps = [0, 1, 2]
                need_ab = True
            seq = []
            if need_ab:
                seq += [("in", c) for c in (4, 5, 6, 7, 0, 1, 2, 3)]
            if not halo:
                seq += [("in", 8 + c) for c in range(6)]
            for g in grps:
                seq += [("in", 14 + 2 * g), ("in", 15 + 2 * g), ("in", 20 + 2 * g), ("in", 21 + 2 * g)]
            if sample:
                seq += [("in", 26 + c) for c in range(16)]
            rhs_h = lambda k: hT[:, k, 0:TT]
            for nm, ci, wb, wkey in wstream(seq):
                if PLAN.get("wonly"):
                    S.op("dve", lambda e, wb=wb: e.tensor_copy(tmpA[:, 0:8], wb[:, 0:8]), reads=[wkey], writes=["tmpA"])
                    continue
                if PLAN.get("skipk") and 14 <= ci < 20:
                    continue
                if PLAN.get("skipv") and 20 <= ci < 26:
                    continue
                if ci < 8 or (8 <= ci < 20) or ci >= 26:
                    bank, pkey = fm_matmul(wb, wkey, 8, 128, rhs_h, ["hT"], TT)
                if 4 <= ci < 8:
                    c = ci - 4
                    S.op("act", lambda e, bank=bank, c=c: e.activation(gates[:, c, 0:TT], ps[:, bank, 0:TT], AF.Sigmoid),
                         reads=[pkey], writes=[("gates", c)])
                elif ci < 4:
                    c = ci
                    if sample:
                        dstu = u_ext[:, c, 0:4 * 34].rearrange("p (s t) -> p s t", t=34)[:, :, 30:34]
                        srcp = ps[:, bank, 0:TT].rearrange("p (s t) -> p s t", t=4)
                        srcg = gates[:, c, 0:TT].rearrange("p (s t) -> p s t", t=4)
                    else:
                        dstu = u_ext[:, c, 30:30 + TT]
                        srcp = ps[:, bank, 0:TT]
                        srcg = gates[:, c, 0:TT]
                    if halo:
                        S.op("dve", lambda e, bank=bank, c=c: e.tensor_tensor(tmpA[:, 0:30], ps[:, bank, TT - 30:TT], gates[:, c, TT - 30:TT], ALU.mult),
                             reads=[pkey, ("gates", c)], writes=["tmpA"])
                        S.op("dve", lambda e, c=c: e.tensor_scalar(u_ext[:, c, 0:30], tmpA[:, 0:30], flag_sb[:, 0:1], None, ALU.mult),
                             reads=["tmpA", "flag"], writes=["u_ext_h"])
                    else:
                        S.op("dve", lambda e, dstu=dstu, srcp=srcp, srcg=srcg: e.tensor_tensor(dstu, srcp, srcg, ALU.mult),
                             reads=[pkey, ("gates", c)], writes=[("u_ext", c)])
                elif 8 <= ci < 14:
                    c = ci - 8
                    if kind == "main" and c >= 4:
                        S.op("act", lambda e, bank=bank, c=c: e.activation(
                            qT[:, c, :].rearrange("p (r u) -> p u r", u=32),
                            ps[:, bank, 0:512].rearrange("p (u r) -> p u r", r=16), AF.Copy, scale=0.125),
                            reads=[pkey], writes=["qT"])
                    else:
                        S.op("act", lambda e, bank=bank, c=c: e.activation(qT[:, c, 0:TT], ps[:, bank, 0:TT], AF.Copy, scale=0.125),
                             reads=[pkey], writes=["qT"])
                elif 14 <= ci < 20:
                    g, hc = (ci - 14) // 2, (ci - 14) % 2
                    if g == 2 and sample:
                        S.op("dve", lambda e, bank=bank, hc=hc: e.tensor_copy(kTc2[:, hc, 0:TT], ps[:, bank, 0:TT]),
                             reads=[pkey], writes=[("kTs", 2), "kTc2"])
                    elif g == 2:
                        S.op("dve", lambda e, bank=bank, hc=hc: e.tensor_copy(
                            kTc2[:, hc, :].rearrange("p (r u) -> p u r", u=32),
                            ps[:, bank, 0:512].rearrange("p (u r) -> p u r", r=16)),
                            reads=[pkey], writes=["kTc2"])
                    elif sample:
                        S.op("dve", lambda e, bank=bank, g=g, hc=hc: e.tensor_copy(kT[g][:, hc, 0:TT], ps[:, bank, 0:TT]),
                             reads=[pkey], writes=[("kTs", g), ("kT", g, 0)])
                    else:
                        for b in range(nblk):
                            sl = ring_slot(g, gb0 + b)
                            S.op("dve", lambda e, bank=bank, g=g, hc=hc, b=b, sl=sl: e.tensor_copy(
                                kT[g][:, hc, sl * 128:(sl + 1) * 128], ps[:, bank, b * 128:(b + 1) * 128]),
                                reads=[pkey], writes=[("kT", g, sl)])
                    need_out = sample or (kind == "main" and ((g == 2 and idx >= 4) or (g == 1 and idx == 7) or (g == 0 and idx == 7)))
                    if need_out:
                        tm_matmul(7, [(wb, wkey, 8, 0, 128)], lambda k, b: hT[:, k, b * 128:b * 128 + bt], ["hT"], nblk, bt)
                        S.op("act", lambda e, hc=hc: e.activation(
                            kv_tm[0:bt, hc, 0:nblk * 128],
                            ps[0:bt, 7, 0:nblk * 128], AF.Copy), reads=[("ps", 7)], writes=[("kv_tm", 0, hc)])
                        kv_out(kind, idx, g, 0, hc, nblk, bt)
                elif 20 <= ci < 26:
                    g, hc = (ci - 20) // 2, (ci - 20) % 2
                    need_out = sample or (kind == "main" and ((g == 2 and idx >= 4) or (g == 1 and idx == 7) or (g == 0 and idx == 7)))
                    if g == 2 and not sample:
                        for rg in range(4):
                            vb_ = 6 + (rg % 2)

                            def fn_v(e, rg=rg, vb_=vb_, wb=wb):
                                r_ = None
                                for r4 in range(4):
                                    r = 4 * rg + r4
                                    for k in range(8):
                                        r_ = e.matmul(ps[0:32, vb_, r4 * 128:(r4 + 1) * 128], hT[:, k, r:512:16],
                                                      wb[:, k * 128:(k + 1) * 128], start=(k == 0), stop=(k == 7))
                                return r_
                            S.op("pe", fn_v, reads=[wkey, "hT"], writes=[("ps", vb_)])
                            S.op("dve", lambda e, rg=rg, vb_=vb_, hc=hc: e.tensor_copy(
                                Vc2[0:32, 4 * rg:4 * rg + 4, hc * 128:(hc + 1) * 128],
                                ps[0:32, vb_, 0:512].rearrange("p (r n) -> p r n", n=128)),
                                reads=[("ps", vb_)], writes=["Vc2"])
                        if not need_out:
                            continue
                    tm_matmul(7, [(wb, wkey, 8, 0, 128)], lambda k, b: hT[:, k, b * 128:b * 128 + bt], ["hT"], nblk, bt)
                    if g == 2 and sample:
                        S.op("dve", lambda e, hc=hc: e.tensor_copy(Vc2[0:bt, 0, hc * 128:(hc + 1) * 128], ps[0:bt, 7, 0:128]),
                             reads=[("ps", 7)], writes=[("Vs", 2), "Vc2"])
                    elif g == 2:
                        pass
                    elif sample:
                        S.op("dve", lambda e, g=g, hc=hc: e.tensor_copy(Vr[g][0:bt, hc * 128:(hc + 1) * 128], ps[0:bt, 7, 0:128]),
                             reads=[("ps", 7)], writes=[("Vs", g), ("V", g, 0)])
                    else:
                        for b in range(nblk):
                            sl = ring_slot(g, gb0 + b)
                            S.op("dve", lambda e, g=g, hc=hc, b=b, sl=sl: e.tensor_copy(
                                Vr[g][:, sl * 256 + hc * 128:sl * 256 + (hc + 1) * 128], ps[:, 7, b * 128:(b + 1) * 128]),
                                reads=[("ps", 7)], writes=[("V", g, sl)])
                    need_out = sample or (kind == "main" and ((g == 2 and idx >= 4) or (g == 1 and idx == 7) or (g == 0 and idx == 7)))
                    if need_out:
                        S.op("act", lambda e, hc=hc: e.activation(kv_tm[0:bt, 2 + hc, 0:nblk * 128], ps[0:bt, 7, 0:nblk * 128], AF.Copy),
                             reads=[("ps", 7)], writes=[("kv_tm", 1, hc)])
                        kv_out(kind, idx, g, 1, hc, nblk, bt)
                elif ci >= 26:
                    c = ci - 26
                    S.op("act", lambda e, bank=bank, c=c: e.activation(gates[:, c, 0:TT], ps[:, bank, 0:TT], AF.Sigmoid),
                         reads=[pkey], writes=[("gates", c)])
            if not sample:
                oc0 = 64 if halo else 0
                S.op("dve", lambda e, oc0=oc0: e.tensor_copy(Vc2[0:32, :, 256:320], onesb[0:32, oc0:oc0 + 64].unsqueeze(1).to_broadcast([32, 16, 64])),
                     reads=["onesb"], writes=["Vc2"])
            if halo:
                g2_insert(idx)
                return
            def gates_gen():
                for nm, ci, wb, wkey in wstream([("in", 26 + c) for c in range(16)]):
                    c = ci - 26
                    bank, pkey = fm_matmul(wb, wkey, 8, 128, rhs_h, ["hT"], TT, bank=7)
                    S.op("act", lambda e, bank=bank, c=c: e.activation(gates[:, c, 0:TT], ps[:, bank, 0:TT], AF.Sigmoid),
                         reads=[pkey], writes=[("gates", c)])
                    yield "g"
            gen = conv_branch(kind, idx, nblk, bt, TT, last)
            if sample:
                for _ in gen:
                    pass
                sample_attention()
            else:
                gg = gates_gen()
                prompt_attention(idx, filler=gen, filler2=gg)
                for _ in gg:
                    pass
            for hs in range(4):
                S.op("dve", lambda e, hs=hs: e.reciprocal(rl[0:64, 0:TT], Oacc[64:128, hs, 0:TT]),
                     reads=[("Oacc", hs)], writes=["rl"])
                S.op("dve", lambda e, hs=hs: e.tensor_tensor(OnT[0:64, hs, 0:TT], Oacc[0:64, hs, 0:TT], rl[0:64, 0:TT], ALU.mult),
                     reads=[("Oacc", hs), "rl"], writes=["OnT"])
            if not sample:
                for _ in gen:
                    pass
            seq = []
            for oc in range(8):
                seq += [("ao", oc), ("co", oc)]
            it = wstream(seq)
            for oc in range(8):
                _, _, wbA, keyA = next(it)
                bankA, pkA = fm_matmul(wbA, keyA, 4, 64, lambda k: OnT[0:64, k, 0:TT], ["OnT"], TT)
                S.op("dve", lambda e, bankA=bankA, oc=oc: e.tensor_tensor(tmpA[:, 0:TT], ps[:, bankA, 0:TT], gates[:, 8 + oc, 0:TT], ALU.mult),
                     reads=[pkA, ("gates", 8 + oc)], writes=["tmpA"])
                _, _, wbC, keyC = next(it)
                bankC, pkC = fm_matmul(wbC, keyC, 4, 128, lambda k: sT[:, k, 0:TT], ["sT"], TT)
                S.op("dve", lambda e, bankC=bankC, oc=oc: e.tensor_tensor(tmpB[:, 0:TT], ps[:, bankC, 0:TT], gates[:, oc, 0:TT], ALU.mult),
                     reads=[pkC, ("gates", oc)], writes=["tmpB"])
                S.op("dve", lambda e, oc=oc: e.tensor_tensor(hT[:, oc, 0:TT], tmpA[:, 0:TT], tmpB[:, 0:TT], ALU.add),
                     reads=["tmpA", "tmpB"], writes=["hT"])
            for i, (nm, ci, wb, wkey) in enumerate(wstream([("o", oc) for oc in range(8)])):
                bank = 2 + i % 2
                tm_matmul(bank, [(wb, wkey, 8, 0, 128)], lambda k, b: hT[:, k, b * 128:b * 128 + bt], ["hT"], nblk, bt)
                resid_add(bank, ci, nblk, bt)
            if sample:
                ks = ["cones"] + [(n, cb) for n in ("ctile", "cv", "ck") for cb in range(2)]
                S.op("pool", lambda e: e.memset(actT[:, 21, 0:2], 0.0), reads=ks, writes=["actT"])
            norm_to_hT(nblk, bt, 1, hT, "hT")
            seq = []
            for fc in range(22):
                seq += [("fg", fc), ("fu", fc)]
            it = wstream(seq)
            for fc in range(22):
                _, _, wbG, keyG = next(it)
                bankG, pkG = fm_matmul(wbG, keyG, 8, 128, rhs_h, ["hT"], TT)
                tmpX, tkey = (tmpA, "tmpA") if fc % 2 == 0 else (tmpB, "tmpB")
                S.op("act", lambda e, bankG=bankG, tmpX=tmpX: e.activation(tmpX[:, 0:TT], ps[:, bankG, 0:TT], AF.Silu),
                     reads=[pkG], writes=[tkey])
                _, _, wbU, keyU = next(it)
                bankU, pkU = fm_matmul(wbU, keyU, 8, 128, rhs_h, ["hT"], TT)
                S.op("dve", lambda e, bankU=bankU, fc=fc, tmpX=tmpX: e.tensor_tensor(actT[:, fc, 0:TT], tmpX[:, 0:TT], ps[:, bankU, 0:TT], ALU.mult),
                     reads=[pkU, tkey], writes=["actT"])
            seq = []
            for oc in range(8):
                seq += [("fd0", oc), ("fd1", oc)]
            it = wstream(seq)
            for oc in range(8):
                _, _, wb0, k0 = next(it)
                _, _, wb1, k1 = next(it)
                bank = 2 + oc % 2
                tm_matmul(bank, [(wb0, k0, 11, 0, 128), (wb1, k1, 11, 11, 128)],
                          lambda k, b: actT[:, k, b * 128:b * 128 + bt], ["actT"], nblk, bt)
                resid_add(bank, oc, nblk, bt)
            norm_to_hT(nblk, bt, 2, hT, "hT")
            for b in range(nblk):
                src = psm[:, :] if sample else pp[(idx * 4 + b) * 128:(idx * 4 + b + 1) * 128, :]
                S.dma("sp", lambda e, b=b, src=src: e.dma_start(out=p_tm[0:bt, b, :], in_=src), writes=[("p_tm", b)], key=("ldp", b))
                for k in range(2):
                    S.op("pe", lambda e, b=b, k=k: e.transpose(ps[:, 7, k * 128:k * 128 + bt], p_tm[0:bt, b, k * 128:(k + 1) * 128], ident[0:bt, 0:bt]),
                         reads=[("p_tm", b), "ident"], writes=[("ps", 7)])
                S.op("act", lambda e, b=b: e.activation(pT[:, :, b * 128:b * 128 + bt], ps[:, 7, 0:256].rearrange("p (k n) -> p k n", n=128)[:, :, 0:bt], AF.Copy),
                     reads=[("ps", 7)], writes=["pT"])
            seq = []
            for oc in range(8):
                seq += [("pg", oc), ("pl", oc)]
            it = wstream(seq)
            for oc in range(8):
                bG, bP = (2, 3) if oc % 2 == 0 else (6, 7)
                tmpX, tkey = (tmpA, "tmpA") if oc % 2 == 0 else (tmpB, "tmpB")
                _, _, wbg, kg = next(it)
                tm_matmul(bG, [(wbg, kg, 8, 0, 128)], lambda k, b: hT[:, k, b * 128:b * 128 + bt], ["hT"], nblk, bt)
                S.op("act", lambda e, bG=bG, tmpX=tmpX: e.activation(tmpX[0:bt, 0:nblk * 128], ps[0:bt, bG, 0:nblk * 128], AF.Sigmoid),
                     reads=[("ps", bG)], writes=[tkey])
                _, _, wbp, kp = next(it)
                tm_matmul(bP, [(wbp, kp, 2, 0, 128)], lambda k, b: pT[:, k, b * 128:b * 128 + bt], ["pT"], nblk, bt)
                S.op("dve", lambda e, bP=bP, tmpX=tmpX: e.tensor_tensor(tmpX[0:bt, 0:nblk * 128], tmpX[0:bt, 0:nblk * 128], ps[0:bt, bP, 0:nblk * 128], ALU.mult),
                     reads=[("ps", bP), tkey], writes=[tkey])
                S.op("dve", lambda e, oc=oc, tmpX=tmpX: e.tensor_tensor(
                    x_tm[0:bt, 0:nblk, oc * 128:(oc + 1) * 128],
                    tmpX[0:bt, 0:nblk * 128].rearrange("p (b n) -> p b n", n=128),
                    x_tm[0:bt, 0:nblk, oc * 128:(oc + 1) * 128], ALU.add),
                    reads=[tkey] + [("x", b) for b in range(nblk)], writes=[("x", b) for b in range(nblk)])
            for b in range(nblk):
                hb = b % 2
                S.op("act", lambda e, b=b, hb=hb: e.activation(h_tm[0:bt, hb, :], x_tm[0:bt, b, :], AF.Square, accum_out=stat[0:bt, b:b + 1]),
                     reads=[("x", b)], writes=[("h_tm", hb), ("stat", b)])
            S.op("dve", lambda e: e.tensor_scalar(stat[0:bt, 4:4 + nblk], stat[0:bt, 0:nblk], 1.0 / D, EPS, ALU.mult, ALU.add),
                 reads=[("stat", b) for b in range(nblk)], writes=["stat_b"])
            S.op("pool", lambda e: e.tensor_tensor(stat[0:bt, 8:8 + nblk], stat[0:bt, 4:4 + nblk], mhalf[0:bt, 0:nblk], ALU.pow),
                 reads=["stat_b", "mhalf"], writes=["rstd"])
            for b in range(nblk):
                hb = b % 2
                S.op("dve", lambda e, b=b, hb=hb: e.scalar_tensor_tensor(h_tm[0:bt, hb, :], x_tm[0:bt, b, :], stat[0:bt, 8 + b:9 + b], gfb[0:bt, :], ALU.mult, ALU.mult),
                     reads=[("x", b), "rstd", "gfb"], writes=[("h_tm", hb)])
                dst = ys[:, :] if sample else y[(idx * 4 + b) * 128:(idx * 4 + b + 1) * 128, :]
                S.dma("pool", lambda e, hb=hb, dst=dst: e.dma_start(out=dst, in_=h_tm[0:bt, hb, :]), reads=[("h_tm", hb)], key=("sty", hb))

        def kv_out(kind, idx, g, which, hc, nblk, bt):
            W = GROUPS[g][0]
            col0 = which * 256 + hc * 128
            if kind == "sample":
                for s in range(NSEQ_S):
                    S.dma("pool", lambda e, s=s, g=g, which=which, col0=col0, W=W, hc=hc: e.dma_start(
                        out=kvs[g][s, W - 4:W, col0:col0 + 128], in_=kv_tm[4 * s:4 * s + 4, 2 * which + hc, 0:128]),
                        reads=[("kv_tm", which, hc)], key=("stkv", which, hc, s))
            else:
                for b in range(nblk):
                    tok0 = (idx * 4 + b) * 128 - (MAIN - W)
                    if tok0 < 0:
                        continue
                    S.dma("pool", lambda e, b=b, g=g, which=which, col0=col0, tok0=tok0, hc=hc: e.dma_start(
                        out=kvp[g][tok0:tok0 + 128, col0:col0 + 128], in_=kv_tm[:, 2 * which + hc, b * 128:(b + 1) * 128]),
                        reads=[("kv_tm", which, hc)], key=("stkv", which, hc, b))

        def conv_branch(kind, idx, nblk, bt, TT, last):
            sample = kind == "sample"
            if sample:
                for s in range(NSEQ_S):
                    S.dma("sp", lambda e, s=s: e.dma_start(out=p_tm[0:30, 0:2, :].rearrange("p a b -> p (a b)"), in_=stc[s, :, :]),
                          writes=["stc_sb"], key="ldstc")
                    for c in range(4):
                        S.op("pe", lambda e, c=c: e.transpose(ps[:, 7, c * 32:c * 32 + 30], p_tm[0:30, 0:2, :].rearrange("p a b -> p (a b)")[:, c * 128:(c + 1) * 128], ident[0:30, 0:30]),
                             reads=["stc_sb", "ident"], writes=[("ps", 7)])
                    S.op("dve", lambda e, s=s: e.tensor_copy(
                        u_ext[:, :, 0:4 * 34].rearrange("p c (s t) -> p c s t", t=34)[:, :, s, 0:30],
                        ps[:, 7, 0:128].rearrange("p (c t) -> p c t", t=32)[:, :, 0:30]),
                        reads=[("ps", 7)], writes=[("u_ext", c) for c in range(4)] + ["u_ext_h"])
                    S.dma("sp", lambda e, s=s: e.dma_start(out=convs[s, 0:26, :], in_=stc[s, 4:30, :]), key="cpconvs")
            cbufs = [(tmpA, "tmpA"), (tmpB, "tmpB"), (c_sb[:, 0, :], ("c_sb", 0)), (c_sb[:, 1, :], ("c_sb", 1))]
            if sample:
                x_c = Oacc
                xck = lambda b: [("Oacc", b)]
            else:
                x_c = actT[:, 0:8, :].rearrange("p a b -> p (a b)").bitcast(F32).rearrange("p (b n) -> p b n", n=DC)
                xck = lambda b: ["actT"]
            for cp in range(2):
                for j in range(31):
                    for c in (2 * cp, 2 * cp + 1):
                        bank = 2 + (c % 2)
                        if sample:
                            src = u_ext[:, c, 0:4 * 34].rearrange("p (s t) -> p s t", t=34)[:, :, j:j + 4]
                            acc = ps[:, bank, 0:TT].rearrange("p (s t) -> p s t", t=4)
                        else:
                            src = u_ext[:, c, j:j + TT]
                            acc = ps[:, bank, 0:TT]
                        rk = [("u_ext", c), "cvcols", "u_ext_h"]
                        if j == 0:
                            S.op("dve", lambda e, src=src, acc=acc, c=c: e.tensor_scalar(acc, src, cvcols[:, c, 0:1], cvcols[:, c, 31:32], ALU.mult, ALU.add),
                                 reads=rk, writes=[("ps", bank)])
                        else:
                            S.op("dve", lambda e, src=src, acc=acc, c=c, j=j: e.scalar_tensor_tensor(acc, src, cvcols[:, c, j:j + 1], acc, ALU.mult, ALU.add),
                                 reads=rk + [("ps", bank)], writes=[("ps", bank)])
                        yield "tap"
                for c in (2 * cp, 2 * cp + 1):
                    bank = 2 + (c % 2)
                    cb_ap, cb_key = cbufs[c]
                    S.op("act", lambda e, cb_ap=cb_ap, bank=bank: e.activation(cb_ap[:, 0:TT], ps[:, bank, 0:TT], AF.Copy),
                         reads=[("ps", bank)], writes=[cb_key])
            for c in range(4):
                cb_ap, cb_key = cbufs[c]
                for b in range(nblk):
                    S.op("pe", lambda e, cb_ap=cb_ap, b=b: e.transpose(ps[0:bt, 7, b * 128:(b + 1) * 128], cb_ap[:, b * 128:b * 128 + bt], ident[:, :]),
                         reads=[cb_key, "ident"], writes=[("ps", 7)])
                    yield "tap"
                S.op("act", lambda e, c=c: e.activation(
                    x_c[0:bt, 0:nblk, c * 128:(c + 1) * 128], ps[0:bt, 7, 0:nblk * 128].rearrange("p (b n) -> p b n", n=128), AF.Copy),
                    reads=[("ps", 7)], writes=[k_ for b in range(nblk) for k_ in xck(b)])
                yield "tap"
            for b in range(nblk):
                S.op("act", lambda e, b=b: e.activation(tmpA[0:bt, 0:DC], x_c[0:bt, b, :], AF.Copy, accum_out=stat[0:bt, 12:13]),
                     reads=xck(b), writes=["tmpA", "ln_s1"])
                S.op("act", lambda e, b=b: e.activation(tmpA[0:bt, 0:DC], x_c[0:bt, b, :], AF.Square, accum_out=stat[0:bt, 13:14]),
                     reads=xck(b), writes=["tmpA", "ln_s2"])
                yield "tap"
                S.op("dve", lambda e: e.tensor_scalar(stat[0:bt, 14:15], stat[0:bt, 12:13], 1.0 / DC, None, ALU.mult),
                     reads=["ln_s1"], writes=["ln_mu"])
                S.op("dve", lambda e: e.tensor_tensor(stat[0:bt, 15:16], stat[0:bt, 14:15], stat[0:bt, 14:15], ALU.mult),
                     reads=["ln_mu"], writes=["ln_mu2"])
                S.op("dve", lambda e: e.scalar_tensor_tensor(stat[0:bt, 13:14], stat[0:bt, 13:14], 1.0 / DC, stat[0:bt, 15:16], ALU.mult, ALU.subtract),
                     reads=["ln_s2", "ln_mu2"], writes=["ln_var"])
                S.op("dve", lambda e: e.tensor_scalar(stat[0:bt, 13:14], stat[0:bt, 13:14], EPS, None, ALU.add),
                     reads=["ln_var"], writes=["ln_var"])
                S.op("pool", lambda e: e.tensor_tensor(stat[0:bt, 12:13], stat[0:bt, 13:14], mhalf[0:bt, 0:1], ALU.pow),
                     reads=["ln_var", "mhalf"], writes=["ln_rstd"])
                S.op("dve", lambda e, b=b: e.tensor_scalar(chat[0:bt, b, :], x_c[0:bt, b, :], stat[0:bt, 14:15], stat[0:bt, 12:13], ALU.subtract, ALU.mult),
                     reads=xck(b) + ["ln_mu", "ln_rstd"], writes=[("chat", b)])
                yield "tap"
                for c in range(4):
                    S.op("pe", lambda e, b=b, c=c: e.transpose(psb(7)[:, c * 128:c * 128 + bt], chat[0:bt, b, c * 128:(c + 1) * 128], identb[0:bt, 0:bt]),
                         reads=[("chat", b), "identb"], writes=[("ps", 7)])
                for c in range(4):
                    S.op("act", lambda e, b=b, c=c: e.activation(sT[:, c, b * 128:b * 128 + bt], psb(7)[:, c * 128:c * 128 + bt], AF.Silu,
                                                               scale=cvcols[:, c, 32:33], bias=cvcols[:, c, 33:34]),
                         reads=[("ps", 7), "cvcols"], writes=["sT"])
                yield "tap"
            if sample or last:
                for c in range(4):
                    if sample:
                        srcu = u_ext[:, c, 0:4 * 34].rearrange("p (s t) -> p s t", t=34)[:, :, 30:34]
                        S.op("act", lambda e, srcu=srcu: e.activation(c_sb[:, 0, 0:16].rearrange("p (s t) -> p s t", t=4), srcu, AF.Copy),
                             reads=[("u_ext", c)], writes=[("c_sb", 0)])
                        S.op("pe", lambda e, c=c: e.transpose(ps[0:16, 7, c * 128:(c + 1) * 128], c_sb[:, 0, 0:16], ident[:, :]),
                             reads=[("c_sb", 0), "ident"], writes=[("ps", 7)])
                    else:
                        S.op("pe", lambda e, c=c: e.transpose(ps[:, 7, c * 128:(c + 1) * 128], u_ext[:, c, 30 + 384:30 + 512], ident[:, :]),
                             reads=[("u_ext", c), "ident"], writes=[("ps", 7)])
                n_r = 16 if sample else 128
                S.op("act", lambda e: e.activation(tmpA[0:n_r, 0:512], ps[0:n_r, 7, 0:512], AF.Copy), reads=[("ps", 7)], writes=["tmpA"])
                if sample:
                    for s in range(NSEQ_S):
                        S.dma("pool", lambda e, s=s: e.dma_start(out=convs[s, 26:30, :], in_=tmpA[4 * s:4 * s + 4, 0:512]), reads=["tmpA"], key="stconv")
                else:
                    S.dma("pool", lambda e: e.dma_start(out=convp[:, :], in_=tmpA[98:128, 0:512]), reads=["tmpA"], key="stconv")
            if not sample:
                S.op("act", lambda e: e.activation(u_ext[:, :, 0:30], u_ext[:, :, 512:542], AF.Copy),
                     reads=[("u_ext", c) for c in range(4)], writes=["u_ext_h"])
            yield "end"


        def prompt_attention(idx, filler=None, filler2=None):
            for hs in range(4):
                for g in range(2):
                    h = 4 * g + hs
                    hc = hs // 2
                    po = 64 * (hs % 2)
                    ob = (6, 1)[(hs * 3 + g) % 2]
                    for qb in range(4):
                        gb = 16 + 4 * idx + qb
                        chunks = []
                        t0 = sum(NCH[:g])
                        for m in range(NCH[g]):
                            kb = gb - m
                            sl = ring_slot(g, kb)
                            kap = kT[g][po:po + 64, hc, sl * 128:(sl + 1) * 128]
                            vap = vap_of(g, sl, hs, kb < 16)
                            chunks.append((kap, [("kT", g, sl), ("V", g, sl)], vap, dist[:, t0 + m, :], 128,
                                           dist[:, t0 + m:t0 + NCH[g], :].rearrange("p a b -> p (a b)"),
                                           onesb[:, 64:128] if kb < 16 else onesb[:, 0:64]))
                        attn_block.first = True
                        attn_block.last = True
                        attn_block.obank = ob
                        attn_block(h, slice(qb * 128, (qb + 1) * 128), chunks, slice(qb * 128, (qb + 1) * 128))
                    if g == 0:
                        f = lambda hs=hs, ob=ob: S.op("dve", lambda e: e.tensor_copy(Oacc[:, hs, :], ps[:, ob, :]),
                                                      reads=[("ps", ob)], writes=[("Oacc", hs)])
                    else:
                        f = lambda hs=hs, ob=ob: S.op("dve", lambda e: e.tensor_tensor(Oacc[:, hs, :], ps[:, ob, :], Oacc[:, hs, :], ALU.add),
                                                      reads=[("ps", ob), ("Oacc", hs)], writes=[("Oacc", hs)])
                    attn_block.pending[-1]["post"].append(f)
                g2_batches(idx, hs)
            attn_flush(filler, filler2=filler2)
            g2_insert(4 + idx)

        def g2_batches(idx, hs):
            h = 8 + hs
            hc, po = hs // 2, 64 * (hs % 2)
            rot = (4 + idx) % 4
            sl = -SLOPES[h]
            for part in range(2):
                nk = 128 if part == 0 else 32
                ob = (6, 1)[(hs * 3 + 2 + part) % 2]
                dtab_ = d2[0:nk, rot:rot + 1, :] if part == 0 else d2[0:32, 4:5, :]

                def SE(sbank, pb, part=part, nk=nk, dtab_=dtab_):
                    skey = ("ps", sbank)

                    def fn_s(e):
                        r_ = None
                        for r in range(16):
                            kap = kTr2[po:po + 64, hc, r, :] if part == 0 else kTc2[po:po + 64, hc, r * 32:(r + 1) * 32]
                            r_ = e.matmul(ps[0:nk, sbank, r * 32:(r + 1) * 32], kap, qT[po:po + 64, 4 + hc, r * 32:(r + 1) * 32],
                                          start=True, stop=True)
                        return r_
                    S.op("pe", fn_s, reads=["qT", "kTr2" if part == 0 else "kTc2"], writes=[skey])
                    S.op("dve", lambda e: e.scalar_tensor_tensor(
                        ps[0:nk, sbank, :].rearrange("p (r u) -> p r u", u=32), dtab_.to_broadcast([nk, 16, 32]), sl,
                        ps[0:nk, sbank, :].rearrange("p (r u) -> p r u", u=32), ALU.mult, ALU.add),
                        reads=[skey, "d2"], writes=[skey])
                    S.op("act", lambda e: e.activation(P_sb[0:nk, pb, :], ps[0:nk, sbank, :], AF.Exp),
                         reads=[skey], writes=[("P_sb", pb)])

                def O(pb, part=part, nk=nk, ob=ob):
                    def fn_o(e):
                        r_ = None
                        for r in range(16):
                            vsrc = Vr2r if part == 0 else Vc2
                            e.matmul(ps[0:64, ob, r * 32:(r + 1) * 32], vsrc[0:nk, r, hs * 64:hs * 64 + 64],
                                     P_sb[0:nk, pb, r * 32:(r + 1) * 32], start=True, stop=True)
                            r_ = e.matmul(ps[64:128, ob, r * 32:(r + 1) * 32], vsrc[0:nk, r, 256:320],
                                          P_sb[0:nk, pb, r * 32:(r + 1) * 32], start=True, stop=True)
                        return r_
                    S.op("pe", fn_o, reads=[("P_sb", pb), "Vr2r" if part == 0 else "Vc2"], writes=[("ps", ob)])
                post = lambda ob=ob: S.op("dve", lambda e: e.tensor_tensor(
                    Oacc[:, hs, :].rearrange("p (u r) -> p r u", r=16), ps[:, ob, :].rearrange("p (r u) -> p r u", u=32),
                    Oacc[:, hs, :].rearrange("p (u r) -> p r u", r=16), ALU.add),
                    reads=[("ps", ob), ("Oacc", hs)], writes=[("Oacc", hs)])
                attn_block.pending.append(dict(custom=dict(SE=SE, O=O), pre=attn_block.pre, post=[post]))
                attn_block.pre = []

        def sample_attention():
            for hs in range(4):
                pass
            tiles = []
            for s in range(NSEQ_S):
                tiles.append((s, 0, 0, 0))
                for r in range(4):
                    tiles.append((s, 1, r, 1 + r))
                for r in range(4):
                    tiles.append((s, 2, r, 5 + r))
            for g in range(3):
                pass
            S.op("dve", lambda e: e.memset(Oacc[:, :, 0:TS], 0.0), writes=[("Oacc", hs) for hs in range(4)])
            for hs in range(4):
                hc, po = hs // 2, 64 * (hs % 2)
                for g in range(3):
                    h = 4 * g + hs
                    if g == 2:
                        kap = kTc2[po:po + 64, hc, 0:TS]
                        vap = Vc2[0:TS, 0, hs * 64:hs * 64 + 64]
                    else:
                        kap = kT[g][po:po + 64, hc, 0:TS]
                        vap = Vr[g][0:TS, hs * 64:hs * 64 + 64]
                    attn_block.first = True
                    attn_block.last = True
                    attn_block.obank = (6, 1)[(hs * 3 + g) % 2]
                    attn_block(h, slice(0, TS), [(kap, [("kTs", g), ("Vs", g), ("kT", g, 0), ("V", g, 0), "kTc2", "Vc2"], vap, sdist[0:TS, 9 + g, :], TS, None, onesb[0:TS, 0:64])],
                               slice(0, 16))
                    attn_block.pending[-1]["post"].append(
                        lambda hs=hs, ob=attn_block.obank: S.op("dve", lambda e: e.tensor_tensor(
                            Oacc[:, hs, 0:TS], ps[:, ob, 0:16], Oacc[:, hs, 0:TS], ALU.add),
                            reads=[("ps", ob), ("Oacc", hs)], writes=[("Oacc", hs)]))
            ctile = actT[:, 0:4, :].rearrange("p a b -> p (a b)").bitcast(F32)
            for ti, (s, g, r, tab) in enumerate(tiles):
                d = GROUPS[g][1]
                cb = ti % 2
                cf = ctile[:, cb * 512:(cb + 1) * 512]
                ckey, vkey, kkey = ("ctile", cb), ("cv", cb), ("ck", cb)
                src = cg[g][s].rearrange("(u r) f -> u r f", r=d)[:, r, :]
                vb = actT[:, 4 + cb, :]
                kb_ = actT[:, 6 + cb, :]

                def prep(cf=cf, src=src, vb=vb, kb_=kb_, ckey=ckey, vkey=vkey, kkey=kkey, cb=cb):
                    S.dma("sp", lambda e: e.dma_start(out=cf, in_=src), writes=[ckey], key=("ldc", cb))
                    S.op("act", lambda e: e.activation(vb[:, 0:256], cf[:, 256:512], AF.Copy), reads=[ckey], writes=[vkey])
                    for hc in range(2):
                        S.op("pe", lambda e, hc=hc: e.transpose(ps[:, 7, hc * 128:(hc + 1) * 128], cf[:, hc * 128:(hc + 1) * 128], ident[:, :]),
                             reads=[ckey, "ident"], writes=[("ps", 7)])
                    S.op("dve", lambda e: e.tensor_copy(kb_[:, 0:256], ps[:, 7, 0:256]), reads=[("ps", 7)], writes=[kkey])
                attn_block.pre.append(prep)
                for hs in range(4):
                    hc, po = hs // 2, 64 * (hs % 2)
                    h = 4 * g + hs
                    kap = kb_[po:po + 64, hc * 128:(hc + 1) * 128]
                    vap = actT[:, 4 + cb, hs * 64:hs * 64 + 64]
                    attn_block.first = True
                    attn_block.last = True
                    attn_block.obank = (6, 1)[(ti * 4 + hs) % 2]
                    attn_block(h, slice(4 * s, 4 * s + 4), [(kap, [kkey, vkey], vap, sdist[:, tab, 4 * s:4 * s + 4], 128, None, onesb[:, 0:64])],
                               slice(0, 4))
                    attn_block.pending[-1]["post"].append(
                        lambda hs=hs, s=s, ob=attn_block.obank: S.op("dve", lambda e: e.tensor_tensor(
                            Oacc[:, hs, 4 * s:4 * s + 4], ps[:, ob, 0:4], Oacc[:, hs, 4 * s:4 * s + 4], ALU.add),
                            reads=[("ps", ob), ("Oacc", hs)], writes=[("Oacc", hs)]))
                if r == 0:
                    L = GROUPS[g][0]
                    S.dma("sp", lambda e, s=s, g=g, L=L: e.dma_start(out=kvs[g][s, 0:L - 4, :], in_=cg[g][s, 4:L, :]), key=("cpkv", g))
            attn_flush(None)

        def sample_prep():
            ks = ["actT", "cones"] + [(n, cb) for n in ("ctile", "cv", "ck") for cb in range(2)]
            S.op("pool", lambda e: e.memset(actT[:, 4:6, 256:320], 1.0), reads=[], writes=ks)

        for ht in PLAN["halo"]:
            run_tile("halo", ht)
        for i in PLAN["main"]:
            run_tile("main", i)
        if PLAN["sample"]:
            sample_prep()
            run_tile("sample", 0)
        nops = S.finalize()
        PLAN["nops"] = nops
        PLAN["sbuf_left"] = nc.sbuf_bytes_remaining
    return nc


_CACHE = {}


def kernel(x_prompt, x_sample, state_conv, cache_kv_g0, cache_kv_g1, cache_kv_g2, p_prompt, p_sample,
           g_mix, w_in, w_dw, b_dw, ln_g, ln_b, w_conv_out, w_att_out, w_out, g_ffn, w_ffn_gate,
           w_ffn_up, w_ffn_down, g_ple, w_ple_gate, w_ple, g_final):
    f = lambda a: np.ascontiguousarray(np.asarray(a, dtype=np.float32))
    x_prompt, x_sample, state_conv, p_prompt, p_sample = map(f, (x_prompt, x_sample, state_conv, p_prompt, p_sample))
    caches = [f(cache_kv_g0), f(cache_kv_g1), f(cache_kv_g2)]
    if "nc" not in _CACHE:
        _CACHE["nc"] = build_program()
    nc = _CACHE["nc"]
    shared = dict(
        dtab=dist_tables(), d2tab=g2_tables(), sdtab=sample_dist_tables(),
        g3=np.ascontiguousarray(np.stack([f(g_mix)[0], f(g_ffn)[0], f(g_ple)[0]])),
        gfin=f(g_final).reshape(1, D),
        cvp=np.ascontiguousarray(np.concatenate([f(w_dw)[0], f(b_dw), f(ln_g), f(ln_b)], axis=0)),
        w_in=f(w_in)[0], w_co=f(w_conv_out)[0], w_ao=f(w_att_out)[0], w_o=f(w_out)[0],
        w_fg=f(w_ffn_gate)[0], w_fu=f(w_ffn_up)[0], w_fd=f(w_ffn_down)[0], w_pg=f(w_ple_gate)[0], w_pl=f(w_ple)[0],
    )
    in_maps = []
    for c in range(NCORES):
        b, half = c // 2, c % 2
        xhc = np.zeros((HALO + MAIN, D), np.float32)
        if half == 1:
            xhc[:] = x_prompt[b, MAIN - HALO:2 * MAIN]
        else:
            xhc[HALO:] = x_prompt[b, 0:MAIN]
        m = dict(shared)
        m["xh"] = xhc
        m["pp"] = np.ascontiguousarray(p_prompt[0, b, half * MAIN:(half + 1) * MAIN])
        m["flag"] = np.full((128, 1), float(half), np.float32)
        m["xs"] = np.ascontiguousarray(x_sample[4 * c:4 * c + 4].reshape(TS, D))
        m["psm"] = np.ascontiguousarray(p_sample[0, 4 * c:4 * c + 4].reshape(TS, DPLE))
        m["stc"] = np.ascontiguousarray(state_conv[0, 4 * c:4 * c + 4])
        for g in range(3):
            m["cg%d" % g] = np.ascontiguousarray(caches[g][0, 4 * c:4 * c + 4].reshape(NSEQ_S, GROUPS[g][0], 512))
        in_maps.append(m)
    res = run_bass_kernel_spmd(nc, in_maps, core_ids=list(range(NCORES)))
    R = res.results
    y_prompt = np.empty((4, 8192, D), np.float32)
    y_sample = np.empty((32, 4, D), np.float32)
    conv_p = np.empty((1, 4, 30, DC), np.float32)
    conv_s = np.empty((1, 32, 30, DC), np.float32)
    kv_p = [np.empty((1, 4, GROUPS[g][0], 2, 4, 64), np.float32) for g in range(3)]
    kv_s = [np.empty((1, 32, GROUPS[g][0], 2, 4, 64), np.float32) for g in range(3)]
    for c in range(NCORES):
        b, half = c // 2, c % 2
        y_prompt[b, half * MAIN:(half + 1) * MAIN] = R[c]["y"]
        y_sample[4 * c:4 * c + 4] = R[c]["ys"].reshape(4, 4, D)
        conv_s[0, 4 * c:4 * c + 4] = R[c]["convs"]
        for g in range(3):
            kv_s[g][0, 4 * c:4 * c + 4] = R[c]["kvs%d" % g].reshape(4, GROUPS[g][0], 2, 4, 64)
        if half == 1:
            conv_p[0, b] = R[c]["convp"]
            for g in range(3):
                kv_p[g][0, b] = R[c]["kvp%d" % g].reshape(GROUPS[g][0], 2, 4, 64)
    return (y_prompt, y_sample, conv_p, conv_s, kv_p[0], kv_s[0], kv_p[1], kv_s[1], kv_p[2], kv_s[2])
```

```python
import numpy as np
from contextlib import ExitStack
import concourse.bass as bass
import concourse.mybir as mybir
from concourse.bass_utils import run_bass_kernel_spmd

F32 = mybir.dt.float32
BF16 = mybir.dt.bfloat16
ALU = mybir.AluOpType
AF = mybir.ActivationFunctionType

D = 1024
DC = 512
NIN = 5376
DFF = 2816
DPLE = 256
EPS = 1e-6
GROUPS = ((128, 1), (512, 4), (2048, 16))
NCH = (2, 5, 17)
RING = (5, 8, 20)
NCORES = 8
NSEQ_S = 4
TS = 16
HALO = 2048
MAIN = 4096
SLOPES = [2.0 ** (-8.0 * (h + 1) / 12.0) for h in range(12)]


class Sched:
    def __init__(self, nc, stack):
        self.nc = nc
        self.stack = stack
        self.ops = []
        self.eng_obj = {"pe": nc.tensor, "act": nc.scalar, "dve": nc.vector,
                        "pool": nc.gpsimd, "sp": nc.sync}
        self.sems = {}
        self.nsem = 0

    def sem(self, stream):
        if stream not in self.sems:
            self.nsem += 1
            self.sems[stream] = self.stack.enter_context(self.nc.semaphore("sm%d" % self.nsem))
        return self.sems[stream]

    def op(self, eng, fn, reads=(), writes=(), dma_key=None):
        self.ops.append(dict(eng=eng, fn=fn, reads=tuple(reads), writes=tuple(writes),
                             dma_key=dma_key, signal=False))

    def dma(self, eng, fn, reads=(), writes=(), key=None):
        self.op(eng, fn, reads, writes, dma_key=key)

    def finalize(self, final_eng="sp"):
        ops = self.ops
        last_dma = {}
        for i, o in enumerate(ops):
            if o["dma_key"] is not None:
                last_dma[o["dma_key"]] = i
        ops.append(dict(eng=final_eng, fn=None, reads=(), writes=(), dma_key=None,
                        signal=False, extra_deps=set(last_dma.values())))
        seqctr, last_write, readers, last_on_key = {}, {}, {}, {}
        known = {e: {} for e in self.eng_obj}
        for i, o in enumerate(ops):
            E = o["eng"]
            stream = E if o["dma_key"] is None else ("dma", o["dma_key"])
            o["stream"] = stream
            deps = set(o.get("extra_deps", ()))
            for k in o["reads"]:
                if k in last_write:
                    deps.add(last_write[k])
                if isinstance(k, tuple) and k[0] == "ps":
                    for rs, ri in readers.get(k, {}).items():
                        if rs != stream:
                            deps.add(ri)
            for k in o["writes"]:
                if k in last_write:
                    deps.add(last_write[k])
                deps.update(readers.get(k, {}).values())
            if o["dma_key"] is not None and o["dma_key"] in last_on_key:
                deps.add(last_on_key[o["dma_key"]])
            best = {}
            for d in deps:
                Dd = ops[d]
                s, q = Dd["stream"], Dd["seq"]
                if s not in best or best[s][0] < q:
                    best[s] = (q, d)
            waits = []
            for s, (q, d) in best.items():
                if E == "pe" and s == "pe":
                    continue
                if known[E].get(s, 0) >= q:
                    continue
                waits.append(d)
                ops[d]["signal"] = True
                for ks, kq in ops[d]["clock"].items():
                    if known[E].get(ks, 0) < kq:
                        known[E][ks] = kq
            o["waits"] = waits
            seqctr[stream] = seqctr.get(stream, 0) + 1
            o["seq"] = seqctr[stream]
            clk = dict(known[E])
            clk[stream] = o["seq"]
            o["clock"] = clk
            if E == "pe" and o["dma_key"] is None:
                known["pe"]["pe"] = o["seq"]
            for k in o["reads"]:
                readers.setdefault(k, {})[stream] = i
            for k in o["writes"]:
                last_write[k] = i
                readers[k] = {}
            if o["dma_key"] is not None:
                last_on_key[o["dma_key"]] = i
        sigctr = {}
        for o in ops:
            e = self.eng_obj[o["eng"]]
            for d in o["waits"]:
                Dd = ops[d]
                e.wait_ge(self.sem(Dd["stream"]), Dd["sigval"])
            if o["fn"] is None:
                continue
            r = o["fn"](e)
            stt = o["stream"]
            if o["dma_key"] is not None:
                insts = r if isinstance(r, (list, tuple)) else [r]
                s = self.sem(stt)
                for ins in insts:
                    ins.then_inc(s, 16)
                sigctr[stt] = sigctr.get(stt, 0) + 16 * len(insts)
                o["sigval"] = sigctr[stt]
            elif o["signal"]:
                ins = r[-1] if isinstance(r, (list, tuple)) else r
                ins.then_inc(self.sem(stt), 1)
                sigctr[stt] = sigctr.get(stt, 0) + 1
                o["sigval"] = sigctr[stt]
        return len(ops)


def g2_tables():
    out = np.full((128, 5, 32), 1e6, np.float32)
    u = np.arange(32)[:, None]
    q = np.arange(32)[None, :]
    for rot in range(4):
        for sl in range(4):
            a = ((rot - sl - 1) % 4) + 1
            du = 32 * a + q - u
            out[32 * sl:32 * sl + 32, rot, :] = np.where(du <= 128, 16.0 * du, 1e6)
    du = q - u
    out[0:32, 4, :] = np.where(du >= 0, 16.0 * du, 1e6)
    return out


def dist_tables():
    out = np.empty((128, 7, 128), np.float32)
    ki = np.arange(128)[:, None]
    qi = np.arange(128)[None, :]
    t = 0
    for g, (w, d) in enumerate(GROUPS[:2]):
        for m in range(NCH[g]):
            dist = 128 * m + qi - ki
            ok = (dist >= 0) & (dist <= w) & (dist % d == 0)
            out[:, t, :] = np.where(ok, dist, 1e6)
            t += 1
    return out


def sample_dist_tables():
    out = np.full((128, 12, 16), 1e6, np.float32)
    rho = np.arange(128)
    for s in range(NSEQ_S):
        for i in range(4):
            col = 4 * s + i
            dist = 128 + i - rho
            out[:, 0, col] = np.where(rho >= i, dist, 1e6)
            out[:, 1 + i, col] = 4 * (128 - rho)
            out[:, 5 + i, col] = 16 * (128 - rho)
    for g, (w, d) in enumerate(GROUPS):
        for kk in range(16):
            for qq in range(16):
                if kk // 4 == qq // 4:
                    dd = (qq % 4) - (kk % 4)
                    if dd >= 0 and dd % d == 0:
                        out[kk, 9 + g, qq] = dd
    return out


PLAN = dict(halo=range(4), main=range(8), sample=True)


def build_program():
    nc = bass.Bass("TRN2", target_bir_lowering=False)
    dt_in = lambda n, sh: nc.dram_tensor(n, sh, F32, kind="ExternalInput").ap()
    dt_out = lambda n, sh: nc.dram_tensor(n, sh, F32, kind="ExternalOutput").ap()
    xh = dt_in("xh", [HALO + MAIN, D])
    pp = dt_in("pp", [MAIN, DPLE])
    flag = dt_in("flag", [128, 1])
    xs = dt_in("xs", [TS, D])
    psm = dt_in("psm", [TS, DPLE])
    stc = dt_in("stc", [NSEQ_S, 30, DC])
    cg = [dt_in("cg%d" % g, [NSEQ_S, GROUPS[g][0], 512]) for g in range(3)]
    dtab = dt_in("dtab", [128, 7, 128])
    d2tab = dt_in("d2tab", [128, 5, 32])
    sdtab = dt_in("sdtab", [128, 12, 16])
    g3 = dt_in("g3", [3, D])
    gfin = dt_in("gfin", [1, D])
    cvp = dt_in("cvp", [34, DC])
    w_in = dt_in("w_in", [D, NIN])
    w_co = dt_in("w_co", [DC, D])
    w_ao = dt_in("w_ao", [256, D])
    w_o = dt_in("w_o", [D, D])
    w_fg = dt_in("w_fg", [D, DFF])
    w_fu = dt_in("w_fu", [D, DFF])
    w_fd = dt_in("w_fd", [DFF, D])
    w_pg = dt_in("w_pg", [D, D])
    w_pl = dt_in("w_pl", [DPLE, D])

    y = dt_out("y", [MAIN, D])
    ys = dt_out("ys", [TS, D])
    convp = dt_out("convp", [30, DC])
    convs = dt_out("convs", [NSEQ_S, 30, DC])
    kvp = [dt_out("kvp%d" % g, [GROUPS[g][0], 512]) for g in range(3)]
    kvs = [dt_out("kvs%d" % g, [NSEQ_S, GROUPS[g][0], 512]) for g in range(3)]

    WSPEC = {
        "in": (w_in, 8, 128, NIN), "co": (w_co, 4, 128, D), "ao": (w_ao, 4, 64, D),
        "o": (w_o, 8, 128, D), "fg": (w_fg, 8, 128, DFF), "fu": (w_fu, 8, 128, DFF),
        "fd0": (w_fd, 11, 128, D), "fd1": (w_fd, 11, 128, D),
        "pg": (w_pg, 8, 128, D), "pl": (w_pl, 2, 128, D),
    }
    scratch = {}
    for nm, (ap, nk, pd, ncol) in WSPEC.items():
        scratch[nm] = nc.dram_tensor("wsc_" + nm, [ncol // 128, 128, nk * 128], BF16, kind="Internal").ap()

    with ExitStack() as st:
        S = Sched(nc, st)
        sb = lambda n, sh, dt: st.enter_context(nc.sbuf_tensor(n, sh, dt))
        ident = sb("ident", [128, 128], F32)
        identb = sb("identb", [128, 128], BF16)
        gcols = sb("gcols", [128, 8, 3], F32)
        cvcols = sb("cvcols", [128, 4, 34], F32)
        gfb = sb("gfb", [128, D], F32)
        flag_sb = sb("flag_sb", [128, 1], F32)
        mhalf = sb("mhalf", [128, 8], F32)
        onesb = sb("onesb", [128, 128], BF16)
        dist = sb("dist", [128, 7, 128], F32)
        d2 = sb("d2", [128, 5, 32], F32)
        sdist = sb("sdist", [128, 12, 16], F32)
        x_tm = sb("x_tm", [128, 4, D], F32)
        h_tm = sb("h_tm", [128, 2, D], F32)
        stat = sb("stat", [128, 16], F32)
        lnst = sb("lnst", [128, 16], F32)
        hT = sb("hT", [128, 8, 512], BF16)
        qT = sb("qT", [128, 6, 512], BF16)
        kT = [sb("kT%d" % g, [128, 2, RING[g] * 128], BF16) for g in range(2)]
        Vr = [sb("Vr%d" % g, [128, RING[g] * 256], BF16) for g in range(2)]
        kTr2 = sb("kTr2", [128, 2, 16, 128], BF16)
        kTc2 = sb("kTc2", [128, 2, 512], BF16)
        Vr2r = sb("Vr2r", [128, 16, 320], BF16)
        Vc2 = sb("Vc2", [128, 16, 320], BF16)
        gates = sb("gates", [128, 16, 512], BF16)
        u_ext = sb("u_ext", [128, 4, 30 + 512], F32)
        c_sb = sb("c_sb", [128, 2, 512], F32)
        chat = sb("chat", [128, 4, 512], BF16)
        sT = sb("sT", [128, 4, 512], BF16)
        P_sb = sb("P_sb", [128, 4, 512], BF16)
        Oacc = sb("Oacc", [128, 4, 512], F32)
        rl = sb("rl", [128, 512], F32)
        OnT = sb("OnT", [64, 4, 512], BF16)
        tmpA = sb("tmpA", [128, 512], F32)
        tmpB = sb("tmpB", [128, 512], F32)
        actT = sb("actT", [128, 22, 512], BF16)
        p_tm = sb("p_tm", [128, 4, DPLE], F32)
        pT = sb("pT", [128, 2, 512], BF16)
        kv_tm = sb("kv_tm", [128, 4, 512], F32)
        NWB = 7
        WLA = 5
        assert NWB >= WLA + 2
        wbuf = [sb("wbuf%d" % i, [128, 11 * 128], BF16) for i in range(NWB)]
        ps = st.enter_context(nc.psum_tensor("ps", [128, 8, 512], F32))
        prm_rows = tmpA
        g3_rows = h_tm[:, 0, :]

        def psb(bank):
            return ps[:, bank, :].bitcast(BF16)

        wstate = dict(n=0, nstg=0, seen=set(), cur={})

        def wload(nm, ci):
            ap, nk, pd, ncol = WSPEC[nm]
            slot = wstate["n"] % NWB
            wstate["n"] += 1
            key = "wbuf%d" % slot
            wb = wbuf[slot]
            dstv = wb[0:pd, 0:nk * 128]
            sc = scratch[nm][ci]
            if (nm, ci) in wstate["seen"]:
                S.dma("sp", lambda e, dstv=dstv, sc=sc, pd=pd, nk=nk: e.dma_start(out=dstv, in_=sc[0:pd, 0:nk * 128]),
                      reads=[("wsc", nm, ci)], writes=[key], key=key)
            else:
                wstate["seen"].add((nm, ci))
                if nm == "ao":
                    src = ap.rearrange("(s d) n -> d s n", d=64)[:, :, ci * 128:(ci + 1) * 128]
                elif nm == "fd1":
                    src = ap[11 * 128:22 * 128, :].rearrange("(k p) n -> p k n", p=128)[:, :, ci * 128:(ci + 1) * 128]
                elif nm == "fd0":
                    src = ap[0:11 * 128, :].rearrange("(k p) n -> p k n", p=128)[:, :, ci * 128:(ci + 1) * 128]
                else:
                    src = ap.rearrange("(k p) n -> p k n", p=128)[:, :, ci * 128:(ci + 1) * 128]
                dst3 = dstv.rearrange("p (k n) -> p k n", n=128)
                S.dma("pool", lambda e, dst3=dst3, src=src: e.dma_start(out=dst3, in_=src), writes=[key], key=("wbufc", slot))
                S.dma("sp", lambda e, dstv=dstv, sc=sc, pd=pd, nk=nk: e.dma_start(out=sc[0:pd, 0:nk * 128], in_=dstv),
                      reads=[key], writes=[("wsc", nm, ci)], key=("wst", slot))
            wstate["cur"][(nm, ci)] = (slot, key)

        def wget(nm, ci):
            if (nm, ci) not in wstate["cur"]:
                wload(nm, ci)
            slot, key = wstate["cur"].pop((nm, ci))
            return wbuf[slot], key

        def wstream(seq):
            seq = list(seq)
            for i in range(min(WLA, len(seq))):
                if seq[i] not in wstate["cur"]:
                    wload(*seq[i])
            for i, (nm, ci) in enumerate(seq):
                if i + WLA < len(seq) and seq[i + WLA] not in wstate["cur"]:
                    wload(*seq[i + WLA])
                wb, key = wget(nm, ci)
                yield nm, ci, wb, key

        S.dma("sp", lambda e: e.dma_start(out=dist[:], in_=dtab[:]), writes=["dist"], key="c_dist")
        S.dma("sp", lambda e: e.dma_start(out=d2[:], in_=d2tab[:]), writes=["d2"], key="c_d2")
        S.dma("sp", lambda e: e.dma_start(out=sdist[:], in_=sdtab[:]), writes=["sdist"], key="c_sdist")
        S.dma("sp", lambda e: e.dma_start(out=prm_rows[0:34, :], in_=cvp[:]), writes=["tmpA"], key="c_prm")
        S.dma("sp", lambda e: e.dma_start(out=g3_rows[0:3, :], in_=g3[:]), writes=[("h_tm", 0)], key="c_g3")
        S.dma("sp", lambda e: e.dma_start(out=gfb[:], in_=gfin.to_broadcast([128, D])), writes=["gfb"], key="c_gf")
        S.dma("sp", lambda e: e.dma_start(out=flag_sb[:], in_=flag[:]), writes=["flag"], key="c_flag")
        S.op("pool", lambda e: e.memset(ident[:], 1.0), writes=["ident"])
        S.op("pool", lambda e: e.affine_select(ident[:], ident[:], [[1, 128]], ALU.is_equal, 0.0, base=0,
                                               channel_multiplier=-1), reads=["ident"], writes=["ident"])
        S.op("pool", lambda e: e.memset(mhalf[:], -0.5), writes=["mhalf"])
        S.op("dve", lambda e: e.tensor_copy(identb[:], ident[:]), reads=["ident"], writes=["identb"])
        S.op("pool", lambda e: e.memset(onesb[:, :], 1.0), writes=["onesb"])
        S.op("dve", lambda e: e.tensor_copy(onesb[:, 64:128], flag_sb[:, 0:1].to_broadcast([128, 64])),
             reads=["flag", "onesb"], writes=["onesb"])
        for c in range(4):
            S.op("pe", lambda e, c=c: e.transpose(ps[:, 0, c * 34:c * 34 + 34], prm_rows[0:34, c * 128:(c + 1) * 128], ident[0:34, 0:34]),
                 reads=["tmpA", "ident"], writes=[("ps", 0)])
        S.op("dve", lambda e: e.tensor_copy(cvcols[:].rearrange("p c j -> p (c j)"), ps[:, 0, 0:136]), reads=[("ps", 0)], writes=["cvcols"])
        for c in range(8):
            S.op("pe", lambda e, c=c: e.transpose(ps[:, 1, c * 3:c * 3 + 3], g3_rows[0:3, c * 128:(c + 1) * 128], ident[0:3, 0:3]),
                 reads=[("h_tm", 0), "ident"], writes=[("ps", 1)])
        S.op("dve", lambda e: e.tensor_copy(gcols[:].rearrange("p c j -> p (c j)"), ps[:, 1, 0:24]), reads=[("ps", 1)], writes=["gcols"])
        S.op("dve", lambda e: e.memset(u_ext[:, :, 0:30], 0.0), writes=["u_ext_h"])

        def norm_to_hT(nblk, bt, gi, dst, dstkey):
            TT = nblk * bt
            for b in range(nblk):
                hb = b % 2
                S.op("act", lambda e, b=b, hb=hb: e.activation(h_tm[0:bt, hb, :], x_tm[0:bt, b, :], AF.Square,
                                                               accum_out=stat[0:bt, b:b + 1]),
                     reads=[("x", b)], writes=[("h_tm", hb), ("stat", b)])
            S.op("dve", lambda e: e.tensor_scalar(stat[0:bt, 4:4 + nblk], stat[0:bt, 0:nblk], 1.0 / D, EPS, ALU.mult, ALU.add),
                 reads=[("stat", b) for b in range(nblk)], writes=["stat_b"])
            S.op("pool", lambda e: e.tensor_tensor(stat[0:bt, 8:8 + nblk], stat[0:bt, 4:4 + nblk], mhalf[0:bt, 0:nblk], ALU.pow),
                 reads=["stat_b", "mhalf"], writes=["rstd"])
            for b in range(nblk):
                hb = b % 2
                S.op("act", lambda e, b=b, hb=hb: e.activation(h_tm[0:bt, hb, :], x_tm[0:bt, b, :], AF.Copy,
                                                               scale=stat[0:bt, 8 + b:9 + b]),
                     reads=[("x", b), "rstd"], writes=[("h_tm", hb)])
                for half in range(2):
                    bank = 2 + half
                    for k4 in range(4):
                        kc = half * 4 + k4
                        S.op("pe", lambda e, hb=hb, kc=kc, k4=k4, bank=bank: e.transpose(
                            ps[:, bank, k4 * 128:k4 * 128 + bt], h_tm[0:bt, hb, kc * 128:(kc + 1) * 128], ident[0:bt, 0:bt]),
                            reads=[("h_tm", hb), "ident"], writes=[("ps", bank)])
                    S.op("dve", lambda e, b=b, half=half, bank=bank: e.tensor_tensor(
                        dst[:, half * 4:half * 4 + 4, b * 128:b * 128 + bt],
                        ps[:, bank, 0:512].rearrange("p (k n) -> p k n", n=128)[:, :, 0:bt],
                        gcols[:, half * 4:half * 4 + 4, gi:gi + 1].to_broadcast([128, 4, bt]), ALU.mult),
                        reads=[("ps", bank), "gcols"], writes=[dstkey])

        fm_bank = [0]

        def fm_matmul(wb, wkey, nk, pd, rhs_fn, rhs_keys, TT, bank=None):
            if bank is None:
                bank = fm_bank[0] % 2
                fm_bank[0] += 1
            key = ("ps", bank)

            def fn(e, bank=bank, wb=wb):
                r = None
                for k in range(nk):
                    r = e.matmul(ps[:, bank, 0:TT], wb[0:pd, k * 128:(k + 1) * 128], rhs_fn(k),
                                 start=(k == 0), stop=(k == nk - 1))
                return r
            S.op("pe", fn, reads=[wkey] + list(rhs_keys), writes=[key])
            return bank, key

        def tm_matmul(bank, pieces, lhs_fn, lhs_keys, nblk, bt, ncol=128, col0=0):
            key = ("ps", bank)
            tot = sum(p[2] for p in pieces)

            def fn(e):
                r = None
                for b in range(nblk):
                    i = 0
                    for (wb, wkey, nk, kofs, pd) in pieces:
                        for k in range(nk):
                            r = e.matmul(ps[0:bt, bank, b * 128 + col0:b * 128 + col0 + ncol], lhs_fn(kofs + k, b),
                                         wb[0:pd, k * 128:k * 128 + ncol], start=(i == 0), stop=(i == tot - 1))
                            i += 1
                return r
            S.op("pe", fn, reads=[p[1] for p in pieces] + list(lhs_keys), writes=[key])
            return key

        def resid_add(bank, oc, nblk, bt):
            S.op("dve", lambda e: e.tensor_tensor(
                x_tm[0:bt, 0:nblk, oc * 128:(oc + 1) * 128],
                ps[0:bt, bank, 0:nblk * 128].rearrange("p (b n) -> p b n", n=128),
                x_tm[0:bt, 0:nblk, oc * 128:(oc + 1) * 128], ALU.add),
                reads=[("ps", bank)] + [("x", b) for b in range(nblk)], writes=[("x", b) for b in range(nblk)])

        def attn_block(h, qcols, chunks, pso_cols):
            n = len(chunks)
            for b0 in range(0, n, 4):
                attn_block.pending.append(dict(
                    h=h, qcols=qcols, grp=chunks[b0:b0 + 4], pso=pso_cols, obank=attn_block.obank,
                    first=(b0 == 0 and attn_block.first), last=(b0 + 4 >= n and attn_block.last),
                    pre=attn_block.pre, post=[]))
                attn_block.pre = []
        attn_block.pending = []
        attn_block.pre = []
        attn_block.first = True
        attn_block.last = True
        attn_block.obank = 6
        SBANKS = (4, 5, 0)

        def attn_flush(filler=None, L=2, filler2=None):
            bl = attn_block.pending
            attn_block.pending = []
            n = len(bl)

            def stage_S(t):
                B = bl[t]
                for f in B["pre"]:
                    f()
                if "custom" in B:
                    B["custom"]["SE"](SBANKS[t % 3], t % 4)
                    return
                h = B["h"]
                po, qc = 64 * (h % 2), h // 2
                qcols = B["qcols"]
                nq = qcols.stop - qcols.start
                sbank = SBANKS[t % 3]
                pb = t % 4
                grp = B["grp"]
                skey = ("ps", sbank)

                def fn_s(e, grp=grp, sbank=sbank, po=po, qc=qc, qcols=qcols, nq=nq):
                    r = None
                    for j, c in enumerate(grp):
                        r = e.matmul(ps[0:c[4], sbank, j * 128:j * 128 + nq], c[0], qT[po:po + 64, qc, qcols],
                                     start=True, stop=True)
                    return r
                rk = set()
                for c in grp:
                    rk.update(c[1])
                B["rk"] = rk
                S.op("pe", fn_s, reads=["qT"] + list(rk), writes=[skey])
                same = all(c[4] == 128 for c in grp) and nq == 128 and grp[0][5] is not None
                sl = -SLOPES[h]
                if same:
                    W = len(grp) * 128
                    dall = grp[0][5][:, 0:W]
                    S.op("dve", lambda e, sbank=sbank, dall=dall, W=W, sl=sl: e.scalar_tensor_tensor(
                        ps[:, sbank, 0:W], dall, sl, ps[:, sbank, 0:W], ALU.mult, ALU.add),
                        reads=[skey, "dist"], writes=[skey])
                    S.op("act", lambda e, sbank=sbank, pb=pb, W=W: e.activation(P_sb[:, pb, 0:W], ps[:, sbank, 0:W], AF.Exp),
                         reads=[skey], writes=[("P_sb", pb)])
                else:
                    for j, c in enumerate(grp):
                        nk = c[4]
                        S.op("dve", lambda e, sbank=sbank, dap=c[3], nk=nk, j=j, sl=sl, nq=nq: e.scalar_tensor_tensor(
                            ps[0:nk, sbank, j * 128:j * 128 + nq], dap, sl, ps[0:nk, sbank, j * 128:j * 128 + nq],
                            ALU.mult, ALU.add), reads=[skey, "sdist"], writes=[skey])
                        S.op("act", lambda e, sbank=sbank, pb=pb, nk=nk, j=j, nq=nq: e.activation(
                            P_sb[0:nk, pb, j * 128:j * 128 + nq], ps[0:nk, sbank, j * 128:j * 128 + nq], AF.Exp),
                            reads=[skey], writes=[("P_sb", pb)])

            def stage_O(t):
                B = bl[t]
                if "custom" in B:
                    B["custom"]["O"](t % 4)
                    for f in B["post"]:
                        f()
                    return
                qcols = B["qcols"]
                nq = qcols.stop - qcols.start
                pb = t % 4
                grp, ob, pso = B["grp"], B["obank"], B["pso"]
                ng = len(grp)

                def fn_o(e, grp=grp, pb=pb, ob=ob, pso=pso, nq=nq, first=B["first"], last=B["last"], ng=ng):
                    r = None
                    for j, c in enumerate(grp):
                        st_ = (j == 0 and first)
                        sp_ = (j == ng - 1 and last)
                        e.matmul(ps[0:64, ob, pso], c[2], P_sb[0:c[4], pb, j * 128:j * 128 + nq], start=st_, stop=sp_)
                        r = e.matmul(ps[64:128, ob, pso], c[6], P_sb[0:c[4], pb, j * 128:j * 128 + nq], start=st_, stop=sp_)
                    return r
                S.op("pe", fn_o, reads=[("P_sb", pb), "onesb"] + list(B["rk"]), writes=[("ps", ob)])
                for f in B["post"]:
                    f()

            fdone = [False]
            for t in range(n + L):
                if t < n:
                    stage_S(t)
                for _r in range(2):
                    if filler is not None and not fdone[0]:
                        v = next(filler, None)
                        if v != "tap":
                            fdone[0] = True
                if filler2 is not None and t % 3 == 1:
                    next(filler2, None)
                if t - L >= 0:
                    stage_O(t - L)

        def g2_insert(tg):
            sl = tg % 4
            for hc in range(2):
                S.op("act", lambda e, hc=hc, sl=sl: e.activation(
                    kTr2[:, hc, :, sl * 32:(sl + 1) * 32], kTc2[:, hc, :].rearrange("p (r u) -> p r u", u=32), AF.Copy),
                    reads=["kTc2"], writes=["kTr2"])
            S.dma("sp", lambda e, sl=sl: e.dma_start(out=Vr2r[32 * sl:32 * sl + 32, :, :], in_=Vc2[0:32, :, :]),
                  reads=["Vc2"], writes=["Vr2r"], key="ins_v2")

        def ring_slot(g, gb):
            return gb % RING[g]

        def vap_of(g, slot, hs, halo):
            base = slot * 256 + hs * 64
            return Vr[g][:, base:base + 64]

        def run_tile(kind, idx):
            sample = kind == "sample"
            halo = kind == "halo"
            nblk, bt = (1, TS) if sample else (4, 128)
            TT = nblk * bt
            gb0 = (4 * idx) if halo else (16 + 4 * idx)
            last = (kind == "main" and idx == 7)
            for b in range(nblk):
                if sample:
                    src = xs[:, :]
                elif halo:
                    src = xh[(idx * 4 + b) * 128:(idx * 4 + b + 1) * 128, :]
                else:
                    src = xh[HALO + (idx * 4 + b) * 128:HALO + (idx * 4 + b + 1) * 128, :]
                S.dma("sp", lambda e, b=b, src=src: e.dma_start(out=x_tm[0:bt, b, :], in_=src),
                      writes=[("x", b)], key=("ldx", b))
            if PLAN.get("stage", 99) < 1:
                return
            norm_to_hT(nblk, bt, 0, hT, "hT")
            if PLAN.get("stage", 99) < 2:
                return
            if halo:
                grps = [2] + ([1] if idx == 3 else []) + ([0] if idx == 3 else [])
                need_ab = idx == 3
            else:
                grps = [0, 1, 2]
                need_ab = True
            seq = []
            if need_ab:
                seq += [("in", c) for c in (4, 5, 6, 7, 0, 1, 2, 3)]
            if not halo:
                seq += [("in", 8 + c) for c in range(6)]
            for g in grps:
                seq += [("in", 14 + 2 * g), ("in", 15 + 2 * g), ("in", 20 + 2 * g), ("in", 21 + 2 * g)]
            if sample:
                seq += [("in", 26 + c) for c in range(16)]
            rhs_h = lambda k: hT[:, k, 0:TT]
            for nm, ci, wb, wkey in wstream(seq):
                if PLAN.get("wonly"):
                    S.op("dve", lambda e, wb=wb: e.tensor_copy(tmpA[:, 0:8], wb[:, 0:8]), reads=[wkey], writes=["tmpA"])
                    continue
                if PLAN.get("skipk") and 14 <= ci < 20:
                    continue
                if PLAN.get("skipv") and 20 <= ci < 26:
                    continue
                if ci < 8 or (8 <= ci < 20) or ci >= 26:
                    bank, pkey = fm_matmul(wb, wkey, 8, 128, rhs_h, ["hT"], TT)
                if 4 <= ci < 8:
                    c = ci - 4
                    S.op("act", lambda e, bank=bank, c=c: e.activation(gates[:, c, 0:TT], ps[:, bank, 0:TT], AF.Sigmoid),
                         reads=[pkey], writes=[("gates", c)])
                elif ci < 4:
                    c = ci
                    if sample:
                        dstu = u_ext[:, c, 0:4 * 34].rearrange("p (s t) -> p s t", t=34)[:, :, 30:34]
                        srcp = ps[:, bank, 0:TT].rearrange("p (s t) -> p s t", t=4)
                        srcg = gates[:, c, 0:TT].rearrange("p (s t) -> p s t", t=4)
                    else:
                        dstu = u_ext[:, c, 30:30 + TT]
                        srcp = ps[:, bank, 0:TT]
                        srcg = gates[:, c, 0:TT]
                    if halo:
                        S.op("dve", lambda e, bank=bank, c=c: e.tensor_tensor(tmpA[:, 0:30], ps[:, bank, TT - 30:TT], gates[:, c, TT - 30:TT], ALU.mult),
                             reads=[pkey, ("gates", c)], writes=["tmpA"])
                        S.op("dve", lambda e, c=c: e.tensor_scalar(u_ext[:, c, 0:30], tmpA[:, 0:30], flag_sb[:, 0:1], None, ALU.mult),
                             reads=["tmpA", "flag"], writes=["u_ext_h"])
                    else:
                        S.op("dve", lambda e, dstu=dstu, srcp=srcp, srcg=srcg: e.tensor_tensor(dstu, srcp, srcg, ALU.mult),
                             reads=[pkey, ("gates", c)], writes=[("u_ext", c)])
                elif 8 <= ci < 14:
                    c = ci - 8
                    if kind == "main" and c >= 4:
                        S.op("act", lambda e, bank=bank, c=c: e.activation(
                            qT[:, c, :].rearrange("p (r u) -> p u r", u=32),
                            ps[:, bank, 0:512].rearrange("p (u r) -> p u r", r=16), AF.Copy, scale=0.125),
                            reads=[pkey], writes=["qT"])
                    else:
                        S.op("act", lambda e, bank=bank, c=c: e.activation(qT[:, c, 0:TT], ps[:, bank, 0:TT], AF.Copy, scale=0.125),
                             reads=[pkey], writes=["qT"])
                elif 14 <= ci < 20:
                    g, hc = (ci - 14) // 2, (ci - 14) % 2
                    if g == 2 and sample:
                        S.op("dve", lambda e, bank=bank, hc=hc: e.tensor_copy(kTc2[:, hc, 0:TT], ps[:, bank, 0:TT]),
                             reads=[pkey], writes=[("kTs", 2), "kTc2"])
                    elif g == 2:
                        S.op("dve", lambda e, bank=bank, hc=hc: e.tensor_copy(
                            kTc2[:, hc, :].rearrange("p (r u) -> p u r", u=32),
                            ps[:, bank, 0:512].rearrange("p (u r) -> p u r", r=16)),
                            reads=[pkey], writes=["kTc2"])
                    elif sample:
                        S.op("dve", lambda e, bank=bank, g=g, hc=hc: e.tensor_copy(kT[g][:, hc, 0:TT], ps[:, bank, 0:TT]),
                             reads=[pkey], writes=[("kTs", g), ("kT", g, 0)])
                    else:
                        for b in range(nblk):
                            sl = ring_slot(g, gb0 + b)
                            S.op("dve", lambda e, bank=bank, g=g, hc=hc, b=b, sl=sl: e.tensor_copy(
                                kT[g][:, hc, sl * 128:(sl + 1) * 128], ps[:, bank, b * 128:(b + 1) * 128]),
                                reads=[pkey], writes=[("kT", g, sl)])
                    need_out = sample or (kind == "main" and ((g == 2 and idx >= 4) or (g == 1 and idx == 7) or (g == 0 and idx == 7)))
                    if need_out:
                        tm_matmul(7, [(wb, wkey, 8, 0, 128)], lambda k, b: hT[:, k, b * 128:b * 128 + bt], ["hT"], nblk, bt)
                        S.op("act", lambda e, hc=hc: e.activation(
                            kv_tm[0:bt, hc, 0:nblk * 128],
                            ps[0:bt, 7, 0:nblk * 128], AF.Copy), reads=[("ps", 7)], writes=[("kv_tm", 0, hc)])
                        kv_out(kind, idx, g, 0, hc, nblk, bt)
                elif 20 <= ci < 26:
                    g, hc = (ci - 20) // 2, (ci - 20) % 2
                    need_out = sample or (kind == "main" and ((g == 2 and idx >= 4) or (g == 1 and idx == 7) or (g == 0 and idx == 7)))
                    if g == 2 and not sample:
                        for rg in range(4):
                            vb_ = 6 + (rg % 2)

                            def fn_v(e, rg=rg, vb_=vb_, wb=wb):
                                r_ = None
                                for r4 in range(4):
                                    r = 4 * rg + r4
                                    for k in range(8):
                                        r_ = e.matmul(ps[0:32, vb_, r4 * 128:(r4 + 1) * 128], hT[:, k, r:512:16],
                                                      wb[:, k * 128:(k + 1) * 128], start=(k == 0), stop=(k == 7))
                                return r_
                            S.op("pe", fn_v, reads=[wkey, "hT"], writes=[("ps", vb_)])
                            S.op("dve", lambda e, rg=rg, vb_=vb_, hc=hc: e.tensor_copy(
                                Vc2[0:32, 4 * rg:4 * rg + 4, hc * 128:(hc + 1) * 128],
                                ps[0:32, vb_, 0:512].rearrange("p (r n) -> p r n", n=128)),
                                reads=[("ps", vb_)], writes=["Vc2"])
                        if not need_out:
                            continue
                    tm_matmul(7, [(wb, wkey, 8, 0, 128)], lambda k, b: hT[:, k, b * 128:b * 128 + bt], ["hT"], nblk, bt)
                    if g == 2 and sample:
                        S.op("dve", lambda e, hc=hc: e.tensor_copy(Vc2[0:bt, 0, hc * 128:(hc + 1) * 128], ps[0:bt, 7, 0:128]),
                             reads=[("ps", 7)], writes=[("Vs", 2), "Vc2"])
                    elif g == 2:
                        pass
                    elif sample:
                        S.op("dve", lambda e, g=g, hc=hc: e.tensor_copy(Vr[g][0:bt, hc * 128:(hc + 1) * 128], ps[0:bt, 7, 0:128]),
                             reads=[("ps", 7)], writes=[("Vs", g), ("V", g, 0)])
                    else:
                        for b in range(nblk):
                            sl = ring_slot(g, gb0 + b)
                            S.op("dve", lambda e, g=g, hc=hc, b=b, sl=sl: e.tensor_copy(
                                Vr[g][:, sl * 256 + hc * 128:sl * 256 + (hc + 1) * 128], ps[:, 7, b * 128:(b + 1) * 128]),
                                reads=[("ps", 7)], writes=[("V", g, sl)])
                    need_out = sample or (kind == "main" and ((g == 2 and idx >= 4) or (g == 1 and idx == 7) or (g == 0 and idx == 7)))
                    if need_out:
                        S.op("act", lambda e, hc=hc: e.activation(kv_tm[0:bt, 2 + hc, 0:nblk * 128], ps[0:bt, 7, 0:nblk * 128], AF.Copy),
                             reads=[("ps", 7)], writes=[("kv_tm", 1, hc)])
                        kv_out(kind, idx, g, 1, hc, nblk, bt)
                elif ci >= 26:
                    c = ci - 26
                    S.op("act", lambda e, bank=bank, c=c: e.activation(gates[:, c, 0:TT], ps[:, bank, 0:TT], AF.Sigmoid),
                         reads=[pkey], writes=[("gates", c)])
            if not sample:
                oc0 = 64 if halo else 0
                S.op("dve", lambda e, oc0=oc0: e.tensor_copy(Vc2[0:32, :, 256:320], onesb[0:32, oc0:oc0 + 64].unsqueeze(1).to_broadcast([32, 16, 64])),
                     reads=["onesb"], writes=["Vc2"])
            if halo:
                g2_insert(idx)
                return
            def gates_gen():
                for nm, ci, wb, wkey in wstream([("in", 26 + c) for c in range(16)]):
                    c = ci - 26
                    bank, pkey = fm_matmul(wb, wkey, 8, 128, rhs_h, ["hT"], TT, bank=7)
                    S.op("act", lambda e, bank=bank, c=c: e.activation(gates[:, c, 0:TT], ps[:, bank, 0:TT], AF.Sigmoid),
                         reads=[pkey], writes=[("gates", c)])
                    yield "g"
            gen = conv_branch(kind, idx, nblk, bt, TT, last)
            if sample:
                for _ in gen:
                    pass
                sample_attention()
            else:
                gg = gates_gen()
                prompt_attention(idx, filler=gen, filler2=gg)
                for _ in gg:
                    pass
            for hs in range(4):
                S.op("dve", lambda e, hs=hs: e.reciprocal(rl[0:64, 0:TT], Oacc[64:128, hs, 0:TT]),
                     reads=[("Oacc", hs)], writes=["rl"])
                S.op("dve", lambda e, hs=hs: e.tensor_tensor(OnT[0:64, hs, 0:TT], Oacc[0:64, hs, 0:TT], rl[0:64, 0:TT], ALU.mult),
                     reads=[("Oacc", hs), "rl"], writes=["OnT"])
            if not sample:
                for _ in gen:
                    pass
            seq = []
            for oc in range(8):
                seq += [("ao", oc), ("co", oc)]
            it = wstream(seq)
            for oc in range(8):
                _, _, wbA, keyA = next(it)
                bankA, pkA = fm_matmul(wbA, keyA, 4, 64, lambda k: OnT[0:64, k, 0:TT], ["OnT"], TT)
                S.op("dve", lambda e, bankA=bankA, oc=oc: e.tensor_tensor(tmpA[:, 0:TT], ps[:, bankA, 0:TT], gates[:, 8 + oc, 0:TT], ALU.mult),
                     reads=[pkA, ("gates", 8 + oc)], writes=["tmpA"])
                _, _, wbC, keyC = next(it)
                bankC, pkC = fm_matmul(wbC, keyC, 4, 128, lambda k: sT[:, k, 0:TT], ["sT"], TT)
                S.op("dve", lambda e, bankC=bankC, oc=oc: e.tensor_tensor(tmpB[:, 0:TT], ps[:, bankC, 0:TT], gates[:, oc, 0:TT], ALU.mult),
                     reads=[pkC, ("gates", oc)], writes=["tmpB"])
                S.op("dve", lambda e, oc=oc: e.tensor_tensor(hT[:, oc, 0:TT], tmpA[:, 0:TT], tmpB[:, 0:TT], ALU.add),
                     reads=["tmpA", "tmpB"], writes=["hT"])
            for i, (nm, ci, wb, wkey) in enumerate(wstream([("o", oc) for oc in range(8)])):
                bank = 2 + i % 2
                tm_matmul(bank, [(wb, wkey, 8, 0, 128)], lambda k, b: hT[:, k, b * 128:b * 128 + bt], ["hT"], nblk, bt)
                resid_add(bank, ci, nblk, bt)
            if sample:
                ks = ["cones"] + [(n, cb) for n in ("ctile", "cv", "ck") for cb in range(2)]
                S.op("pool", lambda e: e.memset(actT[:, 21, 0:2], 0.0), reads=ks, writes=["actT"])
            norm_to_hT(nblk, bt, 1, hT, "hT")
            seq = []
            for fc in range(22):
                seq += [("fg", fc), ("fu", fc)]
            it = wstream(seq)
            for fc in range(22):
                _, _, wbG, keyG = next(it)
                bankG, pkG = fm_matmul(wbG, keyG, 8, 128, rhs_h, ["hT"], TT)
                tmpX, tkey = (tmpA, "tmpA") if fc % 2 == 0 else (tmpB, "tmpB")
                S.op("act", lambda e, bankG=bankG, tmpX=tmpX: e.activation(tmpX[:, 0:TT], ps[:, bankG, 0:TT], AF.Silu),
                     reads=[pkG], writes=[tkey])
                _, _, wbU, keyU = next(it)
                bankU, pkU = fm_matmul(wbU, keyU, 8, 128, rhs_h, ["hT"], TT)
                S.op("dve", lambda e, bankU=bankU, fc=fc, tmpX=tmpX: e.tensor_tensor(actT[:, fc, 0:TT], tmpX[:, 0:TT], ps[:, bankU, 0:TT], ALU.mult),
                     reads=[pkU, tkey], writes=["actT"])
            seq = []
            for oc in range(8):
                seq += [("fd0", oc), ("fd1", oc)]
            it = wstream(seq)
            for oc in range(8):
                _, _, wb0, k0 = next(it)
                _, _, wb1, k1 = next(it)
                bank = 2 + oc % 2
                tm_matmul(bank, [(wb0, k0, 11, 0, 128), (wb1, k1, 11, 11, 128)],
                          lambda k, b: actT[:, k, b * 128:b * 128 + bt], ["actT"], nblk, bt)
                resid_add(bank, oc, nblk, bt)
            norm_to_hT(nblk, bt, 2, hT, "hT")
            for b in range(nblk):
                src = psm[:, :] if sample else pp[(idx * 4 + b) * 128:(idx * 4 + b + 1) * 128, :]
                S.dma("sp", lambda e, b=b, src=src: e.dma_start(out=p_tm[0:bt, b, :], in_=src), writes=[("p_tm", b)], key=("ldp", b))
                for k in range(2):
                    S.op("pe", lambda e, b=b, k=k: e.transpose(ps[:, 7, k * 128:k * 128 + bt], p_tm[0:bt, b, k * 128:(k + 1) * 128], ident[0:bt, 0:bt]),
                         reads=[("p_tm", b), "ident"], writes=[("ps", 7)])
                S.op("act", lambda e, b=b: e.activation(pT[:, :, b * 128:b * 128 + bt], ps[:, 7, 0:256].rearrange("p (k n) -> p k n", n=128)[:, :, 0:bt], AF.Copy),
                     reads=[("ps", 7)], writes=["pT"])
            seq = []
            for oc in range(8):
                seq += [("pg", oc), ("pl", oc)]
            it = wstream(seq)
            for oc in range(8):
                bG, bP = (2, 3) if oc % 2 == 0 else (6, 7)
                tmpX, tkey = (tmpA, "tmpA") if oc % 2 == 0 else (tmpB, "tmpB")
                _, _, wbg, kg = next(it)
                tm_matmul(bG, [(wbg, kg, 8, 0, 128)], lambda k, b: hT[:, k, b * 128:b * 128 + bt], ["hT"], nblk, bt)
                S.op("act", lambda e, bG=bG, tmpX=tmpX: e.activation(tmpX[0:bt, 0:nblk * 128], ps[0:bt, bG, 0:nblk * 128], AF.Sigmoid),
                     reads=[("ps", bG)], writes=[tkey])
                _, _, wbp, kp = next(it)
                tm_matmul(bP, [(wbp, kp, 2, 0, 128)], lambda k, b: pT[:, k, b * 128:b * 128 + bt], ["pT"], nblk, bt)
                S.op("dve", lambda e, bP=bP, tmpX=tmpX: e.tensor_tensor(tmpX[0:bt, 0:nblk * 128], tmpX[0:bt, 0:nblk * 128], ps[0:bt, bP, 0:nblk * 128], ALU.mult),
                     reads=[("ps", bP), tkey], writes=[tkey])
                S.op("dve", lambda e, oc=oc, tmpX=tmpX: e.tensor_tensor(
                    x_tm[0:bt, 0:nblk, oc * 128:(oc + 1) * 128],
                    tmpX[0:bt, 0:nblk * 128].rearrange("p (b n) -> p b n", n=128),
                    x_tm[0:bt, 0:nblk, oc * 128:(oc + 1) * 128], ALU.add),
                    reads=[tkey] + [("x", b) for b in range(nblk)], writes=[("x", b) for b in range(nblk)])
            for b in range(nblk):
                hb = b % 2
                S.op("act", lambda e, b=b, hb=hb: e.activation(h_tm[0:bt, hb, :], x_tm[0:bt, b, :], AF.Square, accum_out=stat[0:bt, b:b + 1]),
                     reads=[("x", b)], writes=[("h_tm", hb), ("stat", b)])
            S.op("dve", lambda e: e.tensor_scalar(stat[0:bt, 4:4 + nblk], stat[0:bt, 0:nblk], 1.0 / D, EPS, ALU.mult, ALU.add),
                 reads=[("stat", b) for b in range(nblk)], writes=["stat_b"])
            S.op("pool", lambda e: e.tensor_tensor(stat[0:bt, 8:8 + nblk], stat[0:bt, 4:4 + nblk], mhalf[0:bt, 0:nblk], ALU.pow),
                 reads=["stat_b", "mhalf"], writes=["rstd"])
            for b in range(nblk):
                hb = b % 2
                S.op("dve", lambda e, b=b, hb=hb: e.scalar_tensor_tensor(h_tm[0:bt, hb, :], x_tm[0:bt, b, :], stat[0:bt, 8 + b:9 + b], gfb[0:bt, :], ALU.mult, ALU.mult),
                     reads=[("x", b), "rstd", "gfb"], writes=[("h_tm", hb)])
                dst = ys[:, :] if sample else y[(idx * 4 + b) * 128:(idx * 4 + b + 1) * 128, :]
                S.dma("pool", lambda e, hb=hb, dst=dst: e.dma_start(out=dst, in_=h_tm[0:bt, hb, :]), reads=[("h_tm", hb)], key=("sty", hb))

        def kv_out(kind, idx, g, which, hc, nblk, bt):
            W = GROUPS[g][0]
            col0 = which * 256 + hc * 128
            if kind == "sample":
                for s in range(NSEQ_S):
                    S.dma("pool", lambda e, s=s, g=g, which=which, col0=col0, W=W, hc=hc: e.dma_start(
                        out=kvs[g][s, W - 4:W, col0:col0 + 128], in_=kv_tm[4 * s:4 * s + 4, 2 * which + hc, 0:128]),
                        reads=[("kv_tm", which, hc)], key=("stkv", which, hc, s))
            else:
                for b in range(nblk):
                    tok0 = (idx * 4 + b) * 128 - (MAIN - W)
                    if tok0 < 0:
                        continue
                    S.dma("pool", lambda e, b=b, g=g, which=which, col0=col0, tok0=tok0, hc=hc: e.dma_start(
                        out=kvp[g][tok0:tok0 + 128, col0:col0 + 128], in_=kv_tm[:, 2 * which + hc, b * 128:(b + 1) * 128]),
                        reads=[("kv_tm", which, hc)], key=("stkv", which, hc, b))

        def conv_branch(kind, idx, nblk, bt, TT, last):
            sample = kind == "sample"
            if sample:
                for s in range(NSEQ_S):
                    S.dma("sp", lambda e, s=s: e.dma_start(out=p_tm[0:30, 0:2, :].rearrange("p a b -> p (a b)"), in_=stc[s, :, :]),
                          writes=["stc_sb"], key="ldstc")
                    for c in range(4):
                        S.op("pe", lambda e, c=c: e.transpose(ps[:, 7, c * 32:c * 32 + 30], p_tm[0:30, 0:2, :].rearrange("p a b -> p (a b)")[:, c * 128:(c + 1) * 128], ident[0:30, 0:30]),
                             reads=["stc_sb", "ident"], writes=[("ps", 7)])
                    S.op("dve", lambda e, s=s: e.tensor_copy(
                        u_ext[:, :, 0:4 * 34].rearrange("p c (s t) -> p c s t", t=34)[:, :, s, 0:30],
                        ps[:, 7, 0:128].rearrange("p (c t) -> p c t", t=32)[:, :, 0:30]),
                        reads=[("ps", 7)], writes=[("u_ext", c) for c in range(4)] + ["u_ext_h"])
                    S.dma("sp", lambda e, s=s: e.dma_start(out=convs[s, 0:26, :], in_=stc[s, 4:30, :]), key="cpconvs")
            cbufs = [(tmpA, "tmpA"), (tmpB, "tmpB"), (c_sb[:, 0, :], ("c_sb", 0)), (c_sb[:, 1, :], ("c_sb", 1))]
            if sample:
                x_c = Oacc
                xck = lambda b: [("Oacc", b)]
            else:
                x_c = actT[:, 0:8, :].rearrange("p a b -> p (a b)").bitcast(F32).rearrange("p (b n) -> p b n", n=DC)
                xck = lambda b: ["actT"]
            for cp in range(2):
                for j in range(31):
                    for c in (2 * cp, 2 * cp + 1):
                        bank = 2 + (c % 2)
                        if sample:
                            src = u_ext[:, c, 0:4 * 34].rearrange("p (s t) -> p s t", t=34)[:, :, j:j + 4]
                            acc = ps[:, bank, 0:TT].rearrange("p (s t) -> p s t", t=4)
                        else:
                            src = u_ext[:, c, j:j + TT]
                            acc = ps[:, bank, 0:TT]
                        rk = [("u_ext", c), "cvcols", "u_ext_h"]
                        if j == 0:
                            S.op("dve", lambda e, src=src, acc=acc, c=c: e.tensor_scalar(acc, src, cvcols[:, c, 0:1], cvcols[:, c, 31:32], ALU.mult, ALU.add),
                                 reads=rk, writes=[("ps", bank)])
                        else:
                            S.op("dve", lambda e, src=src, acc=acc, c=c, j=j: e.scalar_tensor_tensor(acc, src, cvcols[:, c, j:j + 1], acc, ALU.mult, ALU.add),
                                 reads=rk + [("ps", bank)], writes=[("ps", bank)])
                        yield "tap"
                for c in (2 * cp, 2 * cp + 1):
                    bank = 2 + (c % 2)
                    cb_ap, cb_key = cbufs[c]
                    S.op("act", lambda e, cb_ap=cb_ap, bank=bank: e.activation(cb_ap[:, 0:TT], ps[:, bank, 0:TT], AF.Copy),
                         reads=[("ps", bank)], writes=[cb_key])
            for c in range(4):
                cb_ap, cb_key = cbufs[c]
                for b in range(nblk):
                    S.op("pe", lambda e, cb_ap=cb_ap, b=b: e.transpose(ps[0:bt, 7, b * 128:(b + 1) * 128], cb_ap[:, b * 128:b * 128 + bt], ident[:, :]),
                         reads=[cb_key, "ident"], writes=[("ps", 7)])
                    yield "tap"
                S.op("act", lambda e, c=c: e.activation(
                    x_c[0:bt, 0:nblk, c * 128:(c + 1) * 128], ps[0:bt, 7, 0:nblk * 128].rearrange("p (b n) -> p b n", n=128), AF.Copy),
                    reads=[("ps", 7)], writes=[k_ for b in range(nblk) for k_ in xck(b)])
                yield "tap"
            for b in range(nblk):
                S.op("act", lambda e, b=b: e.activation(tmpA[0:bt, 0:DC], x_c[0:bt, b, :], AF.Copy, accum_out=lnst[0:bt, b:b + 1]),
                     reads=xck(b), writes=["tmpA", ("ln_s1", b)])
                S.op("act", lambda e, b=b: e.activation(tmpA[0:bt, 0:DC], x_c[0:bt, b, :], AF.Square, accum_out=lnst[0:bt, 4 + b:5 + b]),
                     reads=xck(b), writes=["tmpA", ("ln_s2", b)])
                yield "tap"
            S.op("dve", lambda e: e.tensor_scalar(lnst[0:bt, 8:8 + nblk], lnst[0:bt, 0:nblk], 1.0 / DC, None, ALU.mult),
                 reads=[("ln_s1", b) for b in range(nblk)], writes=["ln_mu"])
            S.op("dve", lambda e: e.tensor_tensor(lnst[0:bt, 0:nblk], lnst[0:bt, 8:8 + nblk], lnst[0:bt, 8:8 + nblk], ALU.mult),
                 reads=["ln_mu"], writes=["ln_mu2"] + [("ln_s1", b) for b in range(nblk)])
            S.op("dve", lambda e: e.scalar_tensor_tensor(lnst[0:bt, 4:4 + nblk], lnst[0:bt, 4:4 + nblk], 1.0 / DC, lnst[0:bt, 0:nblk], ALU.mult, ALU.subtract),
                 reads=[("ln_s2", b) for b in range(nblk)] + ["ln_mu2"], writes=["ln_var"] + [("ln_s2", b) for b in range(nblk)])
            S.op("dve", lambda e: e.tensor_scalar(lnst[0:bt, 4:4 + nblk], lnst[0:bt, 4:4 + nblk], EPS, None, ALU.add),
                 reads=["ln_var"], writes=["ln_var"] + [("ln_s2", b) for b in range(nblk)])
            S.op("pool", lambda e: e.tensor_tensor(lnst[0:bt, 12:12 + nblk], lnst[0:bt, 4:4 + nblk], mhalf[0:bt, 0:nblk], ALU.pow),
                 reads=["ln_var", "mhalf"], writes=["ln_rstd"])
            yield "tap"
            for b in range(nblk):
                S.op("dve", lambda e, b=b: e.tensor_scalar(chat[0:bt, b, :], x_c[0:bt, b, :], lnst[0:bt, 8 + b:9 + b], lnst[0:bt, 12 + b:13 + b], ALU.subtract, ALU.mult),
                     reads=xck(b) + ["ln_mu", "ln_rstd"], writes=[("chat", b)])
                yield "tap"
                for c in range(4):
                    S.op("pe", lambda e, b=b, c=c: e.transpose(psb(7)[:, c * 128:c * 128 + bt], chat[0:bt, b, c * 128:(c + 1) * 128], identb[0:bt, 0:bt]),
                         reads=[("chat", b), "identb"], writes=[("ps", 7)])
                for c in range(4):
                    S.op("act", lambda e, b=b, c=c: e.activation(sT[:, c, b * 128:b * 128 + bt], psb(7)[:, c * 128:c * 128 + bt], AF.Silu,
                                                               scale=cvcols[:, c, 32:33], bias=cvcols[:, c, 33:34]),
                         reads=[("ps", 7), "cvcols"], writes=["sT"])
                yield "tap"
            if sample or last:
                for c in range(4):
                    if sample:
                        srcu = u_ext[:, c, 0:4 * 34].rearrange("p (s t) -> p s t", t=34)[:, :, 30:34]
                        S.op("act", lambda e, srcu=srcu: e.activation(c_sb[:, 0, 0:16].rearrange("p (s t) -> p s t", t=4), srcu, AF.Copy),
                             reads=[("u_ext", c)], writes=[("c_sb", 0)])
                        S.op("pe", lambda e, c=c: e.transpose(ps[0:16, 7, c * 128:(c + 1) * 128], c_sb[:, 0, 0:16], ident[:, :]),
                             reads=[("c_sb", 0), "ident"], writes=[("ps", 7)])
                    else:
                        S.op("pe", lambda e, c=c: e.transpose(ps[:, 7, c * 128:(c + 1) * 128], u_ext[:, c, 30 + 384:30 + 512], ident[:, :]),
                             reads=[("u_ext", c), "ident"], writes=[("ps", 7)])
                n_r = 16 if sample else 128
                S.op("act", lambda e: e.activation(tmpA[0:n_r, 0:512], ps[0:n_r, 7, 0:512], AF.Copy), reads=[("ps", 7)], writes=["tmpA"])
                if sample:
                    for s in range(NSEQ_S):
                        S.dma("pool", lambda e, s=s: e.dma_start(out=convs[s, 26:30, :], in_=tmpA[4 * s:4 * s + 4, 0:512]), reads=["tmpA"], key="stconv")
                else:
                    S.dma("pool", lambda e: e.dma_start(out=convp[:, :], in_=tmpA[98:128, 0:512]), reads=["tmpA"], key="stconv")
            if not sample:
                S.op("act", lambda e: e.activation(u_ext[:, :, 0:30], u_ext[:, :, 512:542], AF.Copy),
                     reads=[("u_ext", c) for c in range(4)], writes=["u_ext_h"])
            yield "end"


        def prompt_attention(idx, filler=None, filler2=None):
            for hs in range(4):
                for g in range(2):
                    h = 4 * g + hs
                    hc = hs // 2
                    po = 64 * (hs % 2)
                    ob = (6, 1)[(hs * 3 + g) % 2]
                    for qb in range(4):
                        gb = 16 + 4 * idx + qb
                        chunks = []
                        t0 = sum(NCH[:g])
                        for m in range(NCH[g]):
                            kb = gb - m
                            sl = ring_slot(g, kb)
                            kap = kT[g][po:po + 64, hc, sl * 128:(sl + 1) * 128]
                            vap = vap_of(g, sl, hs, kb < 16)
                            chunks.append((kap, [("kT", g, sl), ("V", g, sl)], vap, dist[:, t0 + m, :], 128,
                                           dist[:, t0 + m:t0 + NCH[g], :].rearrange("p a b -> p (a b)"),
                                           onesb[:, 64:128] if kb < 16 else onesb[:, 0:64]))
                        attn_block.first = True
                        attn_block.last = True
                        attn_block.obank = ob
                        attn_block(h, slice(qb * 128, (qb + 1) * 128), chunks, slice(qb * 128, (qb + 1) * 128))
                    if g == 0:
                        f = lambda hs=hs, ob=ob: S.op("dve", lambda e: e.tensor_copy(Oacc[:, hs, :], ps[:, ob, :]),
                                                      reads=[("ps", ob)], writes=[("Oacc", hs)])
                    else:
                        f = lambda hs=hs, ob=ob: S.op("dve", lambda e: e.tensor_tensor(Oacc[:, hs, :], ps[:, ob, :], Oacc[:, hs, :], ALU.add),
                                                      reads=[("ps", ob), ("Oacc", hs)], writes=[("Oacc", hs)])
                    attn_block.pending[-1]["post"].append(f)
                g2_batches(idx, hs)
            attn_flush(filler, filler2=filler2)
            g2_insert(4 + idx)

        def g2_batches(idx, hs):
            h = 8 + hs
            hc, po = hs // 2, 64 * (hs % 2)
            rot = (4 + idx) % 4
            sl = -SLOPES[h]
            for part in range(2):
                nk = 128 if part == 0 else 32
                ob = (6, 1)[(hs * 3 + 2 + part) % 2]
                dtab_ = d2[0:nk, rot:rot + 1, :] if part == 0 else d2[0:32, 4:5, :]

                def SE(sbank, pb, part=part, nk=nk, dtab_=dtab_):
                    skey = ("ps", sbank)

                    def fn_s(e):
                        r_ = None
                        for r in range(16):
                            kap = kTr2[po:po + 64, hc, r, :] if part == 0 else kTc2[po:po + 64, hc, r * 32:(r + 1) * 32]
                            r_ = e.matmul(ps[0:nk, sbank, r * 32:(r + 1) * 32], kap, qT[po:po + 64, 4 + hc, r * 32:(r + 1) * 32],
                                          start=True, stop=True)
                        return r_
                    S.op("pe", fn_s, reads=["qT", "kTr2" if part == 0 else "kTc2"], writes=[skey])
                    S.op("dve", lambda e: e.scalar_tensor_tensor(
                        ps[0:nk, sbank, :].rearrange("p (r u) -> p r u", u=32), dtab_.to_broadcast([nk, 16, 32]), sl,
                        ps[0:nk, sbank, :].rearrange("p (r u) -> p r u", u=32), ALU.mult, ALU.add),
                        reads=[skey, "d2"], writes=[skey])
                    S.op("act", lambda e: e.activation(P_sb[0:nk, pb, :], ps[0:nk, sbank, :], AF.Exp),
                         reads=[skey], writes=[("P_sb", pb)])

                def O(pb, part=part, nk=nk, ob=ob):
                    def fn_o(e):
                        r_ = None
                        for r in range(16):
                            vsrc = Vr2r if part == 0 else Vc2
                            e.matmul(ps[0:64, ob, r * 32:(r + 1) * 32], vsrc[0:nk, r, hs * 64:hs * 64 + 64],
                                     P_sb[0:nk, pb, r * 32:(r + 1) * 32], start=True, stop=True)
                            r_ = e.matmul(ps[64:128, ob, r * 32:(r + 1) * 32], vsrc[0:nk, r, 256:320],
                                          P_sb[0:nk, pb, r * 32:(r + 1) * 32], start=True, stop=True)
                        return r_
                    S.op("pe", fn_o, reads=[("P_sb", pb), "Vr2r" if part == 0 else "Vc2"], writes=[("ps", ob)])
                post = lambda ob=ob: S.op("dve", lambda e: e.tensor_tensor(
                    Oacc[:, hs, :].rearrange("p (u r) -> p r u", r=16), ps[:, ob, :].rearrange("p (r u) -> p r u", u=32),
                    Oacc[:, hs, :].rearrange("p (u r) -> p r u", r=16), ALU.add),
                    reads=[("ps", ob), ("Oacc", hs)], writes=[("Oacc", hs)])
                attn_block.pending.append(dict(custom=dict(SE=SE, O=O), pre=attn_block.pre, post=[post]))
                attn_block.pre = []

        def sample_attention():
            for hs in range(4):
                pass
            tiles = []
            for s in range(NSEQ_S):
                tiles.append((s, 0, 0, 0))
                for r in range(4):
                    tiles.append((s, 1, r, 1 + r))
                for r in range(4):
                    tiles.append((s, 2, r, 5 + r))
            for g in range(3):
                pass
            S.op("dve", lambda e: e.memset(Oacc[:, :, 0:TS], 0.0), writes=[("Oacc", hs) for hs in range(4)])
            for hs in range(4):
                hc, po = hs // 2, 64 * (hs % 2)
                for g in range(3):
                    h = 4 * g + hs
                    if g == 2:
                        kap = kTc2[po:po + 64, hc, 0:TS]
                        vap = Vc2[0:TS, 0, hs * 64:hs * 64 + 64]
                    else:
                        kap = kT[g][po:po + 64, hc, 0:TS]
                        vap = Vr[g][0:TS, hs * 64:hs * 64 + 64]
                    attn_block.first = True
                    attn_block.last = True
                    attn_block.obank = (6, 1)[(hs * 3 + g) % 2]
                    attn_block(h, slice(0, TS), [(kap, [("kTs", g), ("Vs", g), ("kT", g, 0), ("V", g, 0), "kTc2", "Vc2"], vap, sdist[0:TS, 9 + g, :], TS, None, onesb[0:TS, 0:64])],
                               slice(0, 16))
                    attn_block.pending[-1]["post"].append(
                        lambda hs=hs, ob=attn_block.obank: S.op("dve", lambda e: e.tensor_tensor(
                            Oacc[:, hs, 0:TS], ps[:, ob, 0:16], Oacc[:, hs, 0:TS], ALU.add),
                            reads=[("ps", ob), ("Oacc", hs)], writes=[("Oacc", hs)]))
            ctile = actT[:, 0:4, :].rearrange("p a b -> p (a b)").bitcast(F32)
            for ti, (s, g, r, tab) in enumerate(tiles):
                d = GROUPS[g][1]
                cb = ti % 2
                cf = ctile[:, cb * 512:(cb + 1) * 512]
                ckey, vkey, kkey = ("ctile", cb), ("cv", cb), ("ck", cb)
                src = cg[g][s].rearrange("(u r) f -> u r f", r=d)[:, r, :]
                vb = actT[:, 4 + cb, :]
                kb_ = actT[:, 6 + cb, :]

                def prep(cf=cf, src=src, vb=vb, kb_=kb_, ckey=ckey, vkey=vkey, kkey=kkey, cb=cb):
                    S.dma("sp", lambda e: e.dma_start(out=cf, in_=src), writes=[ckey], key=("ldc", cb))
                    S.op("act", lambda e: e.activation(vb[:, 0:256], cf[:, 256:512], AF.Copy), reads=[ckey], writes=[vkey])
                    for hc in range(2):
                        S.op("pe", lambda e, hc=hc: e.transpose(ps[:, 7, hc * 128:(hc + 1) * 128], cf[:, hc * 128:(hc + 1) * 128], ident[:, :]),
                             reads=[ckey, "ident"], writes=[("ps", 7)])
                    S.op("dve", lambda e: e.tensor_copy(kb_[:, 0:256], ps[:, 7, 0:256]), reads=[("ps", 7)], writes=[kkey])
                attn_block.pre.append(prep)
                for hs in range(4):
                    hc, po = hs // 2, 64 * (hs % 2)
                    h = 4 * g + hs
                    kap = kb_[po:po + 64, hc * 128:(hc + 1) * 128]
                    vap = actT[:, 4 + cb, hs * 64:hs * 64 + 64]
                    attn_block.first = True
                    attn_block.last = True
                    attn_block.obank = (6, 1)[(ti * 4 + hs) % 2]
                    attn_block(h, slice(4 * s, 4 * s + 4), [(kap, [kkey, vkey], vap, sdist[:, tab, 4 * s:4 * s + 4], 128, None, onesb[:, 0:64])],
                               slice(0, 4))
                    attn_block.pending[-1]["post"].append(
                        lambda hs=hs, s=s, ob=attn_block.obank: S.op("dve", lambda e: e.tensor_tensor(
                            Oacc[:, hs, 4 * s:4 * s + 4], ps[:, ob, 0:4], Oacc[:, hs, 4 * s:4 * s + 4], ALU.add),
                            reads=[("ps", ob), ("Oacc", hs)], writes=[("Oacc", hs)]))
                if r == 0:
                    L = GROUPS[g][0]
                    S.dma("sp", lambda e, s=s, g=g, L=L: e.dma_start(out=kvs[g][s, 0:L - 4, :], in_=cg[g][s, 4:L, :]), key=("cpkv", g))
            attn_flush(None)

        def sample_prep():
            ks = ["actT", "cones"] + [(n, cb) for n in ("ctile", "cv", "ck") for cb in range(2)]
            S.op("pool", lambda e: e.memset(actT[:, 4:6, 256:320], 1.0), reads=[], writes=ks)

        for ht in PLAN["halo"]:
            run_tile("halo", ht)
        for i in PLAN["main"]:
            run_tile("main", i)
        if PLAN["sample"]:
            sample_prep()
            run_tile("sample", 0)
        nops = S.finalize()
        PLAN["nops"] = nops
        PLAN["sbuf_left"] = nc.sbuf_bytes_remaining
    return nc


_CACHE = {}


def kernel(x_prompt, x_sample, state_conv, cache_kv_g0, cache_kv_g1, cache_kv_g2, p_prompt, p_sample,
           g_mix, w_in, w_dw, b_dw, ln_g, ln_b, w_conv_out, w_att_out, w_out, g_ffn, w_ffn_gate,
           w_ffn_up, w_ffn_down, g_ple, w_ple_gate, w_ple, g_final):
    f = lambda a: np.ascontiguousarray(np.asarray(a, dtype=np.float32))
    x_prompt, x_sample, state_conv, p_prompt, p_sample = map(f, (x_prompt, x_sample, state_conv, p_prompt, p_sample))
    caches = [f(cache_kv_g0), f(cache_kv_g1), f(cache_kv_g2)]
    if "nc" not in _CACHE:
        _CACHE["nc"] = build_program()
    nc = _CACHE["nc"]
    shared = dict(
        dtab=dist_tables(), d2tab=g2_tables(), sdtab=sample_dist_tables(),
        g3=np.ascontiguousarray(np.stack([f(g_mix)[0], f(g_ffn)[0], f(g_ple)[0]])),
        gfin=f(g_final).reshape(1, D),
        cvp=np.ascontiguousarray(np.concatenate([f(w_dw)[0], f(b_dw), f(ln_g), f(ln_b)], axis=0)),
        w_in=f(w_in)[0], w_co=f(w_conv_out)[0], w_ao=f(w_att_out)[0], w_o=f(w_out)[0],
        w_fg=f(w_ffn_gate)[0], w_fu=f(w_ffn_up)[0], w_fd=f(w_ffn_down)[0], w_pg=f(w_ple_gate)[0], w_pl=f(w_ple)[0],
    )
    in_maps = []
    for c in range(NCORES):
        b, half = c // 2, c % 2
        xhc = np.zeros((HALO + MAIN, D), np.float32)
        if half == 1:
            xhc[:] = x_prompt[b, MAIN - HALO:2 * MAIN]
        else:
            xhc[HALO:] = x_prompt[b, 0:MAIN]
        m = dict(shared)
        m["xh"] = xhc
        m["pp"] = np.ascontiguousarray(p_prompt[0, b, half * MAIN:(half + 1) * MAIN])
        m["flag"] = np.full((128, 1), float(half), np.float32)
        m["xs"] = np.ascontiguousarray(x_sample[4 * c:4 * c + 4].reshape(TS, D))
        m["psm"] = np.ascontiguousarray(p_sample[0, 4 * c:4 * c + 4].reshape(TS, DPLE))
        m["stc"] = np.ascontiguousarray(state_conv[0, 4 * c:4 * c + 4])
        for g in range(3):
            m["cg%d" % g] = np.ascontiguousarray(caches[g][0, 4 * c:4 * c + 4].reshape(NSEQ_S, GROUPS[g][0], 512))
        in_maps.append(m)
    res = run_bass_kernel_spmd(nc, in_maps, core_ids=list(range(NCORES)))
    R = res.results
    y_prompt = np.empty((4, 8192, D), np.float32)
    y_sample = np.empty((32, 4, D), np.float32)
    conv_p = np.empty((1, 4, 30, DC), np.float32)
    conv_s = np.empty((1, 32, 30, DC), np.float32)
    kv_p = [np.empty((1, 4, GROUPS[g][0], 2, 4, 64), np.float32) for g in range(3)]
    kv_s = [np.empty((1, 32, GROUPS[g][0], 2, 4, 64), np.float32) for g in range(3)]
    for c in range(NCORES):
        b, half = c // 2, c % 2
        y_prompt[b, half * MAIN:(half + 1) * MAIN] = R[c]["y"]
        y_sample[4 * c:4 * c + 4] = R[c]["ys"].reshape(4, 4, D)
        conv_s[0, 4 * c:4 * c + 4] = R[c]["convs"]
        for g in range(3):
            kv_s[g][0, 4 * c:4 * c + 4] = R[c]["kvs%d" % g].reshape(4, GROUPS[g][0], 2, 4, 64)
        if half == 1:
            conv_p[0, b] = R[c]["convp"]
            for g in range(3):
                kv_p[g][0, b] = R[c]["kvp%d" % g].reshape(GROUPS[g][0], 2, 4, 64)
    return (y_prompt, y_sample, conv_p, conv_s, kv_p[0], kv_s[0], kv_p[1], kv_s[1], kv_p[2], kv_s[2])
```

```python
import numpy as np
from contextlib import ExitStack
import concourse.bass as bass
import concourse.mybir as mybir
from concourse.bass_utils import run_bass_kernel_spmd

F32 = mybir.dt.float32
BF16 = mybir.dt.bfloat16
ALU = mybir.AluOpType
AF = mybir.ActivationFunctionType

D = 1024
DC = 512
NIN = 5376
DFF = 2816
DPLE = 256
EPS = 1e-6
GROUPS = ((128, 1), (512, 4), (2048, 16))
NCH = (2, 5, 17)
RING = (5, 8, 20)
NCORES = 8
NSEQ_S = 4
TS = 16
HALO = 2048
MAIN = 4096
SLOPES = [2.0 ** (-8.0 * (h + 1) / 12.0) for h in range(12)]


class Sched:
    def __init__(self, nc, stack):
        self.nc = nc
        self.stack = stack
        self.ops = []
        self.eng_obj = {"pe": nc.tensor, "act": nc.scalar, "dve": nc.vector,
                        "pool": nc.gpsimd, "sp": nc.sync}
        self.sems = {}
        self.nsem = 0

    def sem(self, stream):
        if stream not in self.sems:
            self.nsem += 1
            self.sems[stream] = self.stack.enter_context(self.nc.semaphore("sm%d" % self.nsem))
        return self.sems[stream]

    def op(self, eng, fn, reads=(), writes=(), dma_key=None):
        self.ops.append(dict(eng=eng, fn=fn, reads=tuple(reads), writes=tuple(writes),
                             dma_key=dma_key, signal=False))

    def dma(self, eng, fn, reads=(), writes=(), key=None):
        self.op(eng, fn, reads, writes, dma_key=key)

    def finalize(self, final_eng="sp"):
        ops = self.ops
        last_dma = {}
        for i, o in enumerate(ops):
            if o["dma_key"] is not None:
                last_dma[o["dma_key"]] = i
        ops.append(dict(eng=final_eng, fn=None, reads=(), writes=(), dma_key=None,
                        signal=False, extra_deps=set(last_dma.values())))
        seqctr, last_write, readers, last_on_key = {}, {}, {}, {}
        known = {e: {} for e in self.eng_obj}
        for i, o in enumerate(ops):
            E = o["eng"]
            stream = E if o["dma_key"] is None else ("dma", o["dma_key"])
            o["stream"] = stream
            deps = set(o.get("extra_deps", ()))
            for k in o["reads"]:
                if k in last_write:
                    deps.add(last_write[k])
                if isinstance(k, tuple) and k[0] == "ps":
                    for rs, ri in readers.get(k, {}).items():
                        if rs != stream:
                            deps.add(ri)
            for k in o["writes"]:
                if k in last_write:
                    deps.add(last_write[k])
                deps.update(readers.get(k, {}).values())
            if o["dma_key"] is not None and o["dma_key"] in last_on_key:
                deps.add(last_on_key[o["dma_key"]])
            best = {}
            for d in deps:
                Dd = ops[d]
                s, q = Dd["stream"], Dd["seq"]
                if s not in best or best[s][0] < q:
                    best[s] = (q, d)
            waits = []
            for s, (q, d) in best.items():
                if E == "pe" and s == "pe":
                    continue
                if known[E].get(s, 0) >= q:
                    continue
                waits.append(d)
                ops[d]["signal"] = True
                for ks, kq in ops[d]["clock"].items():
                    if known[E].get(ks, 0) < kq:
                        known[E][ks] = kq
            o["waits"] = waits
            seqctr[stream] = seqctr.get(stream, 0) + 1
            o["seq"] = seqctr[stream]
            clk = dict(known[E])
            clk[stream] = o["seq"]
            o["clock"] = clk
            if E == "pe" and o["dma_key"] is None:
                known["pe"]["pe"] = o["seq"]
            for k in o["reads"]:
                readers.setdefault(k, {})[stream] = i
            for k in o["writes"]:
                last_write[k] = i
                readers[k] = {}
            if o["dma_key"] is not None:
                last_on_key[o["dma_key"]] = i
        sigctr = {}
        for o in ops:
            e = self.eng_obj[o["eng"]]
            for d in o["waits"]:
                Dd = ops[d]
                e.wait_ge(self.sem(Dd["stream"]), Dd["sigval"])
            if o["fn"] is None:
                continue
            r = o["fn"](e)
            stt = o["stream"]
            if o["dma_key"] is not None:
                insts = r if isinstance(r, (list, tuple)) else [r]
                s = self.sem(stt)
                for ins in insts:
                    ins.then_inc(s, 16)
                sigctr[stt] = sigctr.get(stt, 0) + 16 * len(insts)
                o["sigval"] = sigctr[stt]
            elif o["signal"]:
                ins = r[-1] if isinstance(r, (list, tuple)) else r
                ins.then_inc(self.sem(stt), 1)
                sigctr[stt] = sigctr.get(stt, 0) + 1
                o["sigval"] = sigctr[stt]
        return len(ops)


def g2_tables():
    out = np.full((128, 5, 32), 1e6, np.float32)
    u = np.arange(32)[:, None]
    q = np.arange(32)[None, :]
    for rot in range(4):
        for sl in range(4):
            a = ((rot - sl - 1) % 4) + 1
            du = 32 * a + q - u
            out[32 * sl:32 * sl + 32, rot, :] = np.where(du <= 128, 16.0 * du, 1e6)
    du = q - u
    out[0:32, 4, :] = np.where(du >= 0, 16.0 * du, 1e6)
    return out


def dist_tables():
    out = np.empty((128, 7, 128), np.float32)
    ki = np.arange(128)[:, None]
    qi = np.arange(128)[None, :]
    t = 0
    for g, (w, d) in enumerate(GROUPS[:2]):
        for m in range(NCH[g]):
            dist = 128 * m + qi - ki
            ok = (dist >= 0) & (dist <= w) & (dist % d == 0)
            out[:, t, :] = np.where(ok, dist, 1e6)
            t += 1
    return out


def sample_dist_tables():
    out = np.full((128, 12, 16), 1e6, np.float32)
    rho = np.arange(128)
    for s in range(NSEQ_S):
        for i in range(4):
            col = 4 * s + i
            dist = 128 + i - rho
            out[:, 0, col] = np.where(rho >= i, dist, 1e6)
            out[:, 1 + i, col] = 4 * (128 - rho)
            out[:, 5 + i, col] = 16 * (128 - rho)
    for g, (w, d) in enumerate(GROUPS):
        for kk in range(16):
            for qq in range(16):
                if kk // 4 == qq // 4:
                    dd = (qq % 4) - (kk % 4)
                    if dd >= 0 and dd % d == 0:
                        out[kk, 9 + g, qq] = dd
    return out


PLAN = dict(halo=range(4), main=range(8), sample=True)


def build_program():
    nc = bass.Bass("TRN2", target_bir_lowering=False)
    dt_in = lambda n, sh: nc.dram_tensor(n, sh, F32, kind="ExternalInput").ap()
    dt_out = lambda n, sh: nc.dram_tensor(n, sh, F32, kind="ExternalOutput").ap()
    xh = dt_in("xh", [HALO + MAIN, D])
    pp = dt_in("pp", [MAIN, DPLE])
    flag = dt_in("flag", [128, 1])
    xs = dt_in("xs", [TS, D])
    psm = dt_in("psm", [TS, DPLE])
    stc = dt_in("stc", [NSEQ_S, 30, DC])
    cg = [dt_in("cg%d" % g, [NSEQ_S, GROUPS[g][0], 512]) for g in range(3)]
    dtab = dt_in("dtab", [128, 7, 128])
    d2tab = dt_in("d2tab", [128, 5, 32])
    sdtab = dt_in("sdtab", [128, 12, 16])
    g3 = dt_in("g3", [3, D])
    gfin = dt_in("gfin", [1, D])
    cvp = dt_in("cvp", [34, DC])
    w_in = dt_in("w_in", [D, NIN])
    w_co = dt_in("w_co", [DC, D])
    w_ao = dt_in("w_ao", [256, D])
    w_o = dt_in("w_o", [D, D])
    w_fg = dt_in("w_fg", [D, DFF])
    w_fu = dt_in("w_fu", [D, DFF])
    w_fd = dt_in("w_fd", [DFF, D])
    w_pg = dt_in("w_pg", [D, D])
    w_pl = dt_in("w_pl", [DPLE, D])

    y = dt_out("y", [MAIN, D])
    ys = dt_out("ys", [TS, D])
    convp = dt_out("convp", [30, DC])
    convs = dt_out("convs", [NSEQ_S, 30, DC])
    kvp = [dt_out("kvp%d" % g, [GROUPS[g][0], 512]) for g in range(3)]
    kvs = [dt_out("kvs%d" % g, [NSEQ_S, GROUPS[g][0], 512]) for g in range(3)]

    WSPEC = {
        "in": (w_in, 8, 128, NIN), "co": (w_co, 4, 128, D), "ao": (w_ao, 4, 64, D),
        "o": (w_o, 8, 128, D), "fg": (w_fg, 8, 128, DFF), "fu": (w_fu, 8, 128, DFF),
        "fd0": (w_fd, 11, 128, D), "fd1": (w_fd, 11, 128, D),
        "pg": (w_pg, 8, 128, D), "pl": (w_pl, 2, 128, D),
    }
    scratch = {}
    for nm, (ap, nk, pd, ncol) in WSPEC.items():
        scratch[nm] = nc.dram_tensor("wsc_" + nm, [ncol // 128, 128, nk * 128], BF16, kind="Internal").ap()

    with ExitStack() as st:
        S = Sched(nc, st)
        sb = lambda n, sh, dt: st.enter_context(nc.sbuf_tensor(n, sh, dt))
        ident = sb("ident", [128, 128], F32)
        identb = sb("identb", [128, 128], BF16)
        gcols = sb("gcols", [128, 8, 3], F32)
        cvcols = sb("cvcols", [128, 4, 34], F32)
        gfb = sb("gfb", [128, D], F32)
        flag_sb = sb("flag_sb", [128, 1], F32)
        mhalf = sb("mhalf", [128, 8], F32)
        onesb = sb("onesb", [128, 128], BF16)
        dist = sb("dist", [128, 7, 128], F32)
        d2 = sb("d2", [128, 5, 32], F32)
        sdist = sb("sdist", [128, 12, 16], F32)
        x_tm = sb("x_tm", [128, 4, D], F32)
        h_tm = sb("h_tm", [128, 2, D], F32)
        stat = sb("stat", [128, 16], F32)
        lnst = sb("lnst", [128, 16], F32)
        hT = sb("hT", [128, 8, 512], BF16)
        qT = sb("qT", [128, 6, 512], BF16)
        kT = [sb("kT%d" % g, [128, 2, RING[g] * 128], BF16) for g in range(2)]
        Vr = [sb("Vr%d" % g, [128, RING[g] * 256], BF16) for g in range(2)]
        kTr2 = sb("kTr2", [128, 2, 16, 128], BF16)
        kTc2 = sb("kTc2", [128, 2, 512], BF16)
        Vr2r = sb("Vr2r", [128, 16, 320], BF16)
        Vc2 = sb("Vc2", [128, 16, 320], BF16)
        gates = sb("gates", [128, 16, 512], BF16)
        u_ext = sb("u_ext", [128, 4, 30 + 512], F32)
        c_sb = sb("c_sb", [128, 2, 512], F32)
        chat = sb("chat", [128, 4, 512], BF16)
        sT = sb("sT", [128, 4, 512], BF16)
        P_sb = sb("P_sb", [128, 4, 512], BF16)
        Oacc = sb("Oacc", [128, 4, 512], F32)
        rl = sb("rl", [128, 512], F32)
        OnT = sb("OnT", [64, 4, 512], BF16)
        tmpA = sb("tmpA", [128, 512], F32)
        tmpB = sb("tmpB", [128, 512], F32)
        actT = sb("actT", [128, 22, 512], BF16)
        p_tm = sb("p_tm", [128, 4, DPLE], F32)
        pT = sb("pT", [128, 2, 512], BF16)
        kv_tm = sb("kv_tm", [128, 4, 512], F32)
        NWB = 7
        WLA = 5
        assert NWB >= WLA + 2
        wbuf = [sb("wbuf%d" % i, [128, 11 * 128], BF16) for i in range(NWB)]
        ps = st.enter_context(nc.psum_tensor("ps", [128, 8, 512], F32))
        prm_rows = tmpA
        g3_rows = h_tm[:, 0, :]

        def psb(bank):
            return ps[:, bank, :].bitcast(BF16)

        wstate = dict(n=0, nstg=0, seen=set(), cur={})

        def wload(nm, ci):
            ap, nk, pd, ncol = WSPEC[nm]
            slot = wstate["n"] % NWB
            wstate["n"] += 1
            key = "wbuf%d" % slot
            wb = wbuf[slot]
            dstv = wb[0:pd, 0:nk * 128]
            sc = scratch[nm][ci]
            if (nm, ci) in wstate["seen"]:
                S.dma("sp", lambda e, dstv=dstv, sc=sc, pd=pd, nk=nk: e.dma_start(out=dstv, in_=sc[0:pd, 0:nk * 128]),
                      reads=[("wsc", nm, ci)], writes=[key], key=key)
            else:
                wstate["seen"].add((nm, ci))
                if nm == "ao":
                    src = ap.rearrange("(s d) n -> d s n", d=64)[:, :, ci * 128:(ci + 1) * 128]
                elif nm == "fd1":
                    src = ap[11 * 128:22 * 128, :].rearrange("(k p) n -> p k n", p=128)[:, :, ci * 128:(ci + 1) * 128]
                elif nm == "fd0":
                    src = ap[0:11 * 128, :].rearrange("(k p) n -> p k n", p=128)[:, :, ci * 128:(ci + 1) * 128]
                else:
                    src = ap.rearrange("(k p) n -> p k n", p=128)[:, :, ci * 128:(ci + 1) * 128]
                dst3 = dstv.rearrange("p (k n) -> p k n", n=128)
                S.dma("pool", lambda e, dst3=dst3, src=src: e.dma_start(out=dst3, in_=src), writes=[key], key=("wbufc", slot))
                S.dma("sp", lambda e, dstv=dstv, sc=sc, pd=pd, nk=nk: e.dma_start(out=sc[0:pd, 0:nk * 128], in_=dstv),
                      reads=[key], writes=[("wsc", nm, ci)], key=("wst", slot))
            wstate["cur"][(nm, ci)] = (slot, key)

        def wget(nm, ci):
            if (nm, ci) not in wstate["cur"]:
                wload(nm, ci)
            slot, key = wstate["cur"].pop((nm, ci))
            return wbuf[slot], key

        def wstream(seq):
            seq = list(seq)
            for i in range(min(WLA, len(seq))):
                if seq[i] not in wstate["cur"]:
                    wload(*seq[i])
            for i, (nm, ci) in enumerate(seq):
                if i + WLA < len(seq) and seq[i + WLA] not in wstate["cur"]:
                    wload(*seq[i + WLA])
                wb, key = wget(nm, ci)
                yield nm, ci, wb, key

        S.dma("sp", lambda e: e.dma_start(out=dist[:], in_=dtab[:]), writes=["dist"], key="c_dist")
        S.dma("sp", lambda e: e.dma_start(out=d2[:], in_=d2tab[:]), writes=["d2"], key="c_d2")
        S.dma("sp", lambda e: e.dma_start(out=sdist[:], in_=sdtab[:]), writes=["sdist"], key="c_sdist")
        S.dma("sp", lambda e: e.dma_start(out=prm_rows[0:34, :], in_=cvp[:]), writes=["tmpA"], key="c_prm")
        S.dma("sp", lambda e: e.dma_start(out=g3_rows[0:3, :], in_=g3[:]), writes=[("h_tm", 0)], key="c_g3")
        S.dma("sp", lambda e: e.dma_start(out=gfb[:], in_=gfin.to_broadcast([128, D])), writes=["gfb"], key="c_gf")
        S.dma("sp", lambda e: e.dma_start(out=flag_sb[:], in_=flag[:]), writes=["flag"], key="c_flag")
        S.op("pool", lambda e: e.memset(ident[:], 1.0), writes=["ident"])
        S.op("pool", lambda e: e.affine_select(ident[:], ident[:], [[1, 128]], ALU.is_equal, 0.0, base=0,
                                               channel_multiplier=-1), reads=["ident"], writes=["ident"])
        S.op("pool", lambda e: e.memset(mhalf[:], -0.5), writes=["mhalf"])
        S.op("dve", lambda e: e.tensor_copy(identb[:], ident[:]), reads=["ident"], writes=["identb"])
        S.op("pool", lambda e: e.memset(onesb[:, :], 1.0), writes=["onesb"])
        S.op("dve", lambda e: e.tensor_copy(onesb[:, 64:128], flag_sb[:, 0:1].to_broadcast([128, 64])),
             reads=["flag", "onesb"], writes=["onesb"])
        for c in range(4):
            S.op("pe", lambda e, c=c: e.transpose(ps[:, 0, c * 34:c * 34 + 34], prm_rows[0:34, c * 128:(c + 1) * 128], ident[0:34, 0:34]),
                 reads=["tmpA", "ident"], writes=[("ps", 0)])
        S.op("dve", lambda e: e.tensor_copy(cvcols[:].rearrange("p c j -> p (c j)"), ps[:, 0, 0:136]), reads=[("ps", 0)], writes=["cvcols"])
        for c in range(8):
            S.op("pe", lambda e, c=c: e.transpose(ps[:, 1, c * 3:c * 3 + 3], g3_rows[0:3, c * 128:(c + 1) * 128], ident[0:3, 0:3]),
                 reads=[("h_tm", 0), "ident"], writes=[("ps", 1)])
        S.op("dve", lambda e: e.tensor_copy(gcols[:].rearrange("p c j -> p (c j)"), ps[:, 1, 0:24]), reads=[("ps", 1)], writes=["gcols"])
        S.op("dve", lambda e: e.memset(u_ext[:, :, 0:30], 0.0), writes=["u_ext_h"])

        def norm_to_hT(nblk, bt, gi, dst, dstkey):
            TT = nblk * bt
            for b in range(nblk):
                hb = b % 2
                S.op("act", lambda e, b=b, hb=hb: e.activation(h_tm[0:bt, hb, :], x_tm[0:bt, b, :], AF.Square,
                                                               accum_out=stat[0:bt, b:b + 1]),
                     reads=[("x", b)], writes=[("h_tm", hb), ("stat", b)])
            S.op("dve", lambda e: e.tensor_scalar(stat[0:bt, 4:4 + nblk], stat[0:bt, 0:nblk], 1.0 / D, EPS, ALU.mult, ALU.add),
                 reads=[("stat", b) for b in range(nblk)], writes=["stat_b"])
            S.op("pool", lambda e: e.tensor_tensor(stat[0:bt, 8:8 + nblk], stat[0:bt, 4:4 + nblk], mhalf[0:bt, 0:nblk], ALU.pow),
                 reads=["stat_b", "mhalf"], writes=["rstd"])
            for b in range(nblk):
                hb = b % 2
                S.op("act", lambda e, b=b, hb=hb: e.activation(h_tm[0:bt, hb, :], x_tm[0:bt, b, :], AF.Copy,
                                                               scale=stat[0:bt, 8 + b:9 + b]),
                     reads=[("x", b), "rstd"], writes=[("h_tm", hb)])
                for half in range(2):
                    bank = 2 + half
                    for k4 in range(4):
                        kc = half * 4 + k4
                        S.op("pe", lambda e, hb=hb, kc=kc, k4=k4, bank=bank: e.transpose(
                            ps[:, bank, k4 * 128:k4 * 128 + bt], h_tm[0:bt, hb, kc * 128:(kc + 1) * 128], ident[0:bt, 0:bt]),
                            reads=[("h_tm", hb), "ident"], writes=[("ps", bank)])
                    S.op("dve", lambda e, b=b, half=half, bank=bank: e.tensor_tensor(
                        dst[:, half * 4:half * 4 + 4, b * 128:b * 128 + bt],
                        ps[:, bank, 0:512].rearrange("p (k n) -> p k n", n=128)[:, :, 0:bt],
                        gcols[:, half * 4:half * 4 + 4, gi:gi + 1].to_broadcast([128, 4, bt]), ALU.mult),
                        reads=[("ps", bank), "gcols"], writes=[dstkey])

        fm_bank = [0]

        def fm_matmul(wb, wkey, nk, pd, rhs_fn, rhs_keys, TT, bank=None):
            if bank is None:
                bank = fm_bank[0] % 2
                fm_bank[0] += 1
            key = ("ps", bank)

            def fn(e, bank=bank, wb=wb):
                r = None
                for k in range(nk):
                    r = e.matmul(ps[:, bank, 0:TT], wb[0:pd, k * 128:(k + 1) * 128], rhs_fn(k),
                                 start=(k == 0), stop=(k == nk - 1))
                return r
            S.op("pe", fn, reads=[wkey] + list(rhs_keys), writes=[key])
            return bank, key

        def tm_matmul(bank, pieces, lhs_fn, lhs_keys, nblk, bt, ncol=128, col0=0):
            key = ("ps", bank)
            tot = sum(p[2] for p in pieces)

            def fn(e):
                r = None
                for b in range(nblk):
                    i = 0
                    for (wb, wkey, nk, kofs, pd) in pieces:
                        for k in range(nk):
                            r = e.matmul(ps[0:bt, bank, b * 128 + col0:b * 128 + col0 + ncol], lhs_fn(kofs + k, b),
                                         wb[0:pd, k * 128:k * 128 + ncol], start=(i == 0), stop=(i == tot - 1))
                            i += 1
                return r
            S.op("pe", fn, reads=[p[1] for p in pieces] + list(lhs_keys), writes=[key])
            return key

        def resid_add(bank, oc, nblk, bt):
            S.op("dve", lambda e: e.tensor_tensor(
                x_tm[0:bt, 0:nblk, oc * 128:(oc + 1) * 128],
                ps[0:bt, bank, 0:nblk * 128].rearrange("p (b n) -> p b n", n=128),
                x_tm[0:bt, 0:nblk, oc * 128:(oc + 1) * 128], ALU.add),
                reads=[("ps", bank)] + [("x", b) for b in range(nblk)], writes=[("x", b) for b in range(nblk)])

        def attn_block(h, qcols, chunks, pso_cols):
            n = len(chunks)
            for b0 in range(0, n, 4):
                attn_block.pending.append(dict(
                    h=h, qcols=qcols, grp=chunks[b0:b0 + 4], pso=pso_cols, obank=attn_block.obank,
                    first=(b0 == 0 and attn_block.first), last=(b0 + 4 >= n and attn_block.last),
                    pre=attn_block.pre, post=[]))
                attn_block.pre = []
        attn_block.pending = []
        attn_block.pre = []
        attn_block.first = True
        attn_block.last = True
        attn_block.obank = 6
        SBANKS = (4, 5, 0)

        def attn_flush(filler=None, L=2, filler2=None):
            bl = attn_block.pending
            attn_block.pending = []
            n = len(bl)

            def stage_S(t):
                B = bl[t]
                for f in B["pre"]:
                    f()
                if "custom" in B:
                    B["custom"]["SE"](SBANKS[t % 3], t % 4)
                    return
                h = B["h"]
                po, qc = 64 * (h % 2), h // 2
                qcols = B["qcols"]
                nq = qcols.stop - qcols.start
                sbank = SBANKS[t % 3]
                pb = t % 4
                grp = B["grp"]
                skey = ("ps", sbank)

                def fn_s(e, grp=grp, sbank=sbank, po=po, qc=qc, qcols=qcols, nq=nq):
                    r = None
                    for j, c in enumerate(grp):
                        r = e.matmul(ps[0:c[4], sbank, j * 128:j * 128 + nq], c[0], qT[po:po + 64, qc, qcols],
                                     start=True, stop=True)
                    return r
                rk = set()
                for c in grp:
                    rk.update(c[1])
                B["rk"] = rk
                S.op("pe", fn_s, reads=["qT"] + list(rk), writes=[skey])
                same = all(c[4] == 128 for c in grp) and nq == 128 and grp[0][5] is not None
                sl = -SLOPES[h]
                if same:
                    W = len(grp) * 128
                    dall = grp[0][5][:, 0:W]
                    S.op("dve", lambda e, sbank=sbank, dall=dall, W=W, sl=sl: e.scalar_tensor_tensor(
                        ps[:, sbank, 0:W], dall, sl, ps[:, sbank, 0:W], ALU.mult, ALU.add),
                        reads=[skey, "dist"], writes=[skey])
                    S.op("act", lambda e, sbank=sbank, pb=pb, W=W: e.activation(P_sb[:, pb, 0:W], ps[:, sbank, 0:W], AF.Exp),
                         reads=[skey], writes=[("P_sb", pb)])
                else:
                    for j, c in enumerate(grp):
                        nk = c[4]
                        S.op("dve", lambda e, sbank=sbank, dap=c[3], nk=nk, j=j, sl=sl, nq=nq: e.scalar_tensor_tensor(
                            ps[0:nk, sbank, j * 128:j * 128 + nq], dap, sl, ps[0:nk, sbank, j * 128:j * 128 + nq],
                            ALU.mult, ALU.add), reads=[skey, "sdist"], writes=[skey])
                        S.op("act", lambda e, sbank=sbank, pb=pb, nk=nk, j=j, nq=nq: e.activation(
                            P_sb[0:nk, pb, j * 128:j * 128 + nq], ps[0:nk, sbank, j * 128:j * 128 + nq], AF.Exp),
                            reads=[skey], writes=[("P_sb", pb)])

            def stage_O(t):
                B = bl[t]
                if "custom" in B:
                    B["custom"]["O"](t % 4)
                    for f in B["post"]:
                        f()
                    return
                qcols = B["qcols"]
                nq = qcols.stop - qcols.start
                pb = t % 4
                grp, ob, pso = B["grp"], B["obank"], B["pso"]
                ng = len(grp)

                def fn_o(e, grp=grp, pb=pb, ob=ob, pso=pso, nq=nq, first=B["first"], last=B["last"], ng=ng):
                    r = None
                    for j, c in enumerate(grp):
                        st_ = (j == 0 and first)
                        sp_ = (j == ng - 1 and last)
                        e.matmul(ps[0:64, ob, pso], c[2], P_sb[0:c[4], pb, j * 128:j * 128 + nq], start=st_, stop=sp_)
                        r = e.matmul(ps[64:128, ob, pso], c[6], P_sb[0:c[4], pb, j * 128:j * 128 + nq], start=st_, stop=sp_)
                    return r
                S.op("pe", fn_o, reads=[("P_sb", pb), "onesb"] + list(B["rk"]), writes=[("ps", ob)])
                for f in B["post"]:
                    f()

            fdone = [False]
            for t in range(n + L):
                if t < n:
                    stage_S(t)
                for _r in range(2):
                    if filler is not None and not fdone[0]:
                        v = next(filler, None)
                        if v != "tap":
                            fdone[0] = True
                if filler2 is not None and t % 3 == 1:
                    next(filler2, None)
                if t - L >= 0:
                    stage_O(t - L)

        def g2_insert(tg):
            sl = tg % 4
            for hc in range(2):
                S.op("act", lambda e, hc=hc, sl=sl: e.activation(
                    kTr2[:, hc, :, sl * 32:(sl + 1) * 32], kTc2[:, hc, :].rearrange("p (r u) -> p r u", u=32), AF.Copy),
                    reads=["kTc2"], writes=["kTr2"])
            S.dma("sp", lambda e, sl=sl: e.dma_start(out=Vr2r[32 * sl:32 * sl + 32, :, :], in_=Vc2[0:32, :, :]),
                  reads=["Vc2"], writes=["Vr2r"], key="ins_v2")

        def ring_slot(g, gb):
            return gb % RING[g]

        def vap_of(g, slot, hs, halo):
            base = slot * 256 + hs * 64
            return Vr[g][:, base:base + 64]

        def run_tile(kind, idx):
            sample = kind == "sample"
            halo = kind == "halo"
            nblk, bt = (1, TS) if sample else (4, 128)
            TT = nblk * bt
            gb0 = (4 * idx) if halo else (16 + 4 * idx)
            last = (kind == "main" and idx == 7)
            for b in range(nblk):
                if sample:
                    src = xs[:, :]
                elif halo:
                    src = xh[(idx * 4 + b) * 128:(idx * 4 + b + 1) * 128, :]
                else:
                    src = xh[HALO + (idx * 4 + b) * 128:HALO + (idx * 4 + b + 1) * 128, :]
                S.dma("sp", lambda e, b=b, src=src: e.dma_start(out=x_tm[0:bt, b, :], in_=src),
                      writes=[("x", b)], key=("ldx", b))
            if PLAN.get("stage", 99) < 1:
                return
            norm_to_hT(nblk, bt, 0, hT, "hT")
            if PLAN.get("stage", 99) < 2:
                return
            if halo:
                grps = [2] + ([1] if idx == 3 else []) + ([0] if idx == 3 else [])
                need_ab = idx == 3
            else:
                grps = [0, 1, 2]
                need_ab = True
            seq = []
            if need_ab:
                seq += [("in", c) for c in (4, 5, 6, 7, 0, 1, 2, 3)]
            if not halo:
                seq += [("in", 8 + c) for c in range(6)]
            for g in grps:
                seq += [("in", 14 + 2 * g), ("in", 15 + 2 * g), ("in", 20 + 2 * g), ("in", 21 + 2 * g)]
            if sample:
                seq += [("in", 26 + c) for c in range(16)]
            rhs_h = lambda k: hT[:, k, 0:TT]
            gen = None
            taps_left = [0]
            if kind == "main":
                gen = conv_branch(kind, idx, nblk, bt, TT, last)
                taps_left[0] = 124

            def pull_taps(n):
                n = min(n, taps_left[0])
                for _ in range(n):
                    next(gen)
                taps_left[0] -= n
            for nm, ci, wb, wkey in wstream(seq):
                if ci >= 9:
                    pull_taps(1)
                if PLAN.get("wonly"):
                    S.op("dve", lambda e, wb=wb: e.tensor_copy(tmpA[:, 0:8], wb[:, 0:8]), reads=[wkey], writes=["tmpA"])
                    continue
                if PLAN.get("skipk") and 14 <= ci < 20:
                    continue
                if PLAN.get("skipv") and 20 <= ci < 26:
                    continue
                if ci < 8 or (8 <= ci < 20) or ci >= 26:
                    bank, pkey = fm_matmul(wb, wkey, 8, 128, rhs_h, ["hT"], TT)
                if 4 <= ci < 8:
                    c = ci - 4
                    S.op("act", lambda e, bank=bank, c=c: e.activation(gates[:, c, 0:TT], ps[:, bank, 0:TT], AF.Sigmoid),
                         reads=[pkey], writes=[("gates", c)])
                elif ci < 4:
                    c = ci
                    if sample:
                        dstu = u_ext[:, c, 0:4 * 34].rearrange("p (s t) -> p s t", t=34)[:, :, 30:34]
                        srcp = ps[:, bank, 0:TT].rearrange("p (s t) -> p s t", t=4)
                        srcg = gates[:, c, 0:TT].rearrange("p (s t) -> p s t", t=4)
                    else:
                        dstu = u_ext[:, c, 30:30 + TT]
                        srcp = ps[:, bank, 0:TT]
                        srcg = gates[:, c, 0:TT]
                    if halo:
                        S.op("dve", lambda e, bank=bank, c=c: e.tensor_tensor(tmpA[:, 0:30], ps[:, bank, TT - 30:TT], gates[:, c, TT - 30:TT], ALU.mult),
                             reads=[pkey, ("gates", c)], writes=["tmpA"])
                        S.op("dve", lambda e, c=c: e.tensor_scalar(u_ext[:, c, 0:30], tmpA[:, 0:30], flag_sb[:, 0:1], None, ALU.mult),
                             reads=["tmpA", "flag"], writes=["u_ext_h"])
                    else:
                        S.op("dve", lambda e, dstu=dstu, srcp=srcp, srcg=srcg: e.tensor_tensor(dstu, srcp, srcg, ALU.mult),
                             reads=[pkey, ("gates", c)], writes=[("u_ext", c)])
                elif 8 <= ci < 14:
                    c = ci - 8
                    if kind == "main" and c >= 4:
                        S.op("act", lambda e, bank=bank, c=c: e.activation(
                            qT[:, c, :].rearrange("p (r u) -> p u r", u=32),
                            ps[:, bank, 0:512].rearrange("p (u r) -> p u r", r=16), AF.Copy, scale=0.125),
                            reads=[pkey], writes=["qT"])
                    else:
                        S.op("act", lambda e, bank=bank, c=c: e.activation(qT[:, c, 0:TT], ps[:, bank, 0:TT], AF.Copy, scale=0.125),
                             reads=[pkey], writes=["qT"])
                elif 14 <= ci < 20:
                    g, hc = (ci - 14) // 2, (ci - 14) % 2
                    if g == 2 and sample:
                        S.op("dve", lambda e, bank=bank, hc=hc: e.tensor_copy(kTc2[:, hc, 0:TT], ps[:, bank, 0:TT]),
                             reads=[pkey], writes=[("kTs", 2), "kTc2"])
                    elif g == 2:
                        S.op("dve", lambda e, bank=bank, hc=hc: e.tensor_copy(
                            kTc2[:, hc, :].rearrange("p (r u) -> p u r", u=32),
                            ps[:, bank, 0:512].rearrange("p (u r) -> p u r", r=16)),
                            reads=[pkey], writes=["kTc2"])
                    elif sample:
                        S.op("dve", lambda e, bank=bank, g=g, hc=hc: e.tensor_copy(kT[g][:, hc, 0:TT], ps[:, bank, 0:TT]),
                             reads=[pkey], writes=[("kTs", g), ("kT", g, 0)])
                    else:
                        for b in range(nblk):
                            sl = ring_slot(g, gb0 + b)
                            S.op("dve", lambda e, bank=bank, g=g, hc=hc, b=b, sl=sl: e.tensor_copy(
                                kT[g][:, hc, sl * 128:(sl + 1) * 128], ps[:, bank, b * 128:(b + 1) * 128]),
                                reads=[pkey], writes=[("kT", g, sl)])
                    need_out = sample or (kind == "main" and ((g == 2 and idx >= 4) or (g == 1 and idx == 7) or (g == 0 and idx == 7)))
                    if need_out:
                        tm_matmul(7, [(wb, wkey, 8, 0, 128)], lambda k, b: hT[:, k, b * 128:b * 128 + bt], ["hT"], nblk, bt)
                        S.op("act", lambda e, hc=hc: e.activation(
                            kv_tm[0:bt, hc, 0:nblk * 128],
                            ps[0:bt, 7, 0:nblk * 128], AF.Copy), reads=[("ps", 7)], writes=[("kv_tm", 0, hc)])
                        kv_out(kind, idx, g, 0, hc, nblk, bt)
                elif 20 <= ci < 26:
                    g, hc = (ci - 20) // 2, (ci - 20) % 2
                    need_out = sample or (kind == "main" and ((g == 2 and idx >= 4) or (g == 1 and idx == 7) or (g == 0 and idx == 7)))
                    if g == 2 and not sample:
                        for rg in range(4):
                            vb_ = 6 + (rg % 2)

                            def fn_v(e, rg=rg, vb_=vb_, wb=wb):
                                r_ = None
                                for r4 in range(4):
                                    r = 4 * rg + r4
                                    for k in range(8):
                                        r_ = e.matmul(ps[0:32, vb_, r4 * 128:(r4 + 1) * 128], hT[:, k, r:512:16],
                                                      wb[:, k * 128:(k + 1) * 128], start=(k == 0), stop=(k == 7))
                                return r_
                            S.op("pe", fn_v, reads=[wkey, "hT"], writes=[("ps", vb_)])
                            S.op("dve", lambda e, rg=rg, vb_=vb_, hc=hc: e.tensor_copy(
                                Vc2[0:32, 4 * rg:4 * rg + 4, hc * 128:(hc + 1) * 128],
                                ps[0:32, vb_, 0:512].rearrange("p (r n) -> p r n", n=128)),
                                reads=[("ps", vb_)], writes=["Vc2"])
                        if not need_out:
                            continue
                    tm_matmul(7, [(wb, wkey, 8, 0, 128)], lambda k, b: hT[:, k, b * 128:b * 128 + bt], ["hT"], nblk, bt)
                    if g == 2 and sample:
                        S.op("dve", lambda e, hc=hc: e.tensor_copy(Vc2[0:bt, 0, hc * 128:(hc + 1) * 128], ps[0:bt, 7, 0:128]),
                             reads=[("ps", 7)], writes=[("Vs", 2), "Vc2"])
                    elif g == 2:
                        pass
                    elif sample:
                        S.op("dve", lambda e, g=g, hc=hc: e.tensor_copy(Vr[g][0:bt, hc * 128:(hc + 1) * 128], ps[0:bt, 7, 0:128]),
                             reads=[("ps", 7)], writes=[("Vs", g), ("V", g, 0)])
                    else:
                        for b in range(nblk):
                            sl = ring_slot(g, gb0 + b)
                            S.op("dve", lambda e, g=g, hc=hc, b=b, sl=sl: e.tensor_copy(
                                Vr[g][:, sl * 256 + hc * 128:sl * 256 + (hc + 1) * 128], ps[:, 7, b * 128:(b + 1) * 128]),
                                reads=[("ps", 7)], writes=[("V", g, sl)])
                    need_out = sample or (kind == "main" and ((g == 2 and idx >= 4) or (g == 1 and idx == 7) or (g == 0 and idx == 7)))
                    if need_out:
                        S.op("act", lambda e, hc=hc: e.activation(kv_tm[0:bt, 2 + hc, 0:nblk * 128], ps[0:bt, 7, 0:nblk * 128], AF.Copy),
                             reads=[("ps", 7)], writes=[("kv_tm", 1, hc)])
                        kv_out(kind, idx, g, 1, hc, nblk, bt)
                elif ci >= 26:
                    c = ci - 26
                    S.op("act", lambda e, bank=bank, c=c: e.activation(gates[:, c, 0:TT], ps[:, bank, 0:TT], AF.Sigmoid),
                         reads=[pkey], writes=[("gates", c)])
            if not sample:
                oc0 = 64 if halo else 0
                S.op("dve", lambda e, oc0=oc0: e.tensor_copy(Vc2[0:32, :, 256:320], onesb[0:32, oc0:oc0 + 64].unsqueeze(1).to_broadcast([32, 16, 64])),
                     reads=["onesb"], writes=["Vc2"])
            if halo:
                g2_insert(idx)
                return
            def gates_gen():
                for nm, ci, wb, wkey in wstream([("in", 26 + c) for c in range(16)]):
                    c = ci - 26
                    bank, pkey = fm_matmul(wb, wkey, 8, 128, rhs_h, ["hT"], TT, bank=7)
                    S.op("act", lambda e, bank=bank, c=c: e.activation(gates[:, c, 0:TT], ps[:, bank, 0:TT], AF.Sigmoid),
                         reads=[pkey], writes=[("gates", c)])
                    yield "g"
            if gen is None:
                gen = conv_branch(kind, idx, nblk, bt, TT, last)
            if sample:
                for _ in gen:
                    pass
                sample_attention()
            else:
                gg = gates_gen()
                prompt_attention(idx, filler=gen, filler2=gg)
                for _ in gg:
                    pass
            for hs in range(4):
                S.op("dve", lambda e, hs=hs: e.reciprocal(rl[0:64, 0:TT], Oacc[64:128, hs, 0:TT]),
                     reads=[("Oacc", hs)], writes=["rl"])
                S.op("dve", lambda e, hs=hs: e.tensor_tensor(OnT[0:64, hs, 0:TT], Oacc[0:64, hs, 0:TT], rl[0:64, 0:TT], ALU.mult),
                     reads=[("Oacc", hs), "rl"], writes=["OnT"])
            if not sample:
                for _ in gen:
                    pass
            seq = []
            for oc in range(8):
                seq += [("ao", oc), ("co", oc)]
            it = wstream(seq)
            for oc in range(8):
                _, _, wbA, keyA = next(it)
                bankA, pkA = fm_matmul(wbA, keyA, 4, 64, lambda k: OnT[0:64, k, 0:TT], ["OnT"], TT)
                S.op("dve", lambda e, bankA=bankA, oc=oc: e.tensor_tensor(tmpA[:, 0:TT], ps[:, bankA, 0:TT], gates[:, 8 + oc, 0:TT], ALU.mult),
                     reads=[pkA, ("gates", 8 + oc)], writes=["tmpA"])
                _, _, wbC, keyC = next(it)
                bankC, pkC = fm_matmul(wbC, keyC, 4, 128, lambda k: sT[:, k, 0:TT], ["sT"], TT)
                S.op("dve", lambda e, bankC=bankC, oc=oc: e.tensor_tensor(tmpB[:, 0:TT], ps[:, bankC, 0:TT], gates[:, oc, 0:TT], ALU.mult),
                     reads=[pkC, ("gates", oc)], writes=["tmpB"])
                S.op("dve", lambda e, oc=oc: e.tensor_tensor(hT[:, oc, 0:TT], tmpA[:, 0:TT], tmpB[:, 0:TT], ALU.add),
                     reads=["tmpA", "tmpB"], writes=["hT"])
            for i, (nm, ci, wb, wkey) in enumerate(wstream([("o", oc) for oc in range(8)])):
                bank = 2 + i % 2
                tm_matmul(bank, [(wb, wkey, 8, 0, 128)], lambda k, b: hT[:, k, b * 128:b * 128 + bt], ["hT"], nblk, bt)
                resid_add(bank, ci, nblk, bt)
            if sample:
                ks = ["cones"] + [(n, cb) for n in ("ctile", "cv", "ck") for cb in range(2)]
                S.op("pool", lambda e: e.memset(actT[:, 21, 0:2], 0.0), reads=ks, writes=["actT"])
            norm_to_hT(nblk, bt, 1, hT, "hT")
            seq = []
            for fc in range(22):
                seq += [("fg", fc), ("fu", fc)]
            it = wstream(seq)
            for fc in range(22):
                _, _, wbG, keyG = next(it)
                bankG, pkG = fm_matmul(wbG, keyG, 8, 128, rhs_h, ["hT"], TT)
                tmpX, tkey = (tmpA, "tmpA") if fc % 2 == 0 else (tmpB, "tmpB")
                S.op("act", lambda e, bankG=bankG, tmpX=tmpX: e.activation(tmpX[:, 0:TT], ps[:, bankG, 0:TT], AF.Silu),
                     reads=[pkG], writes=[tkey])
                _, _, wbU, keyU = next(it)
                bankU, pkU = fm_matmul(wbU, keyU, 8, 128, rhs_h, ["hT"], TT)
                S.op("dve", lambda e, bankU=bankU, fc=fc, tmpX=tmpX: e.tensor_tensor(actT[:, fc, 0:TT], tmpX[:, 0:TT], ps[:, bankU, 0:TT], ALU.mult),
                     reads=[pkU, tkey], writes=["actT"])
            seq = []
            for oc in range(8):
                seq += [("fd0", oc), ("fd1", oc)]
            it = wstream(seq)
            for oc in range(8):
                _, _, wb0, k0 = next(it)
                _, _, wb1, k1 = next(it)
                bank = 2 + oc % 2
                tm_matmul(bank, [(wb0, k0, 11, 0, 128), (wb1, k1, 11, 11, 128)],
                          lambda k, b: actT[:, k, b * 128:b * 128 + bt], ["actT"], nblk, bt)
                resid_add(bank, oc, nblk, bt)
            norm_to_hT(nblk, bt, 2, hT, "hT")
            for b in range(nblk):
                src = psm[:, :] if sample else pp[(idx * 4 + b) * 128:(idx * 4 + b + 1) * 128, :]
                S.dma("sp", lambda e, b=b, src=src: e.dma_start(out=p_tm[0:bt, b, :], in_=src), writes=[("p_tm", b)], key=("ldp", b))
                for k in range(2):
                    S.op("pe", lambda e, b=b, k=k: e.transpose(ps[:, 7, k * 128:k * 128 + bt], p_tm[0:bt, b, k * 128:(k + 1) * 128], ident[0:bt, 0:bt]),
                         reads=[("p_tm", b), "ident"], writes=[("ps", 7)])
                S.op("act", lambda e, b=b: e.activation(pT[:, :, b * 128:b * 128 + bt], ps[:, 7, 0:256].rearrange("p (k n) -> p k n", n=128)[:, :, 0:bt], AF.Copy),
                     reads=[("ps", 7)], writes=["pT"])
            seq = []
            for oc in range(8):
                seq += [("pg", oc), ("pl", oc)]
            it = wstream(seq)
            for oc in range(8):
                bG, bP = (2, 3) if oc % 2 == 0 else (6, 7)
                tmpX, tkey = (tmpA, "tmpA") if oc % 2 == 0 else (tmpB, "tmpB")
                _, _, wbg, kg = next(it)
                tm_matmul(bG, [(wbg, kg, 8, 0, 128)], lambda k, b: hT[:, k, b * 128:b * 128 + bt], ["hT"], nblk, bt)
                S.op("act", lambda e, bG=bG, tmpX=tmpX: e.activation(tmpX[0:bt, 0:nblk * 128], ps[0:bt, bG, 0:nblk * 128], AF.Sigmoid),
                     reads=[("ps", bG)], writes=[tkey])
                _, _, wbp, kp = next(it)
                tm_matmul(bP, [(wbp, kp, 2, 0, 128)], lambda k, b: pT[:, k, b * 128:b * 128 + bt], ["pT"], nblk, bt)
                S.op("dve", lambda e, bP=bP, tmpX=tmpX: e.tensor_tensor(tmpX[0:bt, 0:nblk * 128], tmpX[0:bt, 0:nblk * 128], ps[0:bt, bP, 0:nblk * 128], ALU.mult),
                     reads=[("ps", bP), tkey], writes=[tkey])
                S.op("dve", lambda e, oc=oc, tmpX=tmpX: e.tensor_tensor(
                    x_tm[0:bt, 0:nblk, oc * 128:(oc + 1) * 128],
                    tmpX[0:bt, 0:nblk * 128].rearrange("p (b n) -> p b n", n=128),
                    x_tm[0:bt, 0:nblk, oc * 128:(oc + 1) * 128], ALU.add),
                    reads=[tkey] + [("x", b) for b in range(nblk)], writes=[("x", b) for b in range(nblk)])
            for b in range(nblk):
                hb = b % 2
                S.op("act", lambda e, b=b, hb=hb: e.activation(h_tm[0:bt, hb, :], x_tm[0:bt, b, :], AF.Square, accum_out=stat[0:bt, b:b + 1]),
                     reads=[("x", b)], writes=[("h_tm", hb), ("stat", b)])
            S.op("dve", lambda e: e.tensor_scalar(stat[0:bt, 4:4 + nblk], stat[0:bt, 0:nblk], 1.0 / D, EPS, ALU.mult, ALU.add),
                 reads=[("stat", b) for b in range(nblk)], writes=["stat_b"])
            S.op("pool", lambda e: e.tensor_tensor(stat[0:bt, 8:8 + nblk], stat[0:bt, 4:4 + nblk], mhalf[0:bt, 0:nblk], ALU.pow),
                 reads=["stat_b", "mhalf"], writes=["rstd"])
            for b in range(nblk):
                hb = b % 2
                S.op("dve", lambda e, b=b, hb=hb: e.scalar_tensor_tensor(h_tm[0:bt, hb, :], x_tm[0:bt, b, :], stat[0:bt, 8 + b:9 + b], gfb[0:bt, :], ALU.mult, ALU.mult),
                     reads=[("x", b), "rstd", "gfb"], writes=[("h_tm", hb)])
                dst = ys[:, :] if sample else y[(idx * 4 + b) * 128:(idx * 4 + b + 1) * 128, :]
                S.dma("pool", lambda e, hb=hb, dst=dst: e.dma_start(out=dst, in_=h_tm[0:bt, hb, :]), reads=[("h_tm", hb)], key=("sty", hb))

        def kv_out(kind, idx, g, which, hc, nblk, bt):
            W = GROUPS[g][0]
            col0 = which * 256 + hc * 128
            if kind == "sample":
                for s in range(NSEQ_S):
                    S.dma("pool", lambda e, s=s, g=g, which=which, col0=col0, W=W, hc=hc: e.dma_start(
                        out=kvs[g][s, W - 4:W, col0:col0 + 128], in_=kv_tm[4 * s:4 * s + 4, 2 * which + hc, 0:128]),
                        reads=[("kv_tm", which, hc)], key=("stkv", which, hc, s))
            else:
                for b in range(nblk):
                    tok0 = (idx * 4 + b) * 128 - (MAIN - W)
                    if tok0 < 0:
                        continue
                    S.dma("pool", lambda e, b=b, g=g, which=which, col0=col0, tok0=tok0, hc=hc: e.dma_start(
                        out=kvp[g][tok0:tok0 + 128, col0:col0 + 128], in_=kv_tm[:, 2 * which + hc, b * 128:(b + 1) * 128]),
                        reads=[("kv_tm", which, hc)], key=("stkv", which, hc, b))

        def conv_branch(kind, idx, nblk, bt, TT, last):
            sample = kind == "sample"
            if sample:
                for s in range(NSEQ_S):
                    S.dma("sp", lambda e, s=s: e.dma_start(out=p_tm[0:30, 0:2, :].rearrange("p a b -> p (a b)"), in_=stc[s, :, :]),
                          writes=["stc_sb"], key="ldstc")
                    for c in range(4):
                        S.op("pe", lambda e, c=c: e.transpose(ps[:, 7, c * 32:c * 32 + 30], p_tm[0:30, 0:2, :].rearrange("p a b -> p (a b)")[:, c * 128:(c + 1) * 128], ident[0:30, 0:30]),
                             reads=["stc_sb", "ident"], writes=[("ps", 7)])
                    S.op("dve", lambda e, s=s: e.tensor_copy(
                        u_ext[:, :, 0:4 * 34].rearrange("p c (s t) -> p c s t", t=34)[:, :, s, 0:30],
                        ps[:, 7, 0:128].rearrange("p (c t) -> p c t", t=32)[:, :, 0:30]),
                        reads=[("ps", 7)], writes=[("u_ext", c) for c in range(4)] + ["u_ext_h"])
                    S.dma("sp", lambda e, s=s: e.dma_start(out=convs[s, 0:26, :], in_=stc[s, 4:30, :]), key="cpconvs")
            cbufs = [(tmpA, "tmpA"), (tmpB, "tmpB"), (c_sb[:, 0, :], ("c_sb", 0)), (c_sb[:, 1, :], ("c_sb", 1))]
            if sample:
                x_c = Oacc
                xck = lambda b: [("Oacc", b)]
            else:
                x_c = actT[:, 0:8, :].rearrange("p a b -> p (a b)").bitcast(F32).rearrange("p (b n) -> p b n", n=DC)
                xck = lambda b: ["actT"]
            for cp in range(2):
                for j in range(31):
                    for c in (2 * cp, 2 * cp + 1):
                        bank = 2 + (c % 2)
                        if sample:
                            src = u_ext[:, c, 0:4 * 34].rearrange("p (s t) -> p s t", t=34)[:, :, j:j + 4]
                            acc = ps[:, bank, 0:TT].rearrange("p (s t) -> p s t", t=4)
                        else:
                            src = u_ext[:, c, j:j + TT]
                            acc = ps[:, bank, 0:TT]
                        rk = [("u_ext", c), "cvcols", "u_ext_h"]
                        if j == 0:
                            S.op("dve", lambda e, src=src, acc=acc, c=c: e.tensor_scalar(acc, src, cvcols[:, c, 0:1], cvcols[:, c, 31:32], ALU.mult, ALU.add),
                                 reads=rk, writes=[("ps", bank)])
                        else:
                            S.op("dve", lambda e, src=src, acc=acc, c=c, j=j: e.scalar_tensor_tensor(acc, src, cvcols[:, c, j:j + 1], acc, ALU.mult, ALU.add),
                                 reads=rk + [("ps", bank)], writes=[("ps", bank)])
                        yield "tap"
                for c in (2 * cp, 2 * cp + 1):
                    bank = 2 + (c % 2)
                    cb_ap, cb_key = cbufs[c]
                    S.op("act", lambda e, cb_ap=cb_ap, bank=bank: e.activation(cb_ap[:, 0:TT], ps[:, bank, 0:TT], AF.Copy),
                         reads=[("ps", bank)], writes=[cb_key])
            for c in range(4):
                cb_ap, cb_key = cbufs[c]
                for b in range(nblk):
                    S.op("pe", lambda e, cb_ap=cb_ap, b=b: e.transpose(ps[0:bt, 7, b * 128:(b + 1) * 128], cb_ap[:, b * 128:b * 128 + bt], ident[:, :]),
                         reads=[cb_key, "ident"], writes=[("ps", 7)])
                S.op("act", lambda e, c=c: e.activation(
                    x_c[0:bt, 0:nblk, c * 128:(c + 1) * 128], ps[0:bt, 7, 0:nblk * 128].rearrange("p (b n) -> p b n", n=128), AF.Copy),
                    reads=[("ps", 7)], writes=[k_ for b in range(nblk) for k_ in xck(b)])
                yield "tap"
            for b in range(nblk):
                S.op("act", lambda e, b=b: e.activation(tmpA[0:bt, 0:DC], x_c[0:bt, b, :], AF.Copy, accum_out=lnst[0:bt, b:b + 1]),
                     reads=xck(b), writes=["tmpA", ("ln_s1", b)])
                S.op("act", lambda e, b=b: e.activation(tmpA[0:bt, 0:DC], x_c[0:bt, b, :], AF.Square, accum_out=lnst[0:bt, 4 + b:5 + b]),
                     reads=xck(b), writes=["tmpA", ("ln_s2", b)])
                yield "tap"
            S.op("dve", lambda e: e.tensor_scalar(lnst[0:bt, 8:8 + nblk], lnst[0:bt, 0:nblk], 1.0 / DC, None, ALU.mult),
                 reads=[("ln_s1", b) for b in range(nblk)], writes=["ln_mu"])
            S.op("dve", lambda e: e.tensor_tensor(lnst[0:bt, 0:nblk], lnst[0:bt, 8:8 + nblk], lnst[0:bt, 8:8 + nblk], ALU.mult),
                 reads=["ln_mu"], writes=["ln_mu2"] + [("ln_s1", b) for b in range(nblk)])
            S.op("dve", lambda e: e.scalar_tensor_tensor(lnst[0:bt, 4:4 + nblk], lnst[0:bt, 4:4 + nblk], 1.0 / DC, lnst[0:bt, 0:nblk], ALU.mult, ALU.subtract),
                 reads=[("ln_s2", b) for b in range(nblk)] + ["ln_mu2"], writes=["ln_var"] + [("ln_s2", b) for b in range(nblk)])
            S.op("dve", lambda e: e.tensor_scalar(lnst[0:bt, 4:4 + nblk], lnst[0:bt, 4:4 + nblk], EPS, None, ALU.add),
                 reads=["ln_var"], writes=["ln_var"] + [("ln_s2", b) for b in range(nblk)])
            S.op("pool", lambda e: e.tensor_tensor(lnst[0:bt, 12:12 + nblk], lnst[0:bt, 4:4 + nblk], mhalf[0:bt, 0:nblk], ALU.pow),
                 reads=["ln_var", "mhalf"], writes=["ln_rstd"])
            yield "tap"
            for b in range(nblk):
                S.op("dve", lambda e, b=b: e.tensor_scalar(chat[0:bt, b, :], x_c[0:bt, b, :], lnst[0:bt, 8 + b:9 + b], lnst[0:bt, 12 + b:13 + b], ALU.subtract, ALU.mult),
                     reads=xck(b) + ["ln_mu", "ln_rstd"], writes=[("chat", b)])
                yield "tap"
                for c in range(4):
                    S.op("pe", lambda e, b=b, c=c: e.transpose(psb(7)[:, c * 128:c * 128 + bt], chat[0:bt, b, c * 128:(c + 1) * 128], identb[0:bt, 0:bt]),
                         reads=[("chat", b), "identb"], writes=[("ps", 7)])
                for c in range(4):
                    S.op("act", lambda e, b=b, c=c: e.activation(sT[:, c, b * 128:b * 128 + bt], psb(7)[:, c * 128:c * 128 + bt], AF.Silu,
                                                               scale=cvcols[:, c, 32:33], bias=cvcols[:, c, 33:34]),
                         reads=[("ps", 7), "cvcols"], writes=["sT"])
                yield "tap"
            if sample or last:
                for c in range(4):
                    if sample:
                        srcu = u_ext[:, c, 0:4 * 34].rearrange("p (s t) -> p s t", t=34)[:, :, 30:34]
                        S.op("act", lambda e, srcu=srcu: e.activation(c_sb[:, 0, 0:16].rearrange("p (s t) -> p s t", t=4), srcu, AF.Copy),
                             reads=[("u_ext", c)], writes=[("c_sb", 0)])
                        S.op("pe", lambda e, c=c: e.transpose(ps[0:16, 7, c * 128:(c + 1) * 128], c_sb[:, 0, 0:16], ident[:, :]),
                             reads=[("c_sb", 0), "ident"], writes=[("ps", 7)])
                    else:
                        S.op("pe", lambda e, c=c: e.transpose(ps[:, 7, c * 128:(c + 1) * 128], u_ext[:, c, 30 + 384:30 + 512], ident[:, :]),
                             reads=[("u_ext", c), "ident"], writes=[("ps", 7)])
                n_r = 16 if sample else 128
                S.op("act", lambda e: e.activation(tmpA[0:n_r, 0:512], ps[0:n_r, 7, 0:512], AF.Copy), reads=[("ps", 7)], writes=["tmpA"])
                if sample:
                    for s in range(NSEQ_S):
                        S.dma("pool", lambda e, s=s: e.dma_start(out=convs[s, 26:30, :], in_=tmpA[4 * s:4 * s + 4, 0:512]), reads=["tmpA"], key="stconv")
                else:
                    S.dma("pool", lambda e: e.dma_start(out=convp[:, :], in_=tmpA[98:128, 0:512]), reads=["tmpA"], key="stconv")
            if not sample:
                S.op("act", lambda e: e.activation(u_ext[:, :, 0:30], u_ext[:, :, 512:542], AF.Copy),
                     reads=[("u_ext", c) for c in range(4)], writes=["u_ext_h"])
            yield "end"


        def prompt_attention(idx, filler=None, filler2=None):
            for hs in range(4):
                for g in range(2):
                    h = 4 * g + hs
                    hc = hs // 2
                    po = 64 * (hs % 2)
                    ob = (6, 1)[(hs * 3 + g) % 2]
                    for qb in range(4):
                        gb = 16 + 4 * idx + qb
                        chunks = []
                        t0 = sum(NCH[:g])
                        for m in range(NCH[g]):
                            kb = gb - m
                            sl = ring_slot(g, kb)
                            kap = kT[g][po:po + 64, hc, sl * 128:(sl + 1) * 128]
                            vap = vap_of(g, sl, hs, kb < 16)
                            chunks.append((kap, [("kT", g, sl), ("V", g, sl)], vap, dist[:, t0 + m, :], 128,
                                           dist[:, t0 + m:t0 + NCH[g], :].rearrange("p a b -> p (a b)"),
                                           onesb[:, 64:128] if kb < 16 else onesb[:, 0:64]))
                        attn_block.first = True
                        attn_block.last = True
                        attn_block.obank = ob
                        attn_block(h, slice(qb * 128, (qb + 1) * 128), chunks, slice(qb * 128, (qb + 1) * 128))
                    if g == 0:
                        f = lambda hs=hs, ob=ob: S.op("dve", lambda e: e.tensor_copy(Oacc[:, hs, :], ps[:, ob, :]),
                                                      reads=[("ps", ob)], writes=[("Oacc", hs)])
                    else:
                        f = lambda hs=hs, ob=ob: S.op("dve", lambda e: e.tensor_tensor(Oacc[:, hs, :], ps[:, ob, :], Oacc[:, hs, :], ALU.add),
                                                      reads=[("ps", ob), ("Oacc", hs)], writes=[("Oacc", hs)])
                    attn_block.pending[-1]["post"].append(f)
                g2_batches(idx, hs)
            attn_flush(filler, filler2=filler2)
            g2_insert(4 + idx)

        def g2_batches(idx, hs):
            h = 8 + hs
            hc, po = hs // 2, 64 * (hs % 2)
            rot = (4 + idx) % 4
            sl = -SLOPES[h]
            for part in range(2):
                nk = 128 if part == 0 else 32
                ob = (6, 1)[(hs * 3 + 2 + part) % 2]
                dtab_ = d2[0:nk, rot:rot + 1, :] if part == 0 else d2[0:32, 4:5, :]

                def SE(sbank, pb, part=part, nk=nk, dtab_=dtab_):
                    skey = ("ps", sbank)

                    def fn_s(e):
                        r_ = None
                        for r in range(16):
                            kap = kTr2[po:po + 64, hc, r, :] if part == 0 else kTc2[po:po + 64, hc, r * 32:(r + 1) * 32]
                            r_ = e.matmul(ps[0:nk, sbank, r * 32:(r + 1) * 32], kap, qT[po:po + 64, 4 + hc, r * 32:(r + 1) * 32],
                                          start=True, stop=True)
                        return r_
                    S.op("pe", fn_s, reads=["qT", "kTr2" if part == 0 else "kTc2"], writes=[skey])
                    S.op("dve", lambda e: e.scalar_tensor_tensor(
                        ps[0:nk, sbank, :].rearrange("p (r u) -> p r u", u=32), dtab_.to_broadcast([nk, 16, 32]), sl,
                        ps[0:nk, sbank, :].rearrange("p (r u) -> p r u", u=32), ALU.mult, ALU.add),
                        reads=[skey, "d2"], writes=[skey])
                    S.op("act", lambda e: e.activation(P_sb[0:nk, pb, :], ps[0:nk, sbank, :], AF.Exp),
                         reads=[skey], writes=[("P_sb", pb)])

                def O(pb, part=part, nk=nk, ob=ob):
                    def fn_o(e):
                        r_ = None
                        for r in range(16):
                            vsrc = Vr2r if part == 0 else Vc2
                            e.matmul(ps[0:64, ob, r * 32:(r + 1) * 32], vsrc[0:nk, r, hs * 64:hs * 64 + 64],
                                     P_sb[0:nk, pb, r * 32:(r + 1) * 32], start=True, stop=True)
                            r_ = e.matmul(ps[64:128, ob, r * 32:(r + 1) * 32], vsrc[0:nk, r, 256:320],
                                          P_sb[0:nk, pb, r * 32:(r + 1) * 32], start=True, stop=True)
                        return r_
                    S.op("pe", fn_o, reads=[("P_sb", pb), "Vr2r" if part == 0 else "Vc2"], writes=[("ps", ob)])
                post = lambda ob=ob: S.op("dve", lambda e: e.tensor_tensor(
                    Oacc[:, hs, :].rearrange("p (u r) -> p r u", r=16), ps[:, ob, :].rearrange("p (r u) -> p r u", u=32),
                    Oacc[:, hs, :].rearrange("p (u r) -> p r u", r=16), ALU.add),
                    reads=[("ps", ob), ("Oacc", hs)], writes=[("Oacc", hs)])
                attn_block.pending.append(dict(custom=dict(SE=SE, O=O), pre=attn_block.pre, post=[post]))
                attn_block.pre = []

        def sample_attention():
            for hs in range(4):
                pass
            tiles = []
            for s in range(NSEQ_S):
                tiles.append((s, 0, 0, 0))
                for r in range(4):
                    tiles.append((s, 1, r, 1 + r))
                for r in range(4):
                    tiles.append((s, 2, r, 5 + r))
            for g in range(3):
                pass
            S.op("dve", lambda e: e.memset(Oacc[:, :, 0:TS], 0.0), writes=[("Oacc", hs) for hs in range(4)])
            for hs in range(4):
                hc, po = hs // 2, 64 * (hs % 2)
                for g in range(3):
                    h = 4 * g + hs
                    if g == 2:
                        kap = kTc2[po:po + 64, hc, 0:TS]
                        vap = Vc2[0:TS, 0, hs * 64:hs * 64 + 64]
                    else:
                        kap = kT[g][po:po + 64, hc, 0:TS]
                        vap = Vr[g][0:TS, hs * 64:hs * 64 + 64]
                    attn_block.first = True
                    attn_block.last = True
                    attn_block.obank = (6, 1)[(hs * 3 + g) % 2]
                    attn_block(h, slice(0, TS), [(kap, [("kTs", g), ("Vs", g), ("kT", g, 0), ("V", g, 0), "kTc2", "Vc2"], vap, sdist[0:TS, 9 + g, :], TS, None, onesb[0:TS, 0:64])],
                               slice(0, 16))
                    attn_block.pending[-1]["post"].append(
                        lambda hs=hs, ob=attn_block.obank: S.op("dve", lambda e: e.tensor_tensor(
                            Oacc[:, hs, 0:TS], ps[:, ob, 0:16], Oacc[:, hs, 0:TS], ALU.add),
                            reads=[("ps", ob), ("Oacc", hs)], writes=[("Oacc", hs)]))
            ctile = actT[:, 0:4, :].rearrange("p a b -> p (a b)").bitcast(F32)
            for ti, (s, g, r, tab) in enumerate(tiles):
                d = GROUPS[g][1]
                cb = ti % 2
                cf = ctile[:, cb * 512:(cb + 1) * 512]
                ckey, vkey, kkey = ("ctile", cb), ("cv", cb), ("ck", cb)
                src = cg[g][s].rearrange("(u r) f -> u r f", r=d)[:, r, :]
                vb = actT[:, 4 + cb, :]
                kb_ = actT[:, 6 + cb, :]

                def prep(cf=cf, src=src, vb=vb, kb_=kb_, ckey=ckey, vkey=vkey, kkey=kkey, cb=cb):
                    S.dma("sp", lambda e: e.dma_start(out=cf, in_=src), writes=[ckey], key=("ldc", cb))
                    S.op("act", lambda e: e.activation(vb[:, 0:256], cf[:, 256:512], AF.Copy), reads=[ckey], writes=[vkey])
                    for hc in range(2):
                        S.op("pe", lambda e, hc=hc: e.transpose(ps[:, 7, hc * 128:(hc + 1) * 128], cf[:, hc * 128:(hc + 1) * 128], ident[:, :]),
                             reads=[ckey, "ident"], writes=[("ps", 7)])
                    S.op("dve", lambda e: e.tensor_copy(kb_[:, 0:256], ps[:, 7, 0:256]), reads=[("ps", 7)], writes=[kkey])
                attn_block.pre.append(prep)
                for hs in range(4):
                    hc, po = hs // 2, 64 * (hs % 2)
                    h = 4 * g + hs
                    kap = kb_[po:po + 64, hc * 128:(hc + 1) * 128]
                    vap = actT[:, 4 + cb, hs * 64:hs * 64 + 64]
                    attn_block.first = True
                    attn_block.last = True
                    attn_block.obank = (6, 1)[(ti * 4 + hs) % 2]
                    attn_block(h, slice(4 * s, 4 * s + 4), [(kap, [kkey, vkey], vap, sdist[:, tab, 4 * s:4 * s + 4], 128, None, onesb[:, 0:64])],
                               slice(0, 4))
                    attn_block.pending[-1]["post"].append(
                        lambda hs=hs, s=s, ob=attn_block.obank: S.op("dve", lambda e: e.tensor_tensor(
                            Oacc[:, hs, 4 * s:4 * s + 4], ps[:, ob, 0:4], Oacc[:, hs, 4 * s:4 * s + 4], ALU.add),
                            reads=[("ps", ob), ("Oacc", hs)], writes=[("Oacc", hs)]))
                if r == 0:
                    L = GROUPS[g][0]
                    S.dma("sp", lambda e, s=s, g=g, L=L: e.dma_start(out=kvs[g][s, 0:L - 4, :], in_=cg[g][s, 4:L, :]), key=("cpkv", g))
            attn_flush(None)

        def sample_prep():
            ks = ["actT", "cones"] + [(n, cb) for n in ("ctile", "cv", "ck") for cb in range(2)]
            S.op("pool", lambda e: e.memset(actT[:, 4:6, 256:320], 1.0), reads=[], writes=ks)

        for ht in PLAN["halo"]:
            run_tile("halo", ht)
        for i in PLAN["main"]:
            run_tile("main", i)
        if PLAN["sample"]:
            sample_prep()
            run_tile("sample", 0)
        nops = S.finalize()
        PLAN["nops"] = nops
        PLAN["sbuf_left"] = nc.sbuf_bytes_remaining
    return nc


_CACHE = {}


def kernel(x_prompt, x_sample, state_conv, cache_kv_g0, cache_kv_g1, cache_kv_g2, p_prompt, p_sample,
           g_mix, w_in, w_dw, b_dw, ln_g, ln_b, w_conv_out, w_att_out, w_out, g_ffn, w_ffn_gate,
           w_ffn_up, w_ffn_down, g_ple, w_ple_gate, w_ple, g_final):
    f = lambda a: np.ascontiguousarray(np.asarray(a, dtype=np.float32))
    x_prompt, x_sample, state_conv, p_prompt, p_sample = map(f, (x_prompt, x_sample, state_conv, p_prompt, p_sample))
    caches = [f(cache_kv_g0), f(cache_kv_g1), f(cache_kv_g2)]
    if "nc" not in _CACHE:
        _CACHE["nc"] = build_program()
    nc = _CACHE["nc"]
    shared = dict(
        dtab=dist_tables(), d2tab=g2_tables(), sdtab=sample_dist_tables(),
        g3=np.ascontiguousarray(np.stack([f(g_mix)[0], f(g_ffn)[0], f(g_ple)[0]])),
        gfin=f(g_final).reshape(1, D),
        cvp=np.ascontiguousarray(np.concatenate([f(w_dw)[0], f(b_dw), f(ln_g), f(ln_b)], axis=0)),
        w_in=f(w_in)[0], w_co=f(w_conv_out)[0], w_ao=f(w_att_out)[0], w_o=f(w_out)[0],
        w_fg=f(w_ffn_gate)[0], w_fu=f(w_ffn_up)[0], w_fd=f(w_ffn_down)[0], w_pg=f(w_ple_gate)[0], w_pl=f(w_ple)[0],
    )
    in_maps = []
    for c in range(NCORES):
        b, half = c // 2, c % 2
        xhc = np.zeros((HALO + MAIN, D), np.float32)
        if half == 1:
            xhc[:] = x_prompt[b, MAIN - HALO:2 * MAIN]
        else:
            xhc[HALO:] = x_prompt[b, 0:MAIN]
        m = dict(shared)
        m["xh"] = xhc
        m["pp"] = np.ascontiguousarray(p_prompt[0, b, half * MAIN:(half + 1) * MAIN])
        m["flag"] = np.full((128, 1), float(half), np.float32)
        m["xs"] = np.ascontiguousarray(x_sample[4 * c:4 * c + 4].reshape(TS, D))
        m["psm"] = np.ascontiguousarray(p_sample[0, 4 * c:4 * c + 4].reshape(TS, DPLE))
        m["stc"] = np.ascontiguousarray(state_conv[0, 4 * c:4 * c + 4])
        for g in range(3):
            m["cg%d" % g] = np.ascontiguousarray(caches[g][0, 4 * c:4 * c + 4].reshape(NSEQ_S, GROUPS[g][0], 512))
        in_maps.append(m)
    res = run_bass_kernel_spmd(nc, in_maps, core_ids=list(range(NCORES)))
    R = res.results
    y_prompt = np.empty((4, 8192, D), np.float32)
    y_sample = np.empty((32, 4, D), np.float32)
    conv_p = np.empty((1, 4, 30, DC), np.float32)
    conv_s = np.empty((1, 32, 30, DC), np.float32)
    kv_p = [np.empty((1, 4, GROUPS[g][0], 2, 4, 64), np.float32) for g in range(3)]
    kv_s = [np.empty((1, 32, GROUPS[g][0], 2, 4, 64), np.float32) for g in range(3)]
    for c in range(NCORES):
        b, half = c // 2, c % 2
        y_prompt[b, half * MAIN:(half + 1) * MAIN] = R[c]["y"]
        y_sample[4 * c:4 * c + 4] = R[c]["ys"].reshape(4, 4, D)
        conv_s[0, 4 * c:4 * c + 4] = R[c]["convs"]
        for g in range(3):
            kv_s[g][0, 4 * c:4 * c + 4] = R[c]["kvs%d" % g].reshape(4, GROUPS[g][0], 2, 4, 64)
        if half == 1:
            conv_p[0, b] = R[c]["convp"]
            for g in range(3):
                kv_p[g][0, b] = R[c]["kvp%d" % g].reshape(GROUPS[g][0], 2, 4, 64)
    return (y_prompt, y_sample, conv_p, conv_s, kv_p[0], kv_s[0], kv_p[1], kv_s[1], kv_p[2], kv_s[2])
```
